# Optimizing a Trainium2 kernel written in Bass

```python
import math
import jax, jax.numpy as jnp
from jax import lax
import numpy as np


D_MODEL = 2048
BATCH = 4
SEQ = 2048
DEPTH = 4
DEC_BATCH = 8
DEC_SEQ = 1
PAST_LEN = 16384
PAGE_SIZE = 128

N_MIX = (DEPTH + 1) // 2
N_ATTN = DEPTH // 2
D_POOL = D_MODEL // 2
POOL_WINDOWS = (2, 4, 8, 16)
POOL_GROUPS = len(POOL_WINDOWS)
POOL_GC = D_POOL // POOL_GROUPS
POOL_BUF = max(POOL_WINDOWS) - 1
MH = 4
D_MLSTM = D_MODEL // 2
DK = D_MLSTM // MH
DV = D_MLSTM // MH
MLSTM_CHUNK = 64
D_IN_MIX = D_POOL + 4 * D_MLSTM + 2 * MH
D_MIX = D_POOL + D_MLSTM
N_HEADS = 16
HEAD_DIM = D_MODEL // N_HEADS
N_KV_HEADS = 4
GROUP = N_HEADS // N_KV_HEADS
D_Q = N_HEADS * HEAD_DIM
D_KV = N_KV_HEADS * HEAD_DIM
MOBA_BLOCK = 256
MOBA_TOPK = 3
Q_BLOCK = 8
D_FF = 4 * D_MODEL
ALPHA = (2 * DEPTH) ** 0.25
BETA = (8 * DEPTH) ** -0.25
LN_EPS = 1e-5
HEAD_NORM_EPS = 1e-6

kernel_name = 'hybrid_pool_mlstm_moba_deepnorm_step'


def layer_norm(x, g, b):
    xf = x.astype(jnp.float32)
    mu = jnp.mean(xf, axis=-1, keepdims=True)
    var = jnp.mean(jnp.square(xf - mu), axis=-1, keepdims=True)
    return ((xf - mu) * lax.rsqrt(var + LN_EPS) * g + b).astype(x.dtype)


def sq_relu_mlp(x, w_up, w_down):
    return jnp.square(jax.nn.relu(x @ w_up)) @ w_down


def pool_mixer(u, buf, pos0, w_pool, pool_scale):
    B, S, _ = u.shape
    ext = jnp.concatenate([buf.astype(u.dtype), u], axis=1).astype(jnp.float32)
    cs = jnp.concatenate([jnp.zeros((B, 1, D_POOL), jnp.float32), jnp.cumsum(ext, axis=1)], axis=1)
    pos = pos0 + jnp.arange(S)
    outs = []
    for g, w in enumerate(POOL_WINDOWS):
        c0, c1 = g * POOL_GC, (g + 1) * POOL_GC
        hi = cs[:, POOL_BUF + 1:POOL_BUF + 1 + S, c0:c1]
        lo = cs[:, POOL_BUF + 1 - w:POOL_BUF + 1 - w + S, c0:c1]
        cnt = jnp.minimum(w, pos + 1).astype(jnp.float32)[None, :, None]
        outs.append((hi - lo) / cnt - ext[:, POOL_BUF:, c0:c1])
    d = jnp.stack(outs, axis=2)
    y = jnp.einsum('bsgc,gcd->bsgd', d, w_pool.astype(jnp.float32)).reshape(B, S, D_POOL)
    y = y * pool_scale
    return y.astype(u.dtype), ext[:, -POOL_BUF:].astype(u.dtype)


def mlstm_chunk(carry, xs):
    C, n, m = carry
    q, k, v, ig, lf = xs
    L = q.shape[2]
    b = jnp.cumsum(lf, axis=-1)
    a = ig - b
    m_t = b + jnp.maximum(m[..., None], lax.cummax(a, axis=2))
    w_inter = jnp.exp(b + m[..., None] - m_t)
    causal = jnp.tril(jnp.ones((L, L), dtype=bool))
    log_d = a[..., None, :] + b[..., :, None] - m_t[..., :, None]
    dmat = jnp.exp(jnp.where(causal, log_d, -jnp.inf))
    s = jnp.einsum('bhtd,bhsd->bhts', q, k) * dmat
    num = w_inter[..., None] * jnp.einsum('bhtd,bhde->bhte', q, C) + jnp.einsum('bhts,bhse->bhte', s, v)
    den = w_inter * jnp.einsum('bhtd,bhd->bht', q, n) + jnp.sum(s, axis=-1)
    h = num / jnp.maximum(jnp.abs(den), jnp.exp(-m_t))[..., None]
    m_new = m_t[..., -1]
    b_last = b[..., -1]
    w_old = jnp.exp(b_last + m - m_new)
    w_s = jnp.exp(a + b_last[..., None] - m_new[..., None])
    C_new = w_old[..., None, None] * C + jnp.einsum('bhs,bhsd,bhse->bhde', w_s, k, v)
    n_new = w_old[..., None] * n + jnp.einsum('bhs,bhsd->bhd', w_s, k)
    return (C_new, n_new, m_new), h


def mlstm(q, k, v, ig, lf, C0, n0, m0):
    B, H, S, _ = q.shape
    L = math.gcd(S, MLSTM_CHUNK)
    nc = S // L

    def chunked(t):
        return jnp.moveaxis(t.reshape(t.shape[:2] + (nc, L) + t.shape[3:]), 2, 0)

    xs = (chunked(q), chunked(k), chunked(v), chunked(ig), chunked(lf))
    (C, n, m), h = lax.scan(mlstm_chunk, (C0, n0, m0), xs)
    h = jnp.moveaxis(h, 0, 2).reshape(B, H, S, DV)
    return h, C, n, m


def mix_layer(x, pos0, buf, C0, n0, m0, w_in, b_if, w_pool, pool_scale, hn_g, w_out):
    B, S, _ = x.shape
    z = x @ w_in
    u = z[..., :D_POOL]
    q = z[..., D_POOL:D_POOL + D_MLSTM]
    k = z[..., D_POOL + D_MLSTM:D_POOL + 2 * D_MLSTM]
    v = z[..., D_POOL + 2 * D_MLSTM:D_POOL + 3 * D_MLSTM]
    o = z[..., D_POOL + 3 * D_MLSTM:D_POOL + 4 * D_MLSTM]
    gates = (z[..., D_POOL + 4 * D_MLSTM:] + b_if).astype(jnp.float32)
    ig = jnp.transpose(gates[..., :MH], (0, 2, 1))
    lf = jnp.transpose(jax.nn.log_sigmoid(gates[..., MH:]), (0, 2, 1))

    def heads(t):
        return t.reshape(B, S, MH, DK).transpose(0, 2, 1, 3).astype(jnp.float32)

    h, C, n, m = mlstm(heads(q), heads(k) * DK ** -0.5, heads(v), ig, lf,
                       C0.astype(jnp.float32), n0.astype(jnp.float32), m0.astype(jnp.float32))
    mu = jnp.mean(h, axis=-1, keepdims=True)
    var = jnp.mean(jnp.square(h - mu), axis=-1, keepdims=True)
    hn = ((h - mu) * lax.rsqrt(var + HEAD_NORM_EPS)).transpose(0, 2, 1, 3).reshape(B, S, D_MLSTM)
    hn = hn * hn_g * jax.nn.sigmoid(o.astype(jnp.float32))
    y_pool, new_buf = pool_mixer(u, buf, pos0, w_pool, pool_scale)
    y = jnp.concatenate([y_pool, hn.astype(x.dtype)], axis=-1) @ w_out
    return y, new_buf, C.astype(x.dtype), n.astype(x.dtype), m.astype(x.dtype)


def gather_blocks(blocks, ix):
    return jax.vmap(jax.vmap(lambda bl, i: bl[i]))(blocks, ix)


def moba_attention(q, k, v, q_pos):
    B, Sq = q.shape[:2]
    T = k.shape[1]
    nb = -(-T // MOBA_BLOCK)
    pad = ((0, 0), (0, nb * MOBA_BLOCK - T), (0, 0), (0, 0))
    kb = jnp.pad(k.astype(jnp.float32), pad).reshape(B, nb, MOBA_BLOCK, N_KV_HEADS, HEAD_DIM).transpose(0, 3, 1, 2, 4)
    vb = jnp.pad(v.astype(jnp.float32), pad).reshape(B, nb, MOBA_BLOCK, N_KV_HEADS, HEAD_DIM).transpose(0, 3, 1, 2, 4)
    k_mean = jnp.mean(kb, axis=3)
    qg = q.astype(jnp.float32).reshape(B, Sq, N_KV_HEADS, GROUP, HEAD_DIM).transpose(0, 2, 3, 1, 4)
    q_blk = q_pos // MOBA_BLOCK
    gate = jnp.einsum('bhgsd,bhnd->bhgsn', qg, k_mean)
    past = jnp.arange(nb)[None, :] < q_blk[:, None]
    gate = jnp.where(past, gate, -jnp.inf)
    ksel = min(MOBA_TOPK, nb)
    _, idx = lax.top_k(gate, ksel)
    qb = math.gcd(Sq, Q_BLOCK)
    nc = Sq // qb
    scale = HEAD_DIM ** -0.5

    def chunked(t):
        return jnp.moveaxis(t.reshape(t.shape[:3] + (nc, qb) + t.shape[4:]), 3, 0)

    def attend(chunk):
        qc, ic, pc = chunk
        ks = gather_blocks(kb, ic)
        vs = gather_blocks(vb, ic)
        jc = pc // MOBA_BLOCK
        ko = kb[:, :, jc]
        vo = vb[:, :, jc]
        valid = jnp.arange(ksel)[None, :] < jc[:, None]
        s_sel = jnp.einsum('bhgtd,bhgtrnd->bhgtrn', qc, ks) * scale
        s_sel = jnp.where(valid[:, :, None], s_sel, -jnp.inf)
        key_pos = jc[:, None] * MOBA_BLOCK + jnp.arange(MOBA_BLOCK)[None, :]
        s_own = jnp.einsum('bhgtd,bhtnd->bhgtn', qc, ko) * scale
        s_own = jnp.where(key_pos <= pc[:, None], s_own, -jnp.inf)
        s = jnp.concatenate([s_sel.reshape(s_sel.shape[:4] + (ksel * MOBA_BLOCK,)), s_own], axis=-1)
        p = jax.nn.softmax(s, axis=-1)
        p_sel = p[..., :ksel * MOBA_BLOCK].reshape(s_sel.shape)
        p_own = p[..., ksel * MOBA_BLOCK:]
        return jnp.einsum('bhgtrn,bhgtrnd->bhgtd', p_sel, vs) + jnp.einsum('bhgtn,bhtnd->bhgtd', p_own, vo)

    out = lax.map(attend, (chunked(qg), chunked(idx), q_pos.reshape(nc, qb)))
    out = jnp.moveaxis(out, 0, 3).reshape(B, N_KV_HEADS, GROUP, Sq, HEAD_DIM)
    return out.transpose(0, 3, 1, 2, 4).reshape(B, Sq, D_Q).astype(q.dtype)


def moba_layer(x, pos0, past_k, past_v, w_qkv, w_o):
    B, S, _ = x.shape
    z = x @ w_qkv
    q = z[..., :D_Q].reshape(B, S, N_HEADS, HEAD_DIM)
    k = z[..., D_Q:D_Q + D_KV].reshape(B, S, N_KV_HEADS, HEAD_DIM)
    v = z[..., D_Q + D_KV:].reshape(B, S, N_KV_HEADS, HEAD_DIM)
    if past_k is None:
        k_all, v_all = k, v
    else:
        k_all = jnp.concatenate([past_k.astype(k.dtype), k], axis=1)
        v_all = jnp.concatenate([past_v.astype(v.dtype), v], axis=1)
    o = moba_attention(q, k_all, v_all, pos0 + jnp.arange(S))
    return o @ w_o, k, v


def trunk(x, pos0, pool_buf, C0, n0, m0, cache_k, cache_v, page_table, params):
    (w_in_mix, b_if, w_pool, pool_scale, mlstm_norm_g, w_out_mix, w_qkv, w_o,
     ln_mix_g, ln_mix_b, ln_ffn_g, ln_ffn_b, w_up, w_down) = params
    ks, vs, bufs, Cs, ns, ms = [], [], [], [], [], []
    for l in range(DEPTH):
        i = l // 2
        if l % 2 == 0:
            y, nbuf, C, n, m = mix_layer(x, pos0, pool_buf[i], C0[i], n0[i], m0[i], w_in_mix[i], b_if[i],
                                         w_pool[i], pool_scale[i], mlstm_norm_g[i], w_out_mix[i])
            bufs.append(nbuf); Cs.append(C); ns.append(n); ms.append(m)
        else:
            if page_table is None:
                past_k, past_v = None, None
            else:
                Bs = page_table.shape[0]
                past_k = cache_k[i][page_table].reshape(Bs, -1, N_KV_HEADS, HEAD_DIM)
                past_v = cache_v[i][page_table].reshape(Bs, -1, N_KV_HEADS, HEAD_DIM)
            y, k, v = moba_layer(x, pos0, past_k, past_v, w_qkv[i], w_o[i])
            ks.append(k); vs.append(v)
        x = layer_norm(ALPHA * x + y, ln_mix_g[l], ln_mix_b[l])
        x = layer_norm(ALPHA * x + sq_relu_mlp(x, w_up[l], w_down[l]), ln_ffn_g[l], ln_ffn_b[l])
    return (x, jnp.stack(ks), jnp.stack(vs), jnp.stack(bufs), jnp.stack(Cs), jnp.stack(ns), jnp.stack(ms))


def setup_inputs(seed: int = 0) -> dict:
    key = jax.random.key(seed)
    ks = jax.random.split(key, 32)
    n_pages = PAST_LEN // PAGE_SIZE
    n_pool = (5 * DEC_BATCH * n_pages) // 4
    nrm = jax.random.normal
    f32 = jnp.float32
    x_prompt = nrm(ks[0], (BATCH, SEQ, D_MODEL), f32)
    x_sample = nrm(ks[1], (DEC_BATCH, DEC_SEQ, D_MODEL), f32)
    cache_k = nrm(ks[2], (N_ATTN, n_pool, PAGE_SIZE, N_KV_HEADS, HEAD_DIM), f32)
    cache_v = nrm(ks[3], (N_ATTN, n_pool, PAGE_SIZE, N_KV_HEADS, HEAD_DIM), f32)
    state_pool = nrm(ks[4], (N_MIX, DEC_BATCH, POOL_BUF, D_POOL), f32)
    state_C = 0.3 * nrm(ks[5], (N_MIX, DEC_BATCH, MH, DK, DV), f32)
    state_n = 0.3 * nrm(ks[6], (N_MIX, DEC_BATCH, MH, DK), f32)
    state_m = 0.5 * nrm(ks[7], (N_MIX, DEC_BATCH, MH), f32)
    page_table = jax.random.permutation(ks[8], n_pool)[:DEC_BATCH * n_pages].reshape(DEC_BATCH, n_pages).astype(jnp.int32)
    w_in_mix = nrm(ks[9], (N_MIX, D_MODEL, D_IN_MIX), f32) * D_MODEL ** -0.5
    b_i = 0.1 * nrm(ks[10], (N_MIX, MH), f32)
    b_f = jnp.linspace(3.0, 6.0, MH, dtype=f32)[None, :] + 0.1 * nrm(ks[11], (N_MIX, MH), f32)
    b_if = jnp.concatenate([b_i, b_f], axis=-1)
    w_pool = nrm(ks[12], (N_MIX, POOL_GROUPS, POOL_GC, POOL_GC), f32) * POOL_GC ** -0.5
    pool_scale = 1.0 + 0.1 * nrm(ks[13], (N_MIX, D_POOL), f32)
    mlstm_norm_g = 1.0 + 0.1 * nrm(ks[14], (N_MIX, D_MLSTM), f32)
    w_out_mix = nrm(ks[15], (N_MIX, D_MIX, D_MODEL), f32) * (D_MIX ** -0.5 * BETA)
    w_qkv = nrm(ks[16], (N_ATTN, D_MODEL, D_Q + 2 * D_KV), f32) * D_MODEL ** -0.5
    w_o = nrm(ks[17], (N_ATTN, D_Q, D_MODEL), f32) * (D_Q ** -0.5 * BETA)
    ln_mix_g = 1.0 + 0.05 * nrm(ks[18], (DEPTH, D_MODEL), f32)
    ln_mix_b = 0.02 * nrm(ks[19], (DEPTH, D_MODEL), f32)
    ln_ffn_g = 1.0 + 0.05 * nrm(ks[20], (DEPTH, D_MODEL), f32)
    ln_ffn_b = 0.02 * nrm(ks[21], (DEPTH, D_MODEL), f32)
    w_up = nrm(ks[22], (DEPTH, D_MODEL, D_FF), f32) * D_MODEL ** -0.5
    w_down = nrm(ks[23], (DEPTH, D_FF, D_MODEL), f32) * (D_FF ** -0.5 * BETA)
    return {'x_prompt': x_prompt, 'x_sample': x_sample, 'cache_k': cache_k, 'cache_v': cache_v,
            'state_pool': state_pool, 'state_C': state_C, 'state_n': state_n, 'state_m': state_m,
            'page_table': page_table, 'w_in_mix': w_in_mix, 'b_if': b_if, 'w_pool': w_pool,
            'pool_scale': pool_scale, 'mlstm_norm_g': mlstm_norm_g, 'w_out_mix': w_out_mix,
            'w_qkv': w_qkv, 'w_o': w_o, 'ln_mix_g': ln_mix_g, 'ln_mix_b': ln_mix_b,
            'ln_ffn_g': ln_ffn_g, 'ln_ffn_b': ln_ffn_b, 'w_up': w_up, 'w_down': w_down}


def reference(x_prompt, x_sample, cache_k, cache_v, state_pool, state_C, state_n, state_m, page_table,
              w_in_mix, b_if, w_pool, pool_scale, mlstm_norm_g, w_out_mix, w_qkv, w_o,
              ln_mix_g, ln_mix_b, ln_ffn_g, ln_ffn_b, w_up, w_down):
    params = (w_in_mix, b_if, w_pool, pool_scale, mlstm_norm_g, w_out_mix, w_qkv, w_o,
              ln_mix_g, ln_mix_b, ln_ffn_g, ln_ffn_b, w_up, w_down)
    dt = x_prompt.dtype
    Bp = x_prompt.shape[0]
    pool0 = jnp.zeros((N_MIX, Bp, POOL_BUF, D_POOL), dt)
    C0 = jnp.zeros((N_MIX, Bp, MH, DK, DV), dt)
    n0 = jnp.zeros((N_MIX, Bp, MH, DK), dt)
    m0 = jnp.zeros((N_MIX, Bp, MH), dt)
    y_prompt, k_prompt, v_prompt, pool_prompt, C_prompt, n_prompt, m_prompt = trunk(
        x_prompt, 0, pool0, C0, n0, m0, None, None, None, params)
    past_len = page_table.shape[1] * PAGE_SIZE
    y_sample, k_sample, v_sample, pool_sample, C_sample, n_sample, m_sample = trunk(
        x_sample, past_len, state_pool, state_C, state_n, state_m, cache_k, cache_v, page_table, params)
    return (y_prompt, y_sample, k_prompt, v_prompt, k_sample, v_sample,
            pool_prompt, C_prompt, n_prompt, m_prompt, pool_sample, C_sample, n_sample, m_sample)
```

```python
import contextlib
import numpy as np
import concourse.bass as bass
import concourse.mybir as mybir
from concourse.bass_utils import run_bass_kernel_spmd

F32 = mybir.dt.float32
BF16 = mybir.dt.bfloat16
I32 = mybir.dt.int32
AF = mybir.ActivationFunctionType
ALU = mybir.AluOpType
AX = mybir.AxisListType

ENGS = ("pe", "act", "dve", "pool", "sp")
NEG = -30000.0
DBG = set()


class Prog:
    def __init__(self):
        self.ops = []
        self.last_w = {}
        self.readers = {}

    def op(self, eng, fn, R=(), W=(), stream=None, extra=()):
        i = len(self.ops)
        deps = set(extra)
        for r in R:
            w = self.last_w.get(r)
            if w is not None:
                deps.add(w)
        for w_ in W:
            w = self.last_w.get(w_)
            if w is not None:
                deps.add(w)
            rd = self.readers.get(w_)
            if rd:
                for lst in rd.values():
                    deps.update(lst)
        for r in R:
            rd = self.readers.setdefault(r, {})
            if stream is None:
                rd[eng] = [i]
            else:
                rd.setdefault("dma:" + stream, []).append(i)
        for w_ in W:
            self.last_w[w_] = i
            self.readers[w_] = {}
        deps.discard(i)
        self.ops.append(dict(eng=eng, fn=fn, deps=deps, stream=stream))
        return i

    def emit(self, nc):
        ops = self.ops

        def skip(od, o):
            return od["stream"] is None and od["eng"] == "pe" and o["eng"] == "pe" and o["stream"] is None

        needed = set()
        for o in ops:
            for d in o["deps"]:
                if not skip(ops[d], o):
                    needed.add(d)
        cnt = {}
        for i, o in enumerate(ops):
            if o["stream"] is not None:
                k = "dma:" + o["stream"]
                cnt[k] = cnt.get(k, 0) + 16
                o["sig"] = (k, cnt[k])
            elif i in needed:
                k = o["eng"]
                cnt[k] = cnt.get(k, 0) + 1
                o["sig"] = (k, cnt[k])
            else:
                o["sig"] = None
        known = {e: {} for e in ENGS}
        for o in ops:
            e = o["eng"]
            w = {}
            for d in o["deps"]:
                od = ops[d]
                if od["sig"] is None or skip(od, o):
                    continue
                k, v = od["sig"]
                if known[e].get(k, 0) >= v:
                    continue
                w[k] = max(w.get(k, 0), v)
            for k, v in w.items():
                known[e][k] = v
            o["waits"] = sorted(w.items())
        per = {e: [o for o in ops if o["eng"] == e] for e in ENGS}
        with contextlib.ExitStack() as st:
            sems = {k: st.enter_context(nc.semaphore("s_" + "".join(ch if ch.isalnum() else "_" for ch in k))) for k in sorted(cnt)}
            block = st.enter_context(nc.Block())

            def run(engobj, lst):
                for o in lst:
                    for k, v in o["waits"]:
                        engobj.wait_ge(sems[k], v)
                    ins = o["fn"](engobj)
                    if o["sig"] is not None and ins is not None:
                        ins.then_inc(sems[o["sig"][0]], 16 if o["stream"] is not None else 1)

            @block.tensor
            def _(e):
                run(e, per["pe"])

            @block.scalar
            def _(e):
                run(e, per["act"])

            @block.vector
            def _(e):
                run(e, per["dve"])

            @block.gpsimd
            def _(e):
                run(e, per["pool"])

            @block.sync
            def _(e):
                run(e, per["sp"])
        return cnt


class Cfg:
    def __init__(s, S=2048, G=1024, NS=2, DEPTH=4, DFF=8192, NPAGES=128, NPOOL=1280, NSLOT=7):
        s.D = 2048
        s.KC = 16
        s.S, s.G, s.NS, s.DEPTH, s.DFF, s.NPAGES, s.NPOOL, s.NSLOT = S, G, NS, DEPTH, DFF, NPAGES, NPOOL, NSLOT
        s.NG = S // G
        s.TT = S + NS
        s.GN = G + NS
        s.NMIX = (DEPTH + 1) // 2
        s.NATT = max(1, DEPTH // 2)
        s.ALPHA = (2 * DEPTH) ** 0.25
        s.NB = NPAGES // 2
        s.NBP = S // 256
        assert G % 512 == 0 and S % G == 0


POOL_W = (2, 4, 8, 16)


def const_arrays(cfg):
    c = {}
    c["ident"] = np.eye(128, dtype=np.float32)
    s_ = np.arange(128)
    c["maskT"] = (s_[:, None] <= s_[None, :]).astype(np.float32)
    own = np.zeros((128, 2, 256), np.float32)
    for hf in range(2):
        own[:, hf, :] = np.where(np.arange(256)[None, :] <= (hf * 128 + s_)[:, None], 0.0, NEG)
    c["own"] = own.reshape(128, 512)
    sel4 = np.zeros((128, 4, 128), np.float32)
    for hd in range(4):
        sel4[hd, hd, :] = 1.0
    c["sel4"] = sel4.reshape(128, 512)
    e4 = np.zeros((128, 4, cfg.NB), np.float32)
    for hd in range(4):
        e4[hd, hd, :] = 1.0
    c["e4"] = e4.reshape(128, 4 * cfg.NB)
    selk = np.zeros((128, 4, 16), np.float32)
    for k in range(16):
        selk[k, k // 4, k] = 1.0
    c["selk"] = selk.reshape(128, 64)
    sb = np.full((128, 1), NEG, np.float32)
    sb[0, 0] = 0.0
    c["selfb"] = sb
    c["ones"] = np.ones((128, 128), np.float32)
    c["iotaf"] = np.arange(128, dtype=np.float32)[:, None].copy()
    names = ["ident", "maskT", "own", "sel4", "e4", "selk", "selfb", "ones", "iotaf"]
    offs = {}
    o = 0
    for n in names:
        offs[n] = (o, c[n].shape[1])
        o += c[n].shape[1]
    arr = np.concatenate([c[n] for n in names], axis=1)
    return arr, offs


class Gen:
    def __init__(s, nc, cfg, st):
        s.nc, s.cfg, s.P = nc, cfg, Prog()
        c = cfg
        D = c.D
        dt = nc.dram_tensor

        def din(name, shape, dtype=F32):
            return dt(name, list(shape), dtype, kind="ExternalInput").ap()

        def dout(name, shape, dtype=F32):
            return dt(name, list(shape), dtype, kind="ExternalOutput").ap()

        s.carr, s.coff = const_arrays(cfg)
        NC_ = s.carr.shape[1]
        s.i = dict(
            xp=din("xp", [c.S, D]), xs=din("xs", [c.NS, D]),
            ck=din("ck", [c.NATT * c.NPOOL * 128, 512]), cv=din("cv", [c.NATT * c.NPOOL * 128, 512]),
            spool=din("spool", [c.NMIX, c.NS, 15, 1024]), sC=din("sC", [c.NMIX, c.NS, 4, 256, 256]),
            sn=din("sn", [c.NMIX, c.NS, 4, 256]), sm=din("sm", [c.NMIX, c.NS, 4]),
            pt=din("pt", [c.NS, c.NPAGES], I32), iop=din("iop", [128, 1], I32),
            w_in=din("w_in", [c.NMIX, D, 5128]), bif=din("bif", [4, c.NMIX * 2]),
            w_pool=din("w_pool", [c.NMIX, 4, 256, 256]), pscale=din("pscale", [128, c.NMIX * 8]),
            hng=din("hng", [c.NMIX, 1024]), w_out=din("w_out", [c.NMIX, D, D]),
            w_qkv=din("w_qkv", [c.NATT, D, 3072]), w_o=din("w_o", [c.NATT, D, D]),
            lnp=din("lnp", [128, 4 * c.DEPTH * c.KC]),
            w_up=din("w_up", [c.DEPTH, D, c.DFF]), w_down=din("w_down", [c.DEPTH, c.DFF, D]),
            cst=din("cst", [128, NC_]),
        )
        s.o = dict(
            yp=dout("yp", [c.S, D]), ys=dout("ys", [c.NS, D]),
            kp=dout("kp", [c.NATT, c.S, 512]), vp=dout("vp", [c.NATT, c.S, 512]),
            ks=dout("ks", [c.NATT, c.NS, 512]), vs=dout("vs", [c.NATT, c.NS, 512]),
            poolp=dout("poolp", [c.NMIX, 15, 1024]), Cp=dout("Cp", [c.NMIX, 4, 256, 256]),
            np=dout("np", [c.NMIX, 4, 256]), mp=dout("mp", [c.NMIX, 4]),
            pools=dout("pools", [c.NMIX, c.NS, 15, 1024]), Cs=dout("Cs", [c.NMIX, c.NS, 4, 256, 256]),
            ns=dout("ns", [c.NMIX, c.NS, 4, 256]), ms=dout("ms", [c.NMIX, c.NS, 4]),
        )
        sb = lambda name, shape, dtype: st.enter_context(nc.sbuf_tensor(name, list(shape), dtype))
        s.XT = sb("XT", [128, c.KC, c.TT], BF16)
        s.ring = [sb(f"ring{i}", [128, 2048], BF16) for i in range(c.NSLOT)]
        s.SCRN = 21024
        s.SCR = sb("SCR", [128, s.SCRN], F32)
        s.CST = sb("CST", [128, NC_], F32)
        s.IDB = sb("IDB", [128, 128], BF16)
        s.MTB = sb("MTB", [128, 128], BF16)
        s.LNP = sb("LNP", [128, 4 * c.DEPTH * c.KC], F32)
        s.RT = sb("RT", [128, 2, 512], F32)
        s.SM_ = sb("SM_", [128, 256], F32)
        s.DUMMY = sb("DUMMY", [128, 8], F32)
        s.CSTATE = sb("CSTATE", [128, 4, 2, 257], F32)
        s.CSS = sb("CSS", [128, 2, 257], F32)
        s.CB = sb("CB", [128, 2, 257], BF16)
        s.KSF = sb("KSF", [1, 2, c.NS, 512], F32)
        s.ps = [st.enter_context(nc.psum_tensor(f"ps{i}", [128, 512], F32)) for i in range(8)]
        s.rslot = 0
        s.bank = 0
        s.uid = 0
        s.outs = []
        s.reserved = set()

    def cst(s, name):
        o, n = s.coff[name]
        return s.CST[:, o:o + n]

    def scr(s, off, nwords, dtype=F32):
        ap = s.SCR[:, off:off + nwords]
        return ap if dtype == F32 else ap.bitcast(dtype)

    def nb(s):
        while True:
            b = s.bank
            s.bank = (s.bank + 1) % 8
            if b not in s.reserved:
                return b

    def wtile(s, src_ap, nk, ncol):
        sl = s.rslot
        s.rslot = (s.rslot + 1) % s.cfg.NSLOT
        dst = s.ring[sl][:, 0:nk * ncol].rearrange("p (k c) -> p k c", k=nk)
        s.P.op("pool", lambda e: e.dma_start(out=dst, in_=src_ap), R=[], W=[f"ring{sl}"], stream=f"ring{sl}")
        return dst, f"ring{sl}"

    def wgroup(s, w2d, r0, nkc, c0, ncol):
        per = max(1, 2048 // ncol)
        tiles = []
        for k0 in range(0, nkc, per):
            nk = min(per, nkc - k0)
            src = w2d[(r0 + k0) * 128:(r0 + k0 + nk) * 128, c0:c0 + ncol].rearrange("(k p) c -> p k c", p=128)
            tiles.append(s.wtile(src, nk, ncol))

        def acc(kc):
            t, key = tiles[kc // per]
            return t[:, kc % per, :], key
        return acc

    def op(s, eng, fn, R=(), W=(), stream=None, extra=()):
        R = list(R)
        if any(k.startswith("~") for k in R) or any(k.startswith("~") for k in W):
            R.append("~F")
        return s.P.op(eng, fn, R, list(W), stream, extra)

    def fence(s):
        s.P.op("dve", lambda e: e.memset(s.DUMMY[0:1, 0:1], 0.0), [], ["~F"])

    def cp(s, eng, out, in_, R, W):
        if eng == "act":
            s.op("act", lambda e: e.copy(out=out, in_=in_), R, W)
        else:
            s.op(eng, lambda e: e.tensor_copy(out=out, in_=in_), R, W)

    def rsqrt(s, ap, eps, key):
        s.op("dve", lambda e: e.tensor_scalar(out=ap, in0=ap, scalar1=eps, scalar2=None, op0=ALU.add), [key], [key])
        s.op("act", lambda e: e.activation(out=ap, in_=ap, func=AF.Sqrt), [key], [key])
        s.op("dve", lambda e: e.reciprocal(out=ap, in_=ap), [key], [key])

    def mm(s, out, lhsT, rhs, start, stop, R, W):
        s.op("pe", lambda e: e.matmul(out, lhsT, rhs, start=start, stop=stop), R, W)

    def dma(s, out, in_, R, W, stream, q="sp", **kw):
        return s.op(q, lambda e: e.dma_start(out=out, in_=in_, **kw), R, W, stream=stream)

    def newstream(s, base):
        s.uid += 1
        return f"{base}{s.uid}"

    def col_tiles(s, g):
        c = s.cfg
        t = [(g * c.G + i * 512, 512) for i in range(c.G // 512)]
        if g == c.NG - 1:
            t.append((c.S, c.NS))
        return t

    def tok_tiles(s, g):
        c = s.cfg
        t = [(g * c.G + i * 128, 128) for i in range(c.G // 128)]
        if g == c.NG - 1:
            t.append((c.S, c.NS))
        return t

    def xkey(s, col):
        return "XTs" if col >= s.cfg.S else f"XT{col // s.cfg.G}"

    def load_consts(s):
        c = s.cfg
        s.dma(s.CST[:], s.i["cst"], [], ["CST"], "ld_cst")
        s.dma(s.LNP[:], s.i["lnp"], [], ["LNP"], "ld_lnp")
        s.op("dve", lambda e: e.tensor_copy(out=s.IDB[:], in_=s.cst("ident")), ["CST"], ["IDB"])
        s.op("dve", lambda e: e.tensor_copy(out=s.MTB[:], in_=s.cst("maskT")), ["CST"], ["MTB"])

    def load_x(s):
        c = s.cfg
        XL = [s.scr(i * 2048, 2048) for i in range(2)]
        tiles = [(t * 128, 128, s.i["xp"][t * 128:(t + 1) * 128, :]) for t in range(c.S // 128)]
        tiles.append((c.S, c.NS, s.i["xs"]))
        for n, (c0, rows, src) in enumerate(tiles):
            b = n % 2
            s.dma(XL[b][0:rows, :], src, [], [f"~XL{b}"], f"~XL{b}")
            for k4 in range(c.KC // 4):
                bk = s.nb()
                for j in range(4):
                    kc = k4 * 4 + j
                    s.op("pe", lambda e, bk=bk, j=j, kc=kc, b=b, rows=rows: e.transpose(
                        s.ps[bk][:, j * 128:j * 128 + rows], XL[b][0:rows, kc * 128:(kc + 1) * 128],
                        s.cst("ident")[0:rows, 0:rows]), [f"~XL{b}", "CST"], [f"ps{bk}"])
                src_ps = s.ps[bk][:].rearrange("p (j t) -> p j t", j=4)[:, :, 0:rows]
                dst = s.XT[:, k4 * 4:(k4 + 1) * 4, c0:c0 + rows]
                s.cp("act" if k4 % 2 else "dve", dst, src_ps, [f"ps{bk}"], [s.xkey(c0)])

    def lnp(s, kind, l, kc):
        c = s.cfg
        o = (kind * c.DEPTH + l) * c.KC + kc
        return s.LNP[:, o:o + 1]

    def layer_norm(s, pre, prekeys, ncols, kind, l, dst_fn, tmp_off, tmpkeys=(), tmps=None):
        c = s.cfg
        n = ncols
        tk = list(tmpkeys)
        if tmps is None:
            tmps = [s.scr(tmp_off + q * 512, 512) for q in range(5)]
        mean, rstd, t1 = tmps[0][:, 0:n], tmps[1][:, 0:n], tmps[2][:, 0:n]
        sq = [tmps[3][:, 0:n], tmps[4][:, 0:n]]
        b1, b2 = s.nb(), s.nb()
        onesD = s.cst("ones")
        for kc in range(c.KC):
            s.op("act", lambda e, kc=kc: e.activation(out=sq[kc % 2], in_=pre[:, kc, :], func=AF.Square),
                 [prekeys[kc]], [f"~lnsq{kc % 2}"] + tk)
            s.mm(s.ps[b1][:, 0:n], onesD, pre[:, kc, :], kc == 0, kc == c.KC - 1, [prekeys[kc], "CST"], [f"ps{b1}"])
            s.mm(s.ps[b2][:, 0:n], onesD, sq[kc % 2], kc == 0, kc == c.KC - 1, [f"~lnsq{kc % 2}", "CST"], [f"ps{b2}"])
        invD = 1.0 / c.D
        s.op("dve", lambda e: e.tensor_scalar(out=mean, in0=s.ps[b1][:, 0:n], scalar1=invD, scalar2=None, op0=ALU.mult),
             [f"ps{b1}"], ["~lnmean"] + tk)
        s.op("dve", lambda e: e.tensor_tensor(out=t1, in0=mean, in1=mean, op=ALU.mult), ["~lnmean"], ["~lnt1"] + tk)
        s.op("dve", lambda e: e.scalar_tensor_tensor(out=rstd, in0=s.ps[b2][:, 0:n], scalar=invD, in1=t1,
                                                     op0=ALU.mult, op1=ALU.subtract), [f"ps{b2}", "~lnt1"], ["~lnrstd"] + tk)
        s.rsqrt(rstd, 1e-5, "~lnrstd")
        for kc in range(c.KC):
            out, okey = dst_fn(kc)
            s.op("dve", lambda e, kc=kc: e.tensor_tensor(out=pre[:, kc, :], in0=pre[:, kc, :], in1=mean, op=ALU.subtract),
                 [prekeys[kc], "~lnmean"], [prekeys[kc]])
            s.op("dve", lambda e, kc=kc: e.tensor_tensor(out=pre[:, kc, :], in0=pre[:, kc, :], in1=rstd, op=ALU.mult),
                 [prekeys[kc], "~lnrstd"], [prekeys[kc]])
            s.op("act", lambda e, kc=kc, out=out: e.activation(out=out, in_=pre[:, kc, :], func=AF.Identity,
                                                               scale=s.lnp(kind, l, kc), bias=s.lnp(kind + 1, l, kc)),
                 [prekeys[kc], "LNP"], [okey] if okey != prekeys[kc] else [okey])

    def ffn(s, l, g, final):
        c = s.cfg
        s.fence()
        ct = s.col_tiles(g)
        c0g = g * c.G

        def loc(c0):
            return c.G + (c0 - c.S) if c0 >= c.S else c0 - c0g
        ACC = s.scr(0, c.KC * c.GN).rearrange("p (k t) -> p k t", k=c.KC)
        H = s.scr(c.KC * c.GN, 8 * c.GN // 2 + 8, BF16)[:, 0:8 * c.GN].rearrange("p (k t) -> p k t", k=8)
        akeys = [f"~ACC{k}" for k in range(c.KC)]
        for (c0, w) in ct:
            lo = loc(c0)
            for kc in range(c.KC):
                if kc % 2:
                    s.op("act", lambda e, kc=kc, lo=lo, c0=c0, w=w: e.mul(out=ACC[:, kc, lo:lo + w], in_=s.XT[:, kc, c0:c0 + w], mul=c.ALPHA),
                         [s.xkey(c0)], [akeys[kc]])
                else:
                    s.op("dve", lambda e, kc=kc, lo=lo, c0=c0, w=w: e.tensor_scalar(
                        out=ACC[:, kc, lo:lo + w], in0=s.XT[:, kc, c0:c0 + w], scalar1=c.ALPHA, scalar2=None, op0=ALU.mult),
                        [s.xkey(c0)], [akeys[kc]])
        wu, wd = s.i["w_up"][l], s.i["w_down"][l]
        nrt = 0
        for sbk in range(c.DFF // 1024):
            for cg in range(2):
                wa = s.wgroup(wu, 0, c.KC, sbk * 1024 + cg * 512, 512)
                for j in range(4):
                    hc = cg * 4 + j
                    bks = [s.nb() for _ in ct]
                    for kc in range(c.KC):
                        wt, wk = wa(kc)
                        for ti, (c0, w) in enumerate(ct):
                            s.mm(s.ps[bks[ti]][:, 0:w], wt[:, j * 128:(j + 1) * 128], s.XT[:, kc, c0:c0 + w],
                                 kc == 0, kc == c.KC - 1, [wk, s.xkey(c0)], [f"ps{bks[ti]}"])
                    for ti, (c0, w) in enumerate(ct):
                        lo, bk = loc(c0), bks[ti]
                        r = nrt % 2
                        nrt += 1
                        s.op("act", lambda e, bk=bk, w=w, r=r: e.activation(out=s.RT[:, r, 0:w], in_=s.ps[bk][:, 0:w], func=AF.Relu),
                             [f"ps{bk}"], [f"RT{r}"])
                        s.op("act", lambda e, w=w, r=r, hc=hc, lo=lo: e.activation(out=H[:, hc, lo:lo + w], in_=s.RT[:, r, 0:w], func=AF.Square),
                             [f"RT{r}"], [f"~H{hc}"])
            for ocg in range(4):
                wa = s.wgroup(wd, sbk * 8, 8, ocg * 512, 512)
                for j in range(4):
                    oc = ocg * 4 + j
                    bks = [s.nb() for _ in ct]
                    for kk in range(8):
                        wt, wk = wa(kk)
                        for ti, (c0, w) in enumerate(ct):
                            lo = loc(c0)
                            s.mm(s.ps[bks[ti]][:, 0:w], wt[:, j * 128:(j + 1) * 128], H[:, kk, lo:lo + w],
                                 kk == 0, kk == 7, [wk, f"~H{kk}"], [f"ps{bks[ti]}"])
                    for ti, (c0, w) in enumerate(ct):
                        lo, bk = loc(c0), bks[ti]
                        s.op("dve", lambda e, bk=bk, w=w, oc=oc, lo=lo: e.tensor_tensor(
                            out=ACC[:, oc, lo:lo + w], in0=s.ps[bk][:, 0:w], in1=ACC[:, oc, lo:lo + w], op=ALU.add),
                            [f"ps{bk}", akeys[oc]], [akeys[oc]])
        tmp_off = c.KC * c.GN
        hkeys = [f"~H{k}" for k in range(8)]
        for (c0, w) in ct:
            lo = loc(c0)
            pre = ACC[:, :, lo:lo + w]
            if not final:
                s.layer_norm(pre, akeys, w, 2, l, lambda kc, c0=c0, w=w: (s.XT[:, kc, c0:c0 + w], s.xkey(c0)), tmp_off, hkeys)
            else:
                s.layer_norm(pre, akeys, w, 2, l, lambda kc, lo=lo, w=w: (ACC[:, kc, lo:lo + w], akeys[kc]), tmp_off, hkeys)
                s.store_y(ACC, akeys, lo, c0, w, tmp_off + 2560, hkeys)

    def store_y(s, ACC, akeys, lo, c0, w, yoff, hkeys):
        c = s.cfg
        out = s.o["ys"] if c0 >= c.S else s.o["yp"]
        r0 = 0 if c0 >= c.S else c0
        YT = [s.scr(yoff + q * 512, 512) for q in range(4)]
        for t0 in range(0, w, 128):
            rows = min(128, w - t0)
            for k4 in range(c.KC // 4):
                bk = s.nb()
                for j in range(4):
                    kc = k4 * 4 + j
                    s.op("pe", lambda e, bk=bk, j=j, kc=kc, rows=rows, t0=t0: e.transpose(
                        s.ps[bk][0:rows, j * 128:(j + 1) * 128], ACC[:, kc, lo + t0:lo + t0 + rows], s.cst("ident")),
                        [akeys[kc], "CST"], [f"ps{bk}"])
                s.cp("dve" if k4 % 2 else "act", YT[k4][0:rows, :], s.ps[bk][0:rows, :], [f"ps{bk}"], [f"~YT{k4}"] + hkeys)
                s.outs.append(s.op("sp", lambda e, rows=rows, t0=t0, k4=k4: e.dma_start(
                    out=out[r0 + t0:r0 + t0 + rows, k4 * 512:(k4 + 1) * 512], in_=YT[k4][0:rows, :]), [f"~YT{k4}"], [], stream=f"YT{k4}"))

    def finish(s):
        s.op("sp", lambda e: None, [], [], extra=s.outs)

    def mix(s, l, g):
        c = s.cfg
        i = l // 2
        G, NS, GN, KC = c.G, c.NS, c.GN, c.KC
        last = (g == c.NG - 1)
        W = s.i["w_in"][i]
        ct = s.col_tiles(g)
        NCH = G // 128
        tts = [(g * G + k * 128, 128) for k in range(NCH)]
        if last:
            tts += [(c.S + j, 1) for j in range(NS)]
        NT = NCH + NS
        c0g = g * G
        ns_ = NS if last else 0

        def loc(c0):
            return G + (c0 - c.S) if c0 >= c.S else c0 - c0g
        s.fence()
        MT = s.scr(0, 8208, BF16)[:, 0:KC * GN].rearrange("p (k t) -> p k t", k=KC)
        PRE0, DT0, SM0 = 8208, 16416, 20520
        DT = s.scr(DT0, 4104, BF16)[:, 0:8 * GN].rearrange("p (k t) -> p k t", k=8)
        SC = s.scr(SM0, NT * 12).rearrange("p (t k) -> p t k", k=12)
        AE = s.scr(SM0 + 120, 4 * NCH).rearrange("p (h k) -> p h k", h=4)
        AES = s.scr(SM0 + 160, 4 * NS).rearrange("p (h k) -> p h k", h=4)
        UES = s.scr(SM0 + 180, 8 * NS * 16).rearrange("p (c j t) -> p c j t", c=8, j=NS)
        BIF = s.SM_[0:4, 0:4]
        PSC = s.SM_[:, 8:24]
        MCAR = s.SM_[0:4, 24:25]
        SMP = s.SM_[0:4, 32:32 + NS]
        AEL = s.SM_[:, 40:44]
        UCAR = s.SM_[:, 64:184].rearrange("p (c t) -> p c t", c=8)
        if g == 0:
            s.dma(s.SM_[0:4, 0:2], s.i["bif"][:, 2 * i:2 * i + 2], [], ["BIF"], s.newstream("bif"))
            s.dma(PSC[:, 0:8], s.i["pscale"][:, 8 * i:8 * i + 8], [], ["PSC"], s.newstream("psc"))
        gt = [s.scr(PRE0 + k * GN, GN) for k in range(8)]
        IG, LF, MM, FF, T1, T2, AL, BE = [t[0:4, :] for t in gt]
        gk = ["~gIG", "~gLF", "~gMM", "~gFF", "~gT1", "~gT2", "~gAL", "~gBE"]
        wg, wgk = s.wtile(W[:, 5120:5128].rearrange("(k p) c -> p k c", p=128), KC, 8)
        for (c0, w) in ct:
            lo = loc(c0)
            b1, b2 = s.nb(), s.nb()
            for kc in range(KC):
                s.mm(s.ps[b1][0:4, 0:w], wg[:, kc, 0:4], s.XT[:, kc, c0:c0 + w], kc == 0, kc == KC - 1, [wgk, s.xkey(c0)], [f"ps{b1}"])
                s.mm(s.ps[b2][0:4, 0:w], wg[:, kc, 4:8], s.XT[:, kc, c0:c0 + w], kc == 0, kc == KC - 1, [wgk, s.xkey(c0)], [f"ps{b2}"])
            s.op("dve", lambda e, b1=b1, lo=lo, w=w: e.tensor_scalar(out=IG[:, lo:lo + w], in0=s.ps[b1][0:4, 0:w], scalar1=BIF[:, 0:1],
                                                                  scalar2=None, op0=ALU.add), [f"ps{b1}", "BIF"], ["~gIG"])
            s.op("dve", lambda e, b2=b2, lo=lo, w=w: e.tensor_scalar(out=T1[:, lo:lo + w], in0=s.ps[b2][0:4, 0:w], scalar1=BIF[:, 1:2],
                                                                  scalar2=None, op0=ALU.add), [f"ps{b2}", "BIF"], ["~gT1"])
        n = G + ns_
        s.op("act", lambda e: e.activation(out=T1[:, 0:n], in_=T1[:, 0:n], func=AF.Exp, scale=-1.0), ["~gT1"], ["~gT1"])
        s.op("dve", lambda e: e.tensor_scalar(out=T1[:, 0:n], in0=T1[:, 0:n], scalar1=1.0, scalar2=None, op0=ALU.add), ["~gT1"], ["~gT1"])
        s.op("act", lambda e: e.activation(out=T1[:, 0:n], in_=T1[:, 0:n], func=AF.Ln), ["~gT1"], ["~gT1"])
        s.op("dve", lambda e: e.tensor_scalar(out=LF[:, 0:n], in0=T1[:, 0:n], scalar1=-1.0, scalar2=None, op0=ALU.mult), ["~gT1"], ["~gLF"])
        s.op("dve", lambda e: e.memset(T2[:, 0:n], 0.0), [], ["~gT2"])
        if g == 0:
            s.op("dve", lambda e: e.memset(MCAR, 0.0), [], ["MCAR"])
        s.op("dve", lambda e: e.tensor_tensor_scan(out=FF[:, 0:G], data0=LF[:, 0:G], data1=T2[:, 0:G], initial=0.0, op0=ALU.add, op1=ALU.add),
             ["~gLF", "~gT2"], ["~gFF"])
        s.op("dve", lambda e: e.tensor_tensor_scan(out=MM[:, 0:G], data0=LF[:, 0:G], data1=IG[:, 0:G], initial=MCAR, op0=ALU.add, op1=ALU.max),
             ["~gLF", "~gIG", "MCAR"], ["~gMM"])
        s.op("dve", lambda e: e.memset(T1[:, 0:n], 0.0), ["~gLF"], ["~gT1"])
        s.op("dve", lambda e: e.tensor_copy(out=T2[:, 0:128], in_=MCAR.to_broadcast([4, 128])), ["MCAR"], ["~gT2"])
        if NCH > 1:
            v3 = lambda t: t[:, 0:G].rearrange("p (c t) -> p c t", t=128)
            s.op("dve", lambda e: e.tensor_copy(out=v3(T1)[:, 1:NCH, :], in_=v3(FF)[:, 0:NCH - 1, 127:128].to_broadcast([4, NCH - 1, 128])),
                 ["~gFF"], ["~gT1"])
            s.op("dve", lambda e: e.tensor_copy(out=v3(T2)[:, 1:NCH, :], in_=v3(MM)[:, 0:NCH - 1, 127:128].to_broadcast([4, NCH - 1, 128])),
                 ["~gMM"], ["~gT2"])
        s.op("dve", lambda e: e.tensor_copy(out=MCAR, in_=MM[:, G - 1:G]), ["~gMM", "~gT2"], ["MCAR"])
        if last:
            s.dma(SMP, s.i["sm"][i].rearrange("s h -> h s"), [], ["SMP"], s.newstream("smp"), allow_slow_non_contiguous=True)
            s.op("dve", lambda e: e.tensor_copy(out=T2[:, G:n], in_=SMP), ["SMP"], ["~gT2"])
            s.op("dve", lambda e: e.tensor_tensor(out=MM[:, G:n], in0=SMP, in1=LF[:, G:n], op=ALU.add), ["SMP", "~gLF"], ["~gMM"])
            s.op("dve", lambda e: e.tensor_tensor(out=MM[:, G:n], in0=MM[:, G:n], in1=IG[:, G:n], op=ALU.max), ["~gIG", "~gMM"], ["~gMM"])
            s.op("dve", lambda e: e.tensor_copy(out=FF[:, G:n], in_=LF[:, G:n]), ["~gLF"], ["~gFF"])
            s.outs.append(s.dma(s.o["mp"][i].rearrange("(h o) -> h o", o=1), MM[:, G - 1:G], ["~gMM"], [], s.newstream("mo")))
            s.outs.append(s.dma(s.o["ms"][i].rearrange("s h -> h s"), MM[:, G:n], ["~gMM"], [], s.newstream("mo"), allow_slow_non_contiguous=True))
        tt = lambda out, a, b, op, R, Wk: s.op("dve", lambda e: e.tensor_tensor(out=out[:, 0:n], in0=a[:, 0:n], in1=b[:, 0:n], op=op), R, Wk)
        tt(AL, FF, T1, ALU.subtract, ["~gFF", "~gT1"], ["~gAL"])
        tt(AL, AL, MM, ALU.subtract, ["~gAL", "~gMM"], ["~gAL"])
        tt(AL, AL, T2, ALU.add, ["~gAL", "~gT2"], ["~gAL"])
        tt(BE, IG, FF, ALU.subtract, ["~gIG", "~gFF"], ["~gBE"])
        tt(BE, BE, T1, ALU.add, ["~gBE", "~gT1"], ["~gBE"])
        tt(BE, BE, T2, ALU.subtract, ["~gBE", "~gT2"], ["~gBE"])
        s.op("act", lambda e: e.activation(out=AL[:, 0:n], in_=AL[:, 0:n], func=AF.Exp), ["~gAL"], ["~gAL"])
        s.op("act", lambda e: e.activation(out=BE[:, 0:n], in_=BE[:, 0:n], func=AF.Exp), ["~gBE"], ["~gBE"])
        s.op("act", lambda e: e.activation(out=FF[:, 0:n], in_=MM[:, 0:n], func=AF.Exp, scale=-1.0), ["~gMM", "~gFF"], ["~gFF"])
        id4 = s.cst("ident")[0:4, 0:4]
        for ti, (c0, rows) in enumerate(tts):
            lo = loc(c0)
            bk = s.nb()
            for q, (src, key) in enumerate(((AL, "~gAL"), (BE, "~gBE"), (FF, "~gFF"))):
                s.mm(s.ps[bk][0:rows, 4 * q:4 * q + 4], src[:, lo:lo + rows], id4, True, True, [key, "CST"], [f"ps{bk}"])
            s.cp("dve", SC[0:rows, ti, :], s.ps[bk][0:rows, 0:12], [f"ps{bk}"], ["~SC"])
        bk = s.nb()
        sel4 = s.cst("sel4").rearrange("p (h m) -> p h m", h=4)
        for hd in range(4):
            s.mm(s.ps[bk][:, hd * NCH:(hd + 1) * NCH], sel4[0:4, hd, :], AL[:, 127:G:128], True, True, ["~gAL", "CST"], [f"ps{bk}"])
            if last:
                s.mm(s.ps[bk][:, 256 + hd * NS:256 + (hd + 1) * NS], sel4[0:4, hd, :], AL[:, G:n], True, True, ["~gAL", "CST"], [f"ps{bk}"])
        s.cp("dve", AE, s.ps[bk][:, 0:4 * NCH].rearrange("p (h k) -> p h k", h=4), [f"ps{bk}"], ["~AE"])
        if last:
            s.cp("dve", AES, s.ps[bk][:, 256:256 + 4 * NS].rearrange("p (h k) -> p h k", h=4), [f"ps{bk}"], ["~AES"])
        s.fence()
        NU = 15 + G
        UE = s.scr(PRE0, NU)
        PA = s.scr(PRE0 + 1056, NU)
        PB = s.scr(PRE0 + 2112, NU)
        SPL = s.scr(PRE0 + 3168, 1024)
        PO = s.scr(PRE0 + 4192, 1024)
        if last:
            for j in range(NS):
                s.dma(SPL[0:15, :], s.i["spool"][i, j], [], ["~SPL"], "spl")
                for h2 in range(2):
                    bk = s.nb()
                    for q in range(4):
                        ch = h2 * 4 + q
                        s.mm(s.ps[bk][:, q * 16:q * 16 + 15], SPL[0:15, ch * 128:(ch + 1) * 128], s.cst("ident")[0:15, 0:15], True, True,
                             ["~SPL", "CST"], [f"ps{bk}"])
                    s.cp("dve", UES[:, h2 * 4:h2 * 4 + 4, j, 0:15], s.ps[bk][:, 0:64].rearrange("p (q t) -> p q t", q=4)[:, :, 0:15],
                         [f"ps{bk}"], ["~UES"])
        for cg in range(2):
            wa = s.wgroup(W, 0, KC, cg * 512, 512)
            for j in range(4):
                ch = cg * 4 + j
                wdw = POOL_W[ch // 2]
                bks = [s.nb() for _ in ct]
                for kc in range(KC):
                    wt, wk = wa(kc)
                    for ti, (c0, w) in enumerate(ct):
                        s.mm(s.ps[bks[ti]][:, 0:w], wt[:, j * 128:(j + 1) * 128], s.XT[:, kc, c0:c0 + w], kc == 0, kc == KC - 1,
                             [wk, s.xkey(c0)], [f"ps{bks[ti]}"])
                if g == 0:
                    s.op("dve", lambda e: e.memset(UE[:, 0:15], 0.0), [], ["~UE"])
                else:
                    s.cp("dve", UE[:, 0:15], UCAR[:, ch, :], ["UCAR"], ["~UE"])
                for ti, (c0, w) in enumerate(ct):
                    if c0 >= c.S:
                        s.cp("act", UES[:, ch, :, 15:16], s.ps[bks[ti]][:, 0:NS].rearrange("p (j o) -> p j o", o=1), [f"ps{bks[ti]}"], ["~UES"])
                    else:
                        lo = loc(c0)
                        s.cp("act", UE[:, 15 + lo:15 + lo + w], s.ps[bks[ti]][:, 0:w], [f"ps{bks[ti]}"], ["~UE"])
                s.cp("dve", UCAR[:, ch, :], UE[:, G:G + 15], ["~UE"], ["UCAR"])
                cur, curk, lo_v = UE, "~UE", 0
                bufs = [(PA, "~PA"), (PB, "~PB")]
                step, bi = 1, 0
                while step < wdw:
                    dst, dk = bufs[bi]
                    nlo = lo_v + step
                    s.op("dve", lambda e, dst=dst, cur=cur, nlo=nlo, step=step: e.tensor_tensor(
                        out=dst[:, nlo:NU], in0=cur[:, nlo:NU], in1=cur[:, nlo - step:NU - step], op=ALU.add), [curk], [dk])
                    cur, curk, lo_v = dst, dk, nlo
                    step *= 2
                    bi ^= 1
                s.op("dve", lambda e, cur=cur, ch=ch, wdw=wdw: e.scalar_tensor_tensor(
                    out=DT[:, ch, 0:G], in0=cur[:, 15:NU], scalar=1.0 / wdw, in1=UE[:, 15:NU], op0=ALU.mult, op1=ALU.subtract),
                    [curk, "~UE"], ["~DT"])
                if g == 0:
                    for t in range(wdw - 1):
                        s.op("dve", lambda e, cur=cur, ch=ch, t=t: e.scalar_tensor_tensor(
                            out=DT[:, ch, t:t + 1], in0=cur[:, 15 + t:16 + t], scalar=1.0 / (t + 1), in1=UE[:, 15 + t:16 + t],
                            op0=ALU.mult, op1=ALU.subtract), [curk, "~UE"], ["~DT"])
                if last:
                    for jj in range(NS):
                        s.op("dve", lambda e, ch=ch, jj=jj, wdw=wdw: e.reduce_sum(out=PA[:, 0:1], in_=UES[:, ch, jj, 16 - wdw:16], axis=AX.X),
                             ["~UES", curk], ["~PA"])
                        s.op("dve", lambda e, ch=ch, jj=jj, wdw=wdw: e.scalar_tensor_tensor(
                            out=DT[:, ch, G + jj:G + jj + 1], in0=PA[:, 0:1], scalar=1.0 / wdw, in1=UES[:, ch, jj, 15:16],
                            op0=ALU.mult, op1=ALU.subtract), ["~PA", "~UES"], ["~DT"])
        if last:
            srcs = [(lambda ch: UCAR[:, ch, :], "UCAR", s.o["poolp"][i])]
            for jj in range(NS):
                srcs.append((lambda ch, jj=jj: UES[:, ch, jj, 1:16], "~UES", s.o["pools"][i, jj]))
            for fn, key, dst in srcs:
                for h2 in range(2):
                    bk = s.nb()
                    for q in range(4):
                        s.mm(s.ps[bk][0:15, q * 128:(q + 1) * 128], fn(h2 * 4 + q), s.cst("ident"), True, True, [key, "CST"], [f"ps{bk}"])
                    s.cp("dve", PO[0:15, h2 * 512:(h2 + 1) * 512], s.ps[bk][0:15, :], [f"ps{bk}"], ["~PO"])
                s.outs.append(s.dma(dst, PO[0:15, :], ["~PO"], [], "po"))
        wp, wpk = s.wtile(s.i["w_pool"][i].rearrange("g (k p) c -> p (g k) c", p=128), 8, 256)
        for gg in range(4):
            for jo in range(2):
                for (c0, w) in ct:
                    lo = loc(c0)
                    bk = s.nb()
                    for kk in range(2):
                        s.mm(s.ps[bk][:, 0:w], wp[:, gg * 2 + kk, jo * 128:(jo + 1) * 128], DT[:, gg * 2 + kk, lo:lo + w], kk == 0, kk == 1,
                             [wpk, "~DT"], [f"ps{bk}"])
                    s.op("dve", lambda e, bk=bk, w=w, lo=lo, oc=gg * 2 + jo: e.tensor_scalar(
                        out=MT[:, oc, lo:lo + w], in0=s.ps[bk][:, 0:w], scalar1=PSC[:, oc:oc + 1], scalar2=None, op0=ALU.mult),
                        [f"ps{bk}", "PSC"], ["~MT"])
        s.fence()
        o_ = PRE0
        QT = s.scr(o_, 1026, BF16)[:, 0:2 * GN].rearrange("p (k t) -> p k t", k=2); o_ += 1026
        KT = s.scr(o_, 1026, BF16)[:, 0:2 * GN].rearrange("p (k t) -> p k t", k=2); o_ += 1026
        KB = s.scr(o_, NT * 128, BF16).rearrange("p (t d) -> p t d", d=256); o_ += NT * 128
        VE = s.scr(o_, NT * 130, BF16).rearrange("p (t d) -> p t d", d=260); o_ += NT * 130
        OS = s.scr(o_, NT * 128, BF16).rearrange("p (t d) -> p t d", d=256); o_ += NT * 128
        HH = s.scr(o_, 256); o_ += 256
        HC = s.scr(o_, 256); o_ += 256
        HN = s.scr(o_, 128, BF16); o_ += 128
        AT = s.scr(o_, 64, BF16); o_ += 64
        SS = s.scr(o_, 16); o_ += 16
        HG = s.scr(o_, 1024); o_ += 1024
        assert o_ <= DT0
        s.dma(HG, s.i["hng"][i:i + 1, :].partition_broadcast(128).rearrange("p o c -> p (o c)") if False else
              s.i["hng"][i:i + 1, :].to_broadcast([128, 1024]), [], ["~HG"], s.newstream("hg"))
        for hd in range(4):
            qc, kcol, vc, oc_ = 1024 + hd * 256, 2048 + hd * 256, 3072 + hd * 256, 4096 + hd * 256
            CST_ = s.CSTATE[:, hd]
            ck_ = f"CST{hd}"
            for (col, dstT, key, mul) in ((qc, QT, "~QT", 1.0), (kcol, KT, "~KT", 0.0625)):
                wa = s.wgroup(W, 0, KC, col, 256)
                for jh in range(2):
                    bks = [s.nb() for _ in ct]
                    for kc in range(KC):
                        wt, wk = wa(kc)
                        for ti, (c0, w) in enumerate(ct):
                            s.mm(s.ps[bks[ti]][:, 0:w], wt[:, jh * 128:(jh + 1) * 128], s.XT[:, kc, c0:c0 + w], kc == 0, kc == KC - 1,
                                 [wk, s.xkey(c0)], [f"ps{bks[ti]}"])
                    for ti, (c0, w) in enumerate(ct):
                        lo = loc(c0)
                        s.op("act", lambda e, bk=bks[ti], w=w, lo=lo, jh=jh, dstT=dstT, mul=mul: e.mul(
                            out=dstT[:, jh, lo:lo + w], in_=s.ps[bk][:, 0:w], mul=mul), [f"ps{bks[ti]}"], [key])
                if col == kcol:
                    for ti, (c0, rows) in enumerate(tts):
                        bk = s.nb()
                        for kc in range(KC):
                            wt, wk = wa(kc)
                            s.mm(s.ps[bk][0:rows, 0:256], s.XT[:, kc, c0:c0 + rows], wt, kc == 0, kc == KC - 1, [wk, s.xkey(c0)], [f"ps{bk}"])
                        s.op("dve", lambda e, bk=bk, rows=rows, ti=ti, hd=hd: e.tensor_scalar(
                            out=KB[0:rows, ti, :], in0=s.ps[bk][0:rows, 0:256], scalar1=SC[0:rows, ti, 4 + hd:5 + hd], scalar2=0.0625,
                            op0=ALU.mult, op1=ALU.mult), [f"ps{bk}", "~SC"], ["~KB"])
            for (col, which) in ((vc, "v"), (oc_, "o")):
                wa = s.wgroup(W, 0, KC, col, 256)
                for ti, (c0, rows) in enumerate(tts):
                    bk = s.nb()
                    for kc in range(KC):
                        wt, wk = wa(kc)
                        s.mm(s.ps[bk][0:rows, 0:256], s.XT[:, kc, c0:c0 + rows], wt, kc == 0, kc == KC - 1, [wk, s.xkey(c0)], [f"ps{bk}"])
                    if which == "v":
                        s.cp("act", VE[0:rows, ti, 0:256], s.ps[bk][0:rows, 0:256], [f"ps{bk}"], ["~VE"])
                        s.op("dve", lambda e, rows=rows, ti=ti: e.memset(VE[0:rows, ti, 256:257], 1.0), [], ["~VE"])
                    else:
                        s.op("act", lambda e, bk=bk, rows=rows, ti=ti: e.activation(out=OS[0:rows, ti, :], in_=s.ps[bk][0:rows, 0:256], func=AF.Sigmoid),
                             [f"ps{bk}"], ["~OS"])
            for ti, (c0, rows) in enumerate(tts):
                lo = loc(c0)
                samp = c0 >= c.S
                jsm = c0 - c.S
                if samp:
                    ST_, stk = s.CSS[:], "CSS"
                    s.dma(ST_[:, :, 0:256], s.i["sC"][i, jsm, hd].rearrange("(j p) e -> p j e", p=128), [], [stk], "css")
                    s.dma(ST_[:, :, 256:257], s.i["sn"][i, jsm, hd].rearrange("(j p o) -> p j o", p=128, o=1), [], [stk], "css",
                          allow_slow_non_contiguous=True)
                else:
                    ST_, stk = CST_, ck_
                    if g == 0 and ti == 0:
                        s.op("dve", lambda e, ST_=ST_: e.memset(ST_, 0.0), [], [stk])
                    else:
                        sc_ap = AEL[:, hd:hd + 1] if ti == 0 else AE[:, hd, ti - 1:ti]
                        s.op("dve", lambda e, ST_=ST_, sc_ap=sc_ap: e.tensor_scalar(out=ST_, in0=ST_, scalar1=sc_ap, scalar2=None, op0=ALU.mult),
                             [stk, "AEL", "~AE"], [stk])
                s.cp("act", s.CB[:], ST_, [stk], ["CB"])
                bk = s.nb()
                for jh in range(2):
                    s.mm(s.ps[bk][0:rows, 0:rows], KT[:, jh, lo:lo + rows], QT[:, jh, lo:lo + rows], jh == 0, jh == 1, ["~KT", "~QT"], [f"ps{bk}"])
                s.op("dve", lambda e, bk=bk, rows=rows, ti=ti, hd=hd: e.scalar_tensor_tensor(
                    out=AT[0:rows, 0:rows], in0=s.ps[bk][0:rows, 0:rows], scalar=SC[0:rows, ti, 4 + hd:5 + hd], in1=s.MTB[0:rows, 0:rows],
                    op0=ALU.mult, op1=ALU.mult), [f"ps{bk}", "~SC", "MTB"], ["~AT"])
                bk = s.nb()
                for jh in range(2):
                    s.mm(s.ps[bk][0:rows, 0:257], QT[:, jh, lo:lo + rows], s.CB[:, jh, :], jh == 0, False, ["~QT", "CB"], [f"ps{bk}"])
                s.mm(s.ps[bk][0:rows, 0:257], AT[0:rows, 0:rows], VE[0:rows, ti, 0:257], False, True, ["~AT", "~VE"], [f"ps{bk}"])
                al = SC[0:rows, ti, hd:hd + 1]
                em = SC[0:rows, ti, 8 + hd:9 + hd]
                s1, s2, s3 = SS[0:rows, 0:1], SS[0:rows, 1:2], SS[0:rows, 2:3]
                s.op("dve", lambda e, bk=bk, rows=rows, al=al, s1=s1: e.tensor_tensor(out=s1, in0=s.ps[bk][0:rows, 256:257], in1=al, op=ALU.mult),
                     [f"ps{bk}", "~SC"], ["~SS"])
                s.op("dve", lambda e, s1=s1, s2=s2: e.tensor_scalar(out=s2, in0=s1, scalar1=-1.0, scalar2=None, op0=ALU.mult), ["~SS"], ["~SS"])
                s.op("dve", lambda e, s1=s1, s2=s2: e.tensor_tensor(out=s2, in0=s1, in1=s2, op=ALU.max), ["~SS"], ["~SS"])
                s.op("dve", lambda e, s2=s2, em=em: e.tensor_tensor(out=s2, in0=s2, in1=em, op=ALU.max), ["~SS", "~SC"], ["~SS"])
                s.op("dve", lambda e, s2=s2: e.reciprocal(out=s2, in_=s2), ["~SS"], ["~SS"])
                s.op("dve", lambda e, s2=s2, s3=s3, al=al: e.tensor_tensor(out=s3, in0=al, in1=s2, op=ALU.mult), ["~SS", "~SC"], ["~SS"])
                s.op("dve", lambda e, bk=bk, rows=rows, s3=s3: e.tensor_scalar(out=HH[0:rows, :], in0=s.ps[bk][0:rows, 0:256], scalar1=s3, scalar2=None,
                                                                         op0=ALU.mult), [f"ps{bk}", "~SS"], ["~HH"])
                s4, s5 = SS[0:rows, 3:4], SS[0:rows, 4:5]
                s.op("dve", lambda e, rows=rows, s4=s4: e.reduce_sum(out=s4, in_=HH[0:rows, :], axis=AX.X), ["~HH"], ["~SS"])
                s.op("dve", lambda e, s4=s4: e.tensor_scalar(out=s4, in0=s4, scalar1=1.0 / 256, scalar2=None, op0=ALU.mult), ["~SS"], ["~SS"])
                s.op("dve", lambda e, rows=rows, s4=s4: e.tensor_scalar(out=HH[0:rows, :], in0=HH[0:rows, :], scalar1=s4, scalar2=None, op0=ALU.subtract),
                     ["~HH", "~SS"], ["~HH"])
                s.op("dve", lambda e, s5=s5: e.memset(s5, 0.0), [], ["~SS5"])
                s.op("act", lambda e, rows=rows, s5=s5: e.activation(out=HC[0:rows, :], in_=HH[0:rows, :], func=AF.Square, accum_out=s5),
                     ["~HH"], ["~HC", "~SS5"])
                s.op("dve", lambda e, s5=s5: e.tensor_scalar(out=s5, in0=s5, scalar1=1.0 / 256, scalar2=None, op0=ALU.mult), ["~SS5"], ["~SS5"])
                s.rsqrt(s5, 1e-6, "~SS5")
                s.op("dve", lambda e, rows=rows, s5=s5, hd=hd: e.scalar_tensor_tensor(
                    out=HC[0:rows, :], in0=HH[0:rows, :], scalar=s5, in1=HG[0:rows, hd * 256:(hd + 1) * 256], op0=ALU.mult, op1=ALU.mult),
                    ["~HH", "~SS5", "~HG", "~HC"], ["~HC"])
                s.op("dve", lambda e, rows=rows, ti=ti: e.tensor_tensor(out=HN[0:rows, :], in0=HC[0:rows, :], in1=OS[0:rows, ti, :], op=ALU.mult),
                     ["~HC", "~OS"], ["~HN"])
                bk = s.nb()
                psb = s.ps[bk][:].bitcast(BF16)
                for jh in range(2):
                    s.op("pe", lambda e, psb=psb, jh=jh, rows=rows: e.transpose(psb[:, jh * 128:jh * 128 + rows], HN[0:rows, jh * 128:(jh + 1) * 128],
                                                                             s.IDB[0:rows, 0:rows]), ["~HN", "IDB"], [f"ps{bk}"])
                s.cp("act", MT[:, 8 + 2 * hd:10 + 2 * hd, lo:lo + rows], psb[:, 0:256].rearrange("p (j t) -> p j t", j=2)[:, :, 0:rows],
                     [f"ps{bk}"], ["~MT"])
                for jh in range(2):
                    bk = s.nb()
                    s.mm(s.ps[bk][:, 0:257], KB[0:rows, ti, jh * 128:(jh + 1) * 128], VE[0:rows, ti, 0:257], True, True, ["~KB", "~VE"], [f"ps{bk}"])
                    s.op("dve", lambda e, bk=bk, jh=jh, ST_=ST_: e.tensor_tensor(out=ST_[:, jh, :], in0=s.ps[bk][:, 0:257], in1=ST_[:, jh, :], op=ALU.add),
                         [f"ps{bk}", stk], [stk])
                if samp:
                    s.op("dve", lambda e, ST_=ST_, hd=hd, jsm=jsm: e.tensor_scalar(out=ST_, in0=ST_, scalar1=AES[:, hd, jsm:jsm + 1], scalar2=None,
                                                                              op0=ALU.mult), [stk, "~AES"], [stk])
                    s.outs.append(s.dma(s.o["Cs"][i, jsm, hd].rearrange("(j p) e -> p j e", p=128), ST_[:, :, 0:256], [stk], [], "cso"))
                    s.outs.append(s.dma(s.o["ns"][i, jsm, hd].rearrange("(j p o) -> p j o", p=128, o=1), ST_[:, :, 256:257], [stk], [], "cso",
                                        allow_slow_non_contiguous=True))
            if last:
                s.op("dve", lambda e, CST_=CST_, hd=hd: e.tensor_scalar(out=CST_, in0=CST_, scalar1=AE[:, hd, NCH - 1:NCH], scalar2=None, op0=ALU.mult),
                     [ck_, "~AE"], [ck_])
                s.outs.append(s.dma(s.o["Cp"][i, hd].rearrange("(j p) e -> p j e", p=128), CST_[:, :, 0:256], [ck_], [], s.newstream("cpo")))
                s.outs.append(s.dma(s.o["np"][i, hd].rearrange("(j p o) -> p j o", p=128, o=1), CST_[:, :, 256:257], [ck_], [], s.newstream("cpo"),
                                    allow_slow_non_contiguous=True))
        s.cp("dve", AEL, AE[:, :, NCH - 1], ["~AE"], ["AEL"])
        s.proj_ln(s.i["w_out"][i], MT, "~MT", ct, loc, 0, l, PRE0, DT0)

    def proj_ln(s, w2d, SRC, srckey, ct, loc, kind, l, PRE0, TMP0, tmpkeys=(), tmps=None):
        c = s.cfg
        KC = c.KC
        s.fence()
        PRE = s.scr(PRE0, KC * 512).rearrange("p (k t) -> p k t", k=KC)
        pk = [f"~PRE{k}" for k in range(KC)]
        for (c0, w) in ct:
            lo = loc(c0)
            for ocg in range(4):
                wa = s.wgroup(w2d, 0, KC, ocg * 512, 512)
                for j in range(4):
                    oc = ocg * 4 + j
                    bk = s.nb()
                    for kc in range(KC):
                        wt, wk = wa(kc)
                        s.mm(s.ps[bk][:, 0:w], wt[:, j * 128:(j + 1) * 128], SRC[:, kc, lo:lo + w], kc == 0, kc == KC - 1, [wk, srckey], [f"ps{bk}"])
                    s.op("dve", lambda e, bk=bk, w=w, oc=oc, c0=c0: e.scalar_tensor_tensor(
                        out=PRE[:, oc, 0:w], in0=s.XT[:, oc, c0:c0 + w], scalar=c.ALPHA, in1=s.ps[bk][:, 0:w], op0=ALU.mult, op1=ALU.add),
                        [f"ps{bk}", s.xkey(c0)], [pk[oc]])
            s.layer_norm(PRE[:, :, 0:w], pk, w, kind, l, lambda kc, c0=c0, w=w: (s.XT[:, kc, c0:c0 + w], s.xkey(c0)), TMP0, tmpkeys, tmps)

    def attn(s, l, g):
        c = s.cfg
        i = l // 2
        G, NS, KC, S = c.G, c.NS, c.KC, c.S
        last = (g == c.NG - 1)
        W = s.i["w_qkv"][i]
        ct = s.col_tiles(g)
        c0g = g * G
        NBP = c.NBP
        SCL = 128 ** -0.5
        s.fence()
        KT = s.scr(0, 2 * c.TT, BF16).rearrange("p (k t) -> p k t", k=4)
        VB = s.scr(4100, 4096, BF16).rearrange("p (t d) -> p t d", d=512)
        KM = s.scr(8196, 32).rearrange("p (k b) -> p k b", k=4)
        KMB = s.scr(8228, 16, BF16).rearrange("p (k b) -> p k b", k=4)
        OT0, PRE0 = 8260, 12372
        OT = s.scr(OT0, 4112, BF16).rearrange("p (k t) -> p k t", k=16)
        o_ = PRE0
        QT4 = s.scr(o_, 1028, BF16).rearrange("p (k t) -> p k t", k=4); o_ += 1028
        SMX = s.scr(o_, 2048); o_ += 2048
        PB = s.scr(o_, 1024, BF16); o_ += 1024
        PT = s.scr(o_, 1024, BF16).rearrange("p (t q) -> p t q", q=128); o_ += 1024
        GS = s.scr(o_, 32).rearrange("p (h b) -> p h b", h=4); o_ += 32
        CMP = s.scr(o_, 256).rearrange("p (h a b) -> p h a b", h=4, a=8); o_ += 256
        RK = s.scr(o_, 32).rearrange("p (h b) -> p h b", h=4); o_ += 32
        SBI = s.scr(o_, 32).rearrange("p (h b) -> p h b", h=4); o_ += 32
        OQ = s.scr(o_, 64, BF16); o_ += 64
        ST4 = s.scr(o_, 16); o_ += 16
        QROW = s.scr(o_, NS * 1024, BF16).rearrange("p (j d) -> p j d", j=NS); o_ += NS * 1024
        assert o_ <= PRE0 + 8192
        KST = [s.scr(PRE0 + 1028 + b * 512, 512) for b in range(2)]
        QSF = s.SM_[:, 192:192 + 16 * NS].rearrange("p (h j) -> p h j", h=16)
        tts = [(c0g + k * 128, 128) for k in range(G // 128)]
        if last:
            tts += [(S + j, 1) for j in range(NS)]
        wa = s.wgroup(W, 0, KC, 2048, 512)
        for j in range(4):
            bks = [s.nb() for _ in ct]
            for kc in range(KC):
                wt, wk = wa(kc)
                for ti, (c0, w) in enumerate(ct):
                    s.mm(s.ps[bks[ti]][:, 0:w], wt[:, j * 128:(j + 1) * 128], s.XT[:, kc, c0:c0 + w], kc == 0, kc == KC - 1,
                         [wk, s.xkey(c0)], [f"ps{bks[ti]}"])
            for ti, (c0, w) in enumerate(ct):
                if c0 < S:
                    b0 = c0 // 256
                    s.op("dve", lambda e, j=j, b0=b0: e.memset(KM[:, j, b0:b0 + 2], 0.0), [], ["~KM"])
                    for bb in range(2):
                        s.op("act", lambda e, bk=bks[ti], j=j, b0=b0, bb=bb, c0=c0: e.activation(
                            out=KT[:, j, c0 + bb * 256:c0 + (bb + 1) * 256], in_=s.ps[bk][:, bb * 256:(bb + 1) * 256], func=AF.Copy,
                            accum_out=KM[:, j, b0 + bb:b0 + bb + 1]), [f"ps{bks[ti]}", "~KM"], ["~KT", "~KM"])
                else:
                    s.cp("act", KT[:, j, c0:c0 + w], s.ps[bks[ti]][:, 0:w], [f"ps{bks[ti]}"], ["~KT"])
        nst = 0
        for which, col in ((("k", 2048), ("v", 2560)) if "noA" not in DBG else ()):
            wv = wa if which == "k" else s.wgroup(W, 0, KC, col, 512)
            for ti, (c0, rows) in enumerate(tts):
                bk = s.nb()
                for kc in range(KC):
                    wt, wk = wv(kc)
                    s.mm(s.ps[bk][0:rows, 0:512], s.XT[:, kc, c0:c0 + rows], wt, kc == 0, kc == KC - 1, [wk, s.xkey(c0)], [f"ps{bk}"])
                if c0 >= S:
                    jj = c0 - S
                    dst = s.KSF[0:1, 0 if which == "k" else 1, jj, :]
                    s.cp("dve", dst, s.ps[bk][0:1, 0:512], [f"ps{bk}"], ["KSF"])
                    s.outs.append(s.dma(s.o["ks" if which == "k" else "vs"][i, jj:jj + 1, :], dst, ["KSF"], [], s.newstream("kso")))
                else:
                    b = nst % 2
                    nst += 1
                    s.cp("dve", KST[b][0:rows, :], s.ps[bk][0:rows, 0:512], [f"ps{bk}"], [f"~KST{b}"])
                    s.outs.append(s.dma(s.o["kp" if which == "k" else "vp"][i, c0:c0 + rows, :], KST[b][0:rows, :], [f"~KST{b}"], [], f"kst{b}"))
                    if which == "v":
                        s.cp("act", VB[0:rows, c0 // 128, :], KST[b][0:rows, :], [f"~KST{b}"], ["~VB"])
        s.cp("dve", KMB[:, :, 0:NBP], KM[:, :, 0:NBP], ["~KM"], ["~KMB"])
        own = s.cst("own").rearrange("p (h k) -> p h k", h=2)
        for st in range(G // 512):
            s.fence()
            cs0 = c0g + st * 512
            lst = last and st == G // 512 - 1
            sct = [(cs0, 512)] + ([(S, NS)] if lst else [])
            sloc = lambda c0: 512 + (c0 - S) if c0 >= S else c0 - cs0
            for kvh in range(4):
                wq = s.wgroup(W, 0, KC, kvh * 512, 512)
                for j in range(4):
                    bks = [s.nb() for _ in sct]
                    for kc in range(KC):
                        wt, wk = wq(kc)
                        for ti, (c0, w) in enumerate(sct):
                            s.mm(s.ps[bks[ti]][:, 0:w], wt[:, j * 128:(j + 1) * 128], s.XT[:, kc, c0:c0 + w], kc == 0, kc == KC - 1,
                                 [wk, s.xkey(c0)], [f"ps{bks[ti]}"])
                    for ti, (c0, w) in enumerate(sct):
                        lo = sloc(c0)
                        s.op("act", lambda e, bk=bks[ti], j=j, lo=lo, w=w: e.mul(out=QT4[:, j, lo:lo + w], in_=s.ps[bk][:, 0:w], mul=SCL),
                             [f"ps{bks[ti]}"], ["~QT4"])
                        if c0 >= S:
                            s.op("act", lambda e, bk=bks[ti], h=kvh * 4 + j: e.mul(out=QSF[:, h, :], in_=s.ps[bk][:, 0:NS], mul=SCL),
                                 [f"ps{bks[ti]}"], ["QSF"])
                if False:
                    for jj in range(NS):
                        bk = s.nb()
                        for kc in range(KC):
                            wt, wk = wq(kc)
                            s.mm(s.ps[bk][0:1, 0:512], s.XT[:, kc, S + jj:S + jj + 1], wt, kc == 0, kc == KC - 1, [wk, "XTs"], [f"ps{bk}"])
                        s.op("act", lambda e, bk=bk, jj=jj, kvh=kvh: e.mul(out=QROW[0:1, jj, kvh * 512:(kvh + 1) * 512], in_=s.ps[bk][0:1, 0:512], mul=SCL),
                             [f"ps{bk}"], ["~QROW"])
                cfl = s.CSTATE[:].rearrange("p a b c -> p (a b c)")
                PBs = [PB, cfl[:, 0:1024].bitcast(BF16)]
                PTs = [PT, cfl[:, 1024:2048].bitcast(BF16).rearrange("p (t q) -> p t q", q=128)]
                CK = ["CST0", "CST1", "CST2", "CST3"]
                pbk = [["~PB0"], ["~PB1"] + CK]
                ptk = [["~PT0"], ["~PT1"] + CK]
                items = [(qt, j) for qt in range(4) for j in range(4)]
                SMXs = [SMX, QROW[:, :, :].rearrange("p j d -> p (j d)").bitcast(F32)[:, 0:2048] if NS * 1024 >= 2048 else SMX]

                def geom(qt):
                    t0 = cs0 + qt * 128
                    jq, hf = t0 // 256, (t0 % 256) // 128
                    return jq, hf, (jq + 1) * 256, jq >= 4

                def gate(qt):
                    jq, hf, nk, sel = geom(qt)
                    if not sel:
                        return
                    bk = s.nb()
                    for j in range(4):
                        s.mm(s.ps[bk][:, j * 8:j * 8 + jq], QT4[:, j, qt * 128:(qt + 1) * 128], KMB[:, kvh, 0:jq], True, True,
                             ["~QT4", "~KMB"], [f"ps{bk}"])
                    s.cp("dve", GS[:, :, 0:jq], s.ps[bk][:, 0:32].rearrange("p (h b) -> p h b", h=4)[:, :, 0:jq], [f"ps{bk}"], ["~GS"])
                    s.op("dve", lambda e: e.tensor_tensor(
                        out=CMP[:, :, 0:jq, 0:jq], in0=GS[:, :, 0:jq].unsqueeze(2).to_broadcast([128, 4, jq, jq]),
                        in1=GS[:, :, 0:jq].unsqueeze(3).to_broadcast([128, 4, jq, jq]), op=ALU.is_gt), ["~GS"], ["~CMP"])
                    s.op("dve", lambda e: e.reduce_sum(out=RK[:, :, 0:jq], in_=CMP[:, :, 0:jq, 0:jq], axis=AX.X), ["~CMP"], ["~RK"])
                    s.op("dve", lambda e: e.tensor_scalar(out=SBI[:, :, 0:jq], in0=RK[:, :, 0:jq], scalar1=2.5, scalar2=NEG,
                                                          op0=ALU.is_ge, op1=ALU.mult), ["~RK"], ["~SBI"])

                def st1(n):
                    qt, j = items[n]
                    jq, hf, nk, sel = geom(qt)
                    P_, pk_ = PBs[n % 2], pbk[n % 2]
                    mx, rs = ST4[:, n % 2:n % 2 + 1], ST4[:, 4 + n % 3:5 + n % 3]
                    rk = f"~ST4r{n % 3}"
                    mk = f"~ST4m{n % 2}"
                    SMX = SMXs[n % 2]
                    sk = f"~SMX{n % 2}"
                    for sl in range((nk + 511) // 512):
                        wk_ = min(512, nk - sl * 512)
                        bk = s.nb()
                        s.mm(s.ps[bk][:, 0:wk_], QT4[:, j, qt * 128:(qt + 1) * 128], KT[:, kvh, sl * 512:sl * 512 + wk_], True, True,
                             ["~QT4", "~KT"], [f"ps{bk}"])
                        for bb in range(wk_ // 256):
                            b = sl * 2 + bb
                            src = s.ps[bk][:, bb * 256:(bb + 1) * 256]
                            dst = SMX[:, b * 256:(b + 1) * 256]
                            if b == jq:
                                s.op("dve", lambda e, src=src, dst=dst: e.tensor_tensor(out=dst, in0=src, in1=own[:, hf, :], op=ALU.add),
                                     [f"ps{bk}", "CST"], [sk])
                            elif sel:
                                if sl == jq // 2:
                                    s.op("dve", lambda e, src=src, dst=dst, b=b: e.tensor_scalar(out=dst, in0=src, scalar1=SBI[:, j, b:b + 1],
                                                                                            scalar2=None, op0=ALU.add), [f"ps{bk}", "~SBI"], [sk])
                                else:
                                    s.op("act", lambda e, src=src, dst=dst, b=b: e.activation(out=dst, in_=src, func=AF.Identity, bias=SBI[:, j, b:b + 1],
                                                                                          scale=1.0), [f"ps{bk}", "~SBI"], [sk])
                            else:
                                s.cp("dve" if sl == jq // 2 else "act", dst, src, [f"ps{bk}"], [sk])
                    s.op("dve", lambda e: e.reduce_max(out=mx, in_=SMX[:, 0:nk], axis=AX.X), [sk], [mk])
                    s.op("dve", lambda e: e.tensor_scalar(out=mx, in0=mx, scalar1=-1.0, scalar2=None, op0=ALU.mult), [mk], [mk])
                    s.op("dve", lambda e: e.memset(rs, 0.0), [], [rk])
                    s.op("act", lambda e: e.activation(out=P_[:, 0:nk], in_=SMX[:, 0:nk], func=AF.Exp, bias=mx, scale=1.0, accum_out=rs),
                         [sk, mk, rk], pk_ + [rk])

                def st2(n):
                    qt, j = items[n]
                    jq, hf, nk, sel = geom(qt)
                    P_, pk_ = PBs[n % 2], pbk[n % 2]
                    T_, tk_ = PTs[n % 2], ptk[n % 2]
                    nkt = nk // 128
                    for k8 in range(0, nkt, 8):
                        n8 = min(8, nkt - k8)
                        bk = s.nb()
                        psb = s.ps[bk][:].bitcast(BF16)
                        for q in range(n8):
                            s.op("pe", lambda e, psb=psb, q=q, k8=k8: e.transpose(psb[:, q * 128:(q + 1) * 128], P_[:, (k8 + q) * 128:(k8 + q + 1) * 128],
                                                                              s.IDB[:]), pk_[0:1] + ["IDB"], [f"ps{bk}"])
                        s.cp("act" if (k8 // 8) % 2 else "dve", T_[:, k8:k8 + n8, :], psb[:, 0:n8 * 128].rearrange("p (t q) -> p t q", q=128),
                             [f"ps{bk}"], tk_)

                def st3(n):
                    qt, j = items[n]
                    jq, hf, nk, sel = geom(qt)
                    T_, tk_ = PTs[n % 2], ptk[n % 2]
                    rs = ST4[:, 4 + n % 3:5 + n % 3]
                    rk = f"~ST4r{n % 3}"
                    nkt = nk // 128
                    h = kvh * 4 + j
                    bk = s.nb()
                    for kt in range(nkt):
                        s.mm(s.ps[bk][:, 0:128], T_[:, kt, :], VB[:, kt, kvh * 128:(kvh + 1) * 128], kt == 0, kt == nkt - 1, tk_[0:1] + ["~VB"], [f"ps{bk}"])
                    s.op("dve", lambda e: e.reciprocal(out=rs, in_=rs), [rk], [rk])
                    s.op("dve", lambda e: e.tensor_scalar(out=OQ[:, :], in0=s.ps[bk][:, 0:128], scalar1=rs, scalar2=None, op0=ALU.mult),
                         [f"ps{bk}", rk], ["~OQ"])
                    bk2 = s.nb()
                    psb = s.ps[bk2][:].bitcast(BF16)
                    s.op("pe", lambda e: e.transpose(psb[:, 0:128], OQ[:, :], s.IDB[:]), ["~OQ", "IDB"], [f"ps{bk2}"])
                    s.cp("act", OT[:, h, qt * 128:(qt + 1) * 128], psb[:, 0:128], [f"ps{bk2}"], ["~OT"])

                NI = len(items) if "noB" not in DBG else 0
                for n in range(NI + 2):
                    if n < NI:
                        if items[n][1] == 0:
                            gate(items[n][0])
                        st1(n)
                    if 1 <= n <= NI:
                        st2(n - 1)
                    if 2 <= n:
                        st3(n - 2)
            if lst and "nosamp" not in DBG:
                for jj in range(NS):
                    s.sample_attn(i, jj, OT, KT, QSF, PRE0)
            cfl = s.CSTATE[:].rearrange("p a b c -> p (a b c)")
            tmps = [cfl[:, q * 512:(q + 1) * 512] for q in range(4)] + [s.RT[:, 0, :]]
            if "noproj" not in DBG:
              s.proj_ln(s.i["w_o"][i], OT, "~OT", sct, sloc, 0, l, PRE0, None, ["CST0", "CST1", "CST2", "CST3", "RT0"], tmps)

    def sample_attn(s, i, jj, OT, KTall, QSF, PRE0):
        c = s.cfg
        NP, NB = c.NPAGES, c.NB
        NP1 = NP + 1
        SCLK = ["CST0", "CST1", "CST2", "CST3"]
        s.fence()
        o_ = PRE0
        IDX = s.scr(o_, NP).bitcast(I32); o_ += NP
        IDF = s.scr(o_, NP); o_ += NP
        PTB = s.scr(o_, NP).bitcast(I32); o_ += NP
        STT = s.scr(o_, NP1 * 16).rearrange("p (g h) -> p g h", h=16); o_ += NP1 * 16
        QREP = s.scr(o_, 2048).rearrange("p (h d) -> p h d", h=16); o_ += 2048
        KS = s.scr(o_, 4 * NB).rearrange("p (k b) -> p k b", k=4); o_ += 4 * NB
        SML = s.scr(o_, 128); o_ += 128
        OSB = s.scr(o_, 512); o_ += 512
        assert o_ <= PRE0 + 5556, o_ - PRE0
        cfl0 = s.CSTATE[:].rearrange("p a b c -> p (a b c)")
        KP = [s.RT[:, 0, :], s.RT[:, 1, :], cfl0[:, 1024:1536], cfl0[:, 1536:2048]] + [QREP.rearrange("p h d -> p (h d)")[:, q * 512:(q + 1) * 512] for q in range(4)]
        KPK = [["RT0"], ["RT1"], ["~KP2"] + SCLK, ["~KP3"] + SCLK, ["~KP4"], ["~KP5"], ["~KP6"], ["~KP7"]]
        NKP = len(KP)
        TMP = s.CSTATE[:].rearrange("p a b c -> p (a b c)")[:, 0:2048].rearrange("p (h d) -> p h d", h=16)
        BB = s.CSTATE[:].rearrange("p a b c -> p (a b c)")[:, 0:16 * NB].rearrange("p (h b) -> p h b", h=16)
        ones = s.cst("ones")
        ident = s.cst("ident")
        s.dma(PTB, s.i["pt"][jj:jj + 1, :].to_broadcast([128, NP]), [], ["~PTB"], s.newstream("ptb"))
        s.cp("dve", IDF, PTB, ["~PTB"], ["~IDF"])
        s.op("dve", lambda e: e.tensor_scalar(out=IDF, in0=IDF, scalar1=128.0, scalar2=s.cst("iotaf"), op0=ALU.mult, op1=ALU.add), ["~IDF", "CST"], ["~IDF"])
        if i > 0:
            s.op("dve", lambda e: e.tensor_scalar(out=IDF, in0=IDF, scalar1=float(i * c.NPOOL * 128), scalar2=None, op0=ALU.add), ["~IDF"], ["~IDF"])
        s.cp("dve", IDX, IDF, ["~IDF"], ["~IDX"])
        cfl = s.CSTATE[:].rearrange("p a b c -> p (a b c)")
        KTs = [cfl[:, 0:256].bitcast(BF16), cfl[:, 256:512].bitcast(BF16)]
        cssf = s.CSS[:].rearrange("p a b -> p (a b)")
        KPb = [cssf[:, 0:256].bitcast(BF16), cssf[:, 256:512].bitcast(BF16)]
        QSB = SML[:, 96:96 + 8 * c.NS].bitcast(BF16).rearrange("p (h j) -> p h j", h=16)
        s.cp("dve", QSB, QSF, ["QSF"], ["~QSB"])
        onesb = s.MTB[:, 127:128]
        ksb = s.nb()
        s.reserved.add(ksb)
        sbk = None
        for pg in range(NP):
            b = pg % 2
            kb = pg % NKP
            s.op("pool", lambda e, kb=kb, pg=pg: e.indirect_dma_start(
                out=KP[kb], out_offset=None, in_=s.i["ck"], in_offset=bass.IndirectOffsetOnAxis(ap=IDX[:, pg:pg + 1], axis=0)),
                ["~IDX"], KPK[kb], stream=f"kpg{kb}")
            s.cp("act", KPb[b], KP[kb], [KPK[kb][0]], [f"~KPb{b}", "CSS"])
            tb = s.nb()
            psb = s.ps[tb][:].bitcast(BF16)
            for kvh in range(4):
                s.op("pe", lambda e, psb=psb, kvh=kvh, b=b: e.transpose(psb[:, kvh * 128:(kvh + 1) * 128], KPb[b][:, kvh * 128:(kvh + 1) * 128], s.IDB[:]),
                     [f"~KPb{b}", "IDB"], [f"ps{tb}"])
            s.cp("dve", KTs[b], psb[:, 0:512], [f"ps{tb}"], [f"~KTs{b}"] + SCLK)
            if pg % 32 == 0:
                sbk = s.nb()
                s.reserved.add(sbk)
            for kvh in range(4):
                col = (pg % 32) * 16 + kvh * 4
                s.mm(s.ps[sbk][:, col:col + 4], KTs[b][:, kvh * 128:(kvh + 1) * 128], QSB[:, kvh * 4:(kvh + 1) * 4, jj], True, True,
                     [f"~KTs{b}", "~QSB"], [f"ps{sbk}"])
            for kvh in range(4):
                s.mm(s.ps[ksb][:, kvh * NP + pg:kvh * NP + pg + 1], KPb[b][:, kvh * 128:(kvh + 1) * 128], onesb, True, True,
                     [f"~KPb{b}", "MTB"], [f"ps{ksb}"])
            if pg % 32 == 31 or pg == NP - 1:
                p0 = pg - pg % 32
                npg = pg - p0 + 1
                s.cp("dve", STT[:, p0:p0 + npg, :], s.ps[sbk][:, 0:npg * 16].rearrange("p (g h) -> p g h", h=16), [f"ps{sbk}"], ["~STT"])
                s.reserved.discard(sbk)
        kv = s.ps[ksb][:, 0:4 * NP].rearrange("p (k b t) -> p k b t", k=4, t=2)
        s.cp("dve", KS, kv[:, :, :, 0], [f"ps{ksb}"], ["~KS"])
        s.op("dve", lambda e: e.tensor_tensor(out=KS, in0=KS, in1=kv[:, :, :, 1], op=ALU.add), [f"ps{ksb}", "~KS"], ["~KS"])
        s.reserved.discard(ksb)
        s.op("dve", lambda e: e.memset(STT[:, NP, :], NEG), [], ["~STT"])
        KS32 = SML[:, 8:12]
        s.cp("dve", KS32, KTall[:, :, c.S + jj], ["~KT"], ["~SMLk"])
        bk = s.nb()
        for kvh in range(4):
            s.mm(s.ps[bk][0:1, kvh * 4:(kvh + 1) * 4], KS32[:, kvh:kvh + 1], QSF[:, kvh * 4:(kvh + 1) * 4, jj], True, True, ["~SMLk", "QSF"], [f"ps{bk}"])
        s.cp("dve", STT[0:1, NP, :], s.ps[bk][0:1, 0:16], [f"ps{bk}"], ["~STT"])
        GSS = SML[0:4, 0:8]
        X4 = s.scr(PRE0 + 5556 - 4 * NB - 8, 4 * NB)[0:4, :].rearrange("p (g b) -> p g b", g=4) if False else None
        gb = s.nb()
        for kvh in range(4):
            s.mm(s.ps[gb][0:4, kvh * NB:(kvh + 1) * NB], QSF[:, kvh * 4:(kvh + 1) * 4, jj], KS[:, kvh, :], True, True, ["QSF", "~KS"], [f"ps{gb}"])
        G4 = OSB[0:4, 0:4 * NB].rearrange("p (k b) -> p k b", k=4)
        s.cp("dve", G4, s.ps[gb][0:4, 0:4 * NB].rearrange("p (k b) -> p k b", k=4), [f"ps{gb}"], ["~OSB"])
        e4 = s.cst("e4").rearrange("p (g b) -> p g b", g=4)
        bbk = [s.nb(), s.nb()]
        X4 = TMP.rearrange("p h d -> p (h d)")[0:4, 0:4 * NB].rearrange("p (g b) -> p g b", g=4)
        for kvh in range(4):
            s.op("dve", lambda e, kvh=kvh: e.max(out=GSS, in_=G4[:, kvh, :]), ["~OSB"], ["~SML"])
            s.op("dve", lambda e, kvh=kvh: e.tensor_scalar(out=G4[:, kvh, :], in0=G4[:, kvh, :], scalar1=GSS[:, 2:3], scalar2=None, op0=ALU.is_ge),
                 ["~OSB", "~SML"], ["~OSB"])
            s.op("dve", lambda e, kvh=kvh: e.tensor_scalar(out=G4[:, kvh, :], in0=G4[:, kvh, :], scalar1=1.0, scalar2=-NEG, op0=ALU.subtract, op1=ALU.mult),
                 ["~OSB"], ["~OSB"])
            s.op("dve", lambda e, kvh=kvh: e.tensor_tensor(out=X4, in0=G4[:, kvh, :].unsqueeze(1).to_broadcast([4, 4, NB]), in1=e4[0:4], op=ALU.mult),
                 ["~OSB", "CST"], SCLK)
            half, off = kvh // 2, (kvh % 2) * 4 * NB
            s.mm(s.ps[bbk[half]][:, off:off + 4 * NB], ones[0:4, :], X4.rearrange("p g b -> p (g b)"), True, True, SCLK + ["CST"], [f"ps{bbk[half]}"])
        for half in range(2):
            s.cp("dve", BB[:, half * 8:(half + 1) * 8, :], s.ps[bbk[half]][:, 0:8 * NB].rearrange("p (h b) -> p h b", h=8), [f"ps{bbk[half]}"], SCLK)
        s.op("dve", lambda e: e.tensor_tensor(
            out=STT[:, 0:NP, :].rearrange("p (b t) h -> p b t h", t=2), in0=STT[:, 0:NP, :].rearrange("p (b t) h -> p b t h", t=2),
            in1=BB.rearrange("p h b -> p b h").unsqueeze(2).to_broadcast([128, NB, 2, 16]), op=ALU.add), SCLK + ["~STT"], ["~STT"])
        RM = SML[:, 16:32]
        s.op("dve", lambda e: e.reduce_max(out=RM, in_=STT.rearrange("p g h -> p h g"), axis=AX.X), ["~STT"], ["~SML"])
        bk = s.nb()
        s.op("pe", lambda e, bk=bk: e.transpose(s.ps[bk][0:16, 0:128], RM, ident), ["~SML", "CST"], [f"ps{bk}"])
        GM = SML[0:16, 32:33]
        s.op("dve", lambda e, bk=bk: e.reduce_max(out=GM, in_=s.ps[bk][0:16, 0:128], axis=AX.X), [f"ps{bk}"], ["~SML2"])
        DG = SML[0:16, 48:64]
        s.op("dve", lambda e: e.tensor_scalar(out=DG, in0=ident[0:16, 0:16], scalar1=GM, scalar2=-1.0, op0=ALU.mult, op1=ALU.mult), ["~SML2", "CST"], ["~SML3"])
        bk = s.nb()
        s.mm(s.ps[bk][:, 0:16], ones[0:16, :], DG, True, True, ["~SML3", "CST"], [f"ps{bk}"])
        NGM = SML[:, 64:80]
        s.cp("dve", NGM, s.ps[bk][:, 0:16], [f"ps{bk}"], ["~SML4"])
        s.op("dve", lambda e: e.tensor_tensor(out=STT, in0=STT, in1=NGM.unsqueeze(1).to_broadcast([128, NP1, 16]), op=ALU.add), ["~STT", "~SML4"], ["~STT"])
        s.op("act", lambda e: e.activation(out=STT, in_=STT, func=AF.Exp), ["~STT"], ["~STT"])
        RS = SML[:, 80:96]
        s.op("dve", lambda e: e.reduce_sum(out=RS, in_=STT.rearrange("p g h -> p h g"), axis=AX.X), ["~STT"], ["~SML5"])
        dbk = s.nb()
        s.mm(s.ps[dbk][0:16, 0:1], RS, ones[:, 0:1], True, True, ["~SML5", "CST"], [f"ps{dbk}"])
        DEN = SML[0:16, 33:34]
        s.op("dve", lambda e: e.reciprocal(out=DEN, in_=s.ps[dbk][0:16, 0:1]), [f"ps{dbk}"], ["~SML6"])
        STTb = cfl[:, 0:NP * 8].bitcast(BF16).rearrange("p (g h) -> p g h", h=16)
        s.cp("dve", STTb, STT[:, 0:NP, :], ["~STT"], ["~STTb"] + SCLK)
        VSB = OSB[0:1, 0:256].bitcast(BF16)
        PSB = OSB[0:1, 256:264].bitcast(BF16)
        s.cp("dve", VSB, s.KSF[0:1, 1, jj, :], ["KSF"], ["~OSB"])
        s.cp("dve", PSB, STT[0:1, NP, :], ["~STT"], ["~OSB"])
        obk = s.nb()
        s.reserved.add(obk)
        for pg in range(NP):
            kb = pg % NKP
            b = pg % 2
            s.op("pool", lambda e, kb=kb, pg=pg: e.indirect_dma_start(
                out=KP[kb], out_offset=None, in_=s.i["cv"], in_offset=bass.IndirectOffsetOnAxis(ap=IDX[:, pg:pg + 1], axis=0)),
                ["~IDX"], KPK[kb], stream=f"kpg{kb}")
            s.cp("act", KPb[b], KP[kb], [KPK[kb][0]], [f"~KPb{b}", "CSS"])
            s.mm(s.ps[obk][0:16, 0:512], STTb[:, pg, :], KPb[b], pg == 0, False, ["~STTb", f"~KPb{b}"], [f"ps{obk}"])
        s.mm(s.ps[obk][0:16, 0:512], PSB, VSB, False, True, ["~OSB"], [f"ps{obk}"])
        s.reserved.discard(obk)
        s.op("dve", lambda e: e.tensor_scalar(out=OSB[0:16, :], in0=s.ps[obk][0:16, 0:512], scalar1=DEN, scalar2=None, op0=ALU.mult),
             [f"ps{obk}", "~SML6"], ["~OSB"])
        selk = s.cst("selk").rearrange("p (k n) -> p k n", k=4)
        bk = s.nb()
        for kvh in range(4):
            s.mm(s.ps[bk][:, 0:16], OSB[0:16, kvh * 128:(kvh + 1) * 128], selk[0:16, kvh, :], kvh == 0, kvh == 3, ["~OSB", "CST"], [f"ps{bk}"])
        s.cp("dve", OT[:, :, 512 + jj:513 + jj], s.ps[bk][:, 0:16].unsqueeze(2), [f"ps{bk}"], ["~OT"])

    def run(s, parts=("ffn",)):
        c = s.cfg
        s.load_consts()
        s.load_x()
        for l in range(c.DEPTH):
            for g in range(c.NG):
                if "mix" in parts and l % 2 == 0:
                    s.mix(l, g)
                if "attn" in parts and l % 2 == 1:
                    s.attn(l, g)
            for g in range(c.NG):
                if "ffn" in parts:
                    s.ffn(l, g, final=(l == c.DEPTH - 1))
        s.finish()
        return s.P.emit(s.nc)


def build(cfg, parts=("ffn",)):
    nc = bass.Bass("TRN2", target_bir_lowering=False)
    with contextlib.ExitStack() as st:
        g = Gen(nc, cfg, st)
        g.run(parts)
    return nc, g


def prep_core(cfg, inp, core, carr):
    c = cfg
    sm_ = [c.NS * core + j for j in range(c.NS)]
    A = np.ascontiguousarray
    f32 = np.float32
    b_if = np.asarray(inp["b_if"], f32)
    bif = np.zeros((4, c.NMIX * 2), f32)
    for i in range(c.NMIX):
        bif[:, 2 * i] = b_if[i, 0:4]
        bif[:, 2 * i + 1] = b_if[i, 4:8]
    ps = np.asarray(inp["pool_scale"], f32)
    pscale = A(ps.reshape(c.NMIX, 8, 128).transpose(2, 0, 1).reshape(128, c.NMIX * 8))
    kinds = [np.asarray(inp[k], f32) for k in ("ln_mix_g", "ln_mix_b", "ln_ffn_g", "ln_ffn_b")]
    lnp = A(np.stack(kinds, 0).reshape(4, c.DEPTH, c.KC, 128).transpose(3, 0, 1, 2).reshape(128, 4 * c.DEPTH * c.KC))
    natt = np.asarray(inp["cache_k"]).shape[0]
    im = dict(
        xp=A(inp["x_prompt"][core]), xs=A(np.asarray(inp["x_sample"])[sm_, 0, :]),
        ck=np.asarray(inp["cache_k"]).reshape(-1, 512), cv=np.asarray(inp["cache_v"]).reshape(-1, 512),
        spool=A(np.asarray(inp["state_pool"])[:, sm_]), sC=A(np.asarray(inp["state_C"])[:, sm_]),
        sn=A(np.asarray(inp["state_n"])[:, sm_]), sm=A(np.asarray(inp["state_m"])[:, sm_]),
        pt=A(np.asarray(inp["page_table"], np.int32)[sm_]), iop=np.arange(128, dtype=np.int32)[:, None].copy(),
        w_in=np.asarray(inp["w_in_mix"]), bif=bif, w_pool=np.asarray(inp["w_pool"]), pscale=pscale,
        hng=np.asarray(inp["mlstm_norm_g"]), w_out=np.asarray(inp["w_out_mix"]),
        w_qkv=np.asarray(inp["w_qkv"]), w_o=np.asarray(inp["w_o"]), lnp=lnp,
        w_up=np.asarray(inp["w_up"]), w_down=np.asarray(inp["w_down"]), cst=carr,
    )
    return im


NCORES = 4
_CACHE = {}


def kernel(**inputs):
    cfg = Cfg(S=2048, G=1024, NS=8 // NCORES, DEPTH=4, DFF=8192, NPAGES=128, NPOOL=int(np.asarray(inputs["cache_k"]).shape[1]))
    if "nc" not in _CACHE:
        _CACHE["nc"] = build(cfg, parts=("mix", "attn", "ffn"))
    nc, g = _CACHE["nc"]
    in_maps = [prep_core(cfg, inputs, core, g.carr) for core in range(NCORES)]
    res = run_bass_kernel_spmd(nc, in_maps, core_ids=list(range(NCORES))).results
    f32 = np.float32
    cat = lambda k: np.stack([np.asarray(res[c][k], f32) for c in range(NCORES)], 0)
    NA, NM, NS = cfg.NATT, cfg.NMIX, cfg.NS
    y_prompt = cat("yp")
    y_sample = cat("ys").reshape(8, 1, cfg.D)
    k_prompt = cat("kp").transpose(1, 0, 2, 3).reshape(NA, 4, cfg.S, 4, 128)
    v_prompt = cat("vp").transpose(1, 0, 2, 3).reshape(NA, 4, cfg.S, 4, 128)
    k_sample = cat("ks").transpose(1, 0, 2, 3).reshape(NA, 8, 1, 4, 128)
    v_sample = cat("vs").transpose(1, 0, 2, 3).reshape(NA, 8, 1, 4, 128)
    pool_prompt = cat("poolp").transpose(1, 0, 2, 3)
    C_prompt = cat("Cp").transpose(1, 0, 2, 3, 4)
    n_prompt = cat("np").transpose(1, 0, 2, 3)
    m_prompt = cat("mp").transpose(1, 0, 2)
    pool_sample = cat("pools").transpose(1, 0, 2, 3, 4).reshape(NM, 8, 15, 1024)
    C_sample = cat("Cs").transpose(1, 0, 2, 3, 4, 5).reshape(NM, 8, 4, 256, 256)
    n_sample = cat("ns").transpose(1, 0, 2, 3, 4).reshape(NM, 8, 4, 256)
    m_sample = cat("ms").transpose(1, 0, 2, 3).reshape(NM, 8, 4)
    outs = (y_prompt, y_sample, k_prompt, v_prompt, k_sample, v_sample, pool_prompt, C_prompt, n_prompt, m_prompt,
            pool_sample, C_sample, n_sample, m_sample)
    return tuple(np.ascontiguousarray(o, dtype=f32) for o in outs)
```

```python
import contextlib
import numpy as np
import concourse.bass as bass
import concourse.mybir as mybir
from concourse.bass_utils import run_bass_kernel_spmd

F32 = mybir.dt.float32
BF16 = mybir.dt.bfloat16
I32 = mybir.dt.int32
AF = mybir.ActivationFunctionType
ALU = mybir.AluOpType
AX = mybir.AxisListType

ENGS = ("pe", "act", "dve", "pool", "sp")
NEG = -30000.0
DBG = set()


class Prog:
    def __init__(self):
        self.ops = []
        self.last_w = {}
        self.readers = {}

    def op(self, eng, fn, R=(), W=(), stream=None, extra=()):
        i = len(self.ops)
        deps = set(extra)
        for r in R:
            w = self.last_w.get(r)
            if w is not None:
                deps.add(w)
        for w_ in W:
            w = self.last_w.get(w_)
            if w is not None:
                deps.add(w)
            rd = self.readers.get(w_)
            if rd:
                for lst in rd.values():
                    deps.update(lst)
        for r in R:
            rd = self.readers.setdefault(r, {})
            if stream is None:
                rd[eng] = [i]
            else:
                rd.setdefault("dma:" + stream, []).append(i)
        for w_ in W:
            self.last_w[w_] = i
            self.readers[w_] = {}
        deps.discard(i)
        self.ops.append(dict(eng=eng, fn=fn, deps=deps, stream=stream))
        return i

    def emit(self, nc):
        ops = self.ops

        def skip(od, o):
            return od["stream"] is None and od["eng"] == "pe" and o["eng"] == "pe" and o["stream"] is None

        needed = set()
        for o in ops:
            for d in o["deps"]:
                if not skip(ops[d], o):
                    needed.add(d)
        cnt = {}
        for i, o in enumerate(ops):
            if o["stream"] is not None:
                k = "dma:" + o["stream"]
                cnt[k] = cnt.get(k, 0) + 16
                o["sig"] = (k, cnt[k])
            elif i in needed:
                k = o["eng"]
                cnt[k] = cnt.get(k, 0) + 1
                o["sig"] = (k, cnt[k])
            else:
                o["sig"] = None
        known = {e: {} for e in ENGS}
        for o in ops:
            e = o["eng"]
            w = {}
            for d in o["deps"]:
                od = ops[d]
                if od["sig"] is None or skip(od, o):
                    continue
                k, v = od["sig"]
                if known[e].get(k, 0) >= v:
                    continue
                w[k] = max(w.get(k, 0), v)
            for k, v in w.items():
                known[e][k] = v
            o["waits"] = sorted(w.items())
        per = {e: [o for o in ops if o["eng"] == e] for e in ENGS}
        with contextlib.ExitStack() as st:
            sems = {k: st.enter_context(nc.semaphore("s_" + "".join(ch if ch.isalnum() else "_" for ch in k))) for k in sorted(cnt)}
            block = st.enter_context(nc.Block())

            def run(engobj, lst):
                for o in lst:
                    for k, v in o["waits"]:
                        engobj.wait_ge(sems[k], v)
                    ins = o["fn"](engobj)
                    if o["sig"] is not None and ins is not None:
                        ins.then_inc(sems[o["sig"][0]], 16 if o["stream"] is not None else 1)

            @block.tensor
            def _(e):
                run(e, per["pe"])

            @block.scalar
            def _(e):
                run(e, per["act"])

            @block.vector
            def _(e):
                run(e, per["dve"])

            @block.gpsimd
            def _(e):
                run(e, per["pool"])

            @block.sync
            def _(e):
                run(e, per["sp"])
        return cnt


class Cfg:
    def __init__(s, S=2048, G=1024, NS=2, DEPTH=4, DFF=8192, NPAGES=128, NPOOL=1280, NSLOT=7):
        s.D = 2048
        s.KC = 16
        s.S, s.G, s.NS, s.DEPTH, s.DFF, s.NPAGES, s.NPOOL, s.NSLOT = S, G, NS, DEPTH, DFF, NPAGES, NPOOL, NSLOT
        s.NG = S // G
        s.TT = S + NS
        s.GN = G + NS
        s.NMIX = (DEPTH + 1) // 2
        s.NATT = max(1, DEPTH // 2)
        s.ALPHA = (2 * DEPTH) ** 0.25
        s.NB = NPAGES // 2
        s.NBP = S // 256
        assert G % 512 == 0 and S % G == 0


POOL_W = (2, 4, 8, 16)


def const_arrays(cfg):
    c = {}
    c["ident"] = np.eye(128, dtype=np.float32)
    s_ = np.arange(128)
    c["maskT"] = (s_[:, None] <= s_[None, :]).astype(np.float32)
    own = np.zeros((128, 2, 256), np.float32)
    for hf in range(2):
        own[:, hf, :] = np.where(np.arange(256)[None, :] <= (hf * 128 + s_)[:, None], 0.0, NEG)
    c["own"] = own.reshape(128, 512)
    sel4 = np.zeros((128, 4, 128), np.float32)
    for hd in range(4):
        sel4[hd, hd, :] = 1.0
    c["sel4"] = sel4.reshape(128, 512)
    e4 = np.zeros((128, 4, cfg.NB), np.float32)
    for hd in range(4):
        e4[hd, hd, :] = 1.0
    c["e4"] = e4.reshape(128, 4 * cfg.NB)
    selk = np.zeros((128, 4, 16), np.float32)
    for k in range(16):
        selk[k, k // 4, k] = 1.0
    c["selk"] = selk.reshape(128, 64)
    sb = np.full((128, 1), NEG, np.float32)
    sb[0, 0] = 0.0
    c["selfb"] = sb
    c["ones"] = np.ones((128, 128), np.float32)
    c["iotaf"] = np.arange(128, dtype=np.float32)[:, None].copy()
    names = ["ident", "maskT", "own", "sel4", "e4", "selk", "selfb", "ones", "iotaf"]
    offs = {}
    o = 0
    for n in names:
        offs[n] = (o, c[n].shape[1])
        o += c[n].shape[1]
    arr = np.concatenate([c[n] for n in names], axis=1)
    return arr, offs


class Gen:
    def __init__(s, nc, cfg, st):
        s.nc, s.cfg, s.P = nc, cfg, Prog()
        c = cfg
        D = c.D
        dt = nc.dram_tensor

        def din(name, shape, dtype=F32):
            return dt(name, list(shape), dtype, kind="ExternalInput").ap()

        def dout(name, shape, dtype=F32):
            return dt(name, list(shape), dtype, kind="ExternalOutput").ap()

        s.carr, s.coff = const_arrays(cfg)
        NC_ = s.carr.shape[1]
        s.i = dict(
            xp=din("xp", [c.S, D]), xs=din("xs", [c.NS, D]),
            ck=din("ck", [c.NATT * c.NPOOL * 128, 512]), cv=din("cv", [c.NATT * c.NPOOL * 128, 512]),
            spool=din("spool", [c.NMIX, c.NS, 15, 1024]), sC=din("sC", [c.NMIX, c.NS, 4, 256, 256]),
            sn=din("sn", [c.NMIX, c.NS, 4, 256]), sm=din("sm", [c.NMIX, c.NS, 4]),
            pt=din("pt", [c.NS, c.NPAGES], I32), iop=din("iop", [128, 1], I32),
            w_in=din("w_in", [c.NMIX, D, 5128]), bif=din("bif", [4, c.NMIX * 2]),
            w_pool=din("w_pool", [c.NMIX, 4, 256, 256]), pscale=din("pscale", [128, c.NMIX * 8]),
            hng=din("hng", [c.NMIX, 1024]), w_out=din("w_out", [c.NMIX, D, D]),
            w_qkv=din("w_qkv", [c.NATT, D, 3072]), w_o=din("w_o", [c.NATT, D, D]),
            lnp=din("lnp", [128, 4 * c.DEPTH * c.KC]),
            w_up=din("w_up", [c.DEPTH, D, c.DFF]), w_down=din("w_down", [c.DEPTH, c.DFF, D]),
            cst=din("cst", [128, NC_]),
        )
        s.o = dict(
            yp=dout("yp", [c.S, D]), ys=dout("ys", [c.NS, D]),
            kp=dout("kp", [c.NATT, c.S, 512]), vp=dout("vp", [c.NATT, c.S, 512]),
            ks=dout("ks", [c.NATT, c.NS, 512]), vs=dout("vs", [c.NATT, c.NS, 512]),
            poolp=dout("poolp", [c.NMIX, 15, 1024]), Cp=dout("Cp", [c.NMIX, 4, 256, 256]),
            np=dout("np", [c.NMIX, 4, 256]), mp=dout("mp", [c.NMIX, 4]),
            pools=dout("pools", [c.NMIX, c.NS, 15, 1024]), Cs=dout("Cs", [c.NMIX, c.NS, 4, 256, 256]),
            ns=dout("ns", [c.NMIX, c.NS, 4, 256]), ms=dout("ms", [c.NMIX, c.NS, 4]),
        )
        sb = lambda name, shape, dtype: st.enter_context(nc.sbuf_tensor(name, list(shape), dtype))
        s.XT = sb("XT", [128, c.KC, c.TT], BF16)
        s.ring = [sb(f"ring{i}", [128, 2048], BF16) for i in range(c.NSLOT)]
        s.SCRN = 21024
        s.SCR = sb("SCR", [128, s.SCRN], F32)
        s.CST = sb("CST", [128, NC_], F32)
        s.IDB = sb("IDB", [128, 128], BF16)
        s.MTB = sb("MTB", [128, 128], BF16)
        s.LNP = sb("LNP", [128, 4 * c.DEPTH * c.KC], F32)
        s.RT = sb("RT", [128, 2, 512], F32)
        s.SM_ = sb("SM_", [128, 256], F32)
        s.DUMMY = sb("DUMMY", [128, 8], F32)
        s.CSTATE = sb("CSTATE", [128, 4, 2, 257], F32)
        s.CSS = sb("CSS", [128, 2, 257], F32)
        s.CB = sb("CB", [128, 2, 257], BF16)
        s.KSF = sb("KSF", [1, 2, c.NS, 512], F32)
        s.ps = [st.enter_context(nc.psum_tensor(f"ps{i}", [128, 512], F32)) for i in range(8)]
        s.rslot = 0
        s.bank = 0
        s.uid = 0
        s.outs = []
        s.reserved = set()

    def cst(s, name):
        o, n = s.coff[name]
        return s.CST[:, o:o + n]

    def scr(s, off, nwords, dtype=F32):
        ap = s.SCR[:, off:off + nwords]
        return ap if dtype == F32 else ap.bitcast(dtype)

    def nb(s):
        while True:
            b = s.bank
            s.bank = (s.bank + 1) % 8
            if b not in s.reserved:
                return b

    def wtile(s, src_ap, nk, ncol):
        sl = s.rslot
        s.rslot = (s.rslot + 1) % s.cfg.NSLOT
        dst = s.ring[sl][:, 0:nk * ncol].rearrange("p (k c) -> p k c", k=nk)
        s.P.op("pool", lambda e: e.dma_start(out=dst, in_=src_ap), R=[], W=[f"ring{sl}"], stream=f"ring{sl}")
        return dst, f"ring{sl}"

    def wgroup(s, w2d, r0, nkc, c0, ncol):
        per = max(1, 2048 // ncol)
        tiles = []
        for k0 in range(0, nkc, per):
            nk = min(per, nkc - k0)
            src = w2d[(r0 + k0) * 128:(r0 + k0 + nk) * 128, c0:c0 + ncol].rearrange("(k p) c -> p k c", p=128)
            tiles.append(s.wtile(src, nk, ncol))

        def acc(kc):
            t, key = tiles[kc // per]
            return t[:, kc % per, :], key
        return acc

    def op(s, eng, fn, R=(), W=(), stream=None, extra=()):
        R = list(R)
        if any(k.startswith("~") for k in R) or any(k.startswith("~") for k in W):
            R.append("~F")
        return s.P.op(eng, fn, R, list(W), stream, extra)

    def fence(s):
        s.P.op("dve", lambda e: e.memset(s.DUMMY[0:1, 0:1], 0.0), [], ["~F"])

    def cp(s, eng, out, in_, R, W):
        if eng == "act":
            s.op("act", lambda e: e.copy(out=out, in_=in_), R, W)
        else:
            s.op(eng, lambda e: e.tensor_copy(out=out, in_=in_), R, W)

    def rsqrt(s, ap, eps, key):
        s.op("dve", lambda e: e.tensor_scalar(out=ap, in0=ap, scalar1=eps, scalar2=None, op0=ALU.add), [key], [key])
        s.op("act", lambda e: e.activation(out=ap, in_=ap, func=AF.Sqrt), [key], [key])
        s.op("dve", lambda e: e.reciprocal(out=ap, in_=ap), [key], [key])

    def mm(s, out, lhsT, rhs, start, stop, R, W):
        s.op("pe", lambda e: e.matmul(out, lhsT, rhs, start=start, stop=stop), R, W)

    def dma(s, out, in_, R, W, stream, q="sp", **kw):
        return s.op(q, lambda e: e.dma_start(out=out, in_=in_, **kw), R, W, stream=stream)

    def newstream(s, base):
        s.uid += 1
        return f"{base}{s.uid}"

    def col_tiles(s, g):
        c = s.cfg
        t = [(g * c.G + i * 512, 512) for i in range(c.G // 512)]
        if g == c.NG - 1:
            t.append((c.S, c.NS))
        return t

    def tok_tiles(s, g):
        c = s.cfg
        t = [(g * c.G + i * 128, 128) for i in range(c.G // 128)]
        if g == c.NG - 1:
            t.append((c.S, c.NS))
        return t

    def xkey(s, col):
        return "XTs" if col >= s.cfg.S else f"XT{col // s.cfg.G}"

    def load_consts(s):
        c = s.cfg
        s.dma(s.CST[:], s.i["cst"], [], ["CST"], "ld_cst")
        s.dma(s.LNP[:], s.i["lnp"], [], ["LNP"], "ld_lnp")
        s.op("dve", lambda e: e.tensor_copy(out=s.IDB[:], in_=s.cst("ident")), ["CST"], ["IDB"])
        s.op("dve", lambda e: e.tensor_copy(out=s.MTB[:], in_=s.cst("maskT")), ["CST"], ["MTB"])

    def load_x(s):
        c = s.cfg
        XL = [s.scr(i * 2048, 2048) for i in range(2)]
        tiles = [(t * 128, 128, s.i["xp"][t * 128:(t + 1) * 128, :]) for t in range(c.S // 128)]
        tiles.append((c.S, c.NS, s.i["xs"]))
        for n, (c0, rows, src) in enumerate(tiles):
            b = n % 2
            s.dma(XL[b][0:rows, :], src, [], [f"~XL{b}"], f"~XL{b}")
            for k4 in range(c.KC // 4):
                bk = s.nb()
                for j in range(4):
                    kc = k4 * 4 + j
                    s.op("pe", lambda e, bk=bk, j=j, kc=kc, b=b, rows=rows: e.transpose(
                        s.ps[bk][:, j * 128:j * 128 + rows], XL[b][0:rows, kc * 128:(kc + 1) * 128],
                        s.cst("ident")[0:rows, 0:rows]), [f"~XL{b}", "CST"], [f"ps{bk}"])
                src_ps = s.ps[bk][:].rearrange("p (j t) -> p j t", j=4)[:, :, 0:rows]
                dst = s.XT[:, k4 * 4:(k4 + 1) * 4, c0:c0 + rows]
                s.cp("act" if k4 % 2 else "dve", dst, src_ps, [f"ps{bk}"], [s.xkey(c0)])

    def lnp(s, kind, l, kc):
        c = s.cfg
        o = (kind * c.DEPTH + l) * c.KC + kc
        return s.LNP[:, o:o + 1]

    def layer_norm(s, pre, prekeys, ncols, kind, l, dst_fn, tmp_off, tmpkeys=(), tmps=None):
        c = s.cfg
        n = ncols
        tk = list(tmpkeys)
        if tmps is None:
            tmps = [s.scr(tmp_off + q * 512, 512) for q in range(5)]
        mean, rstd, t1 = tmps[0][:, 0:n], tmps[1][:, 0:n], tmps[2][:, 0:n]
        sq = [tmps[3][:, 0:n], tmps[4][:, 0:n]]
        b1, b2 = s.nb(), s.nb()
        onesD = s.cst("ones")
        for kc in range(c.KC):
            s.op("act", lambda e, kc=kc: e.activation(out=sq[kc % 2], in_=pre[:, kc, :], func=AF.Square),
                 [prekeys[kc]], [f"~lnsq{kc % 2}"] + tk)
            s.mm(s.ps[b1][:, 0:n], onesD, pre[:, kc, :], kc == 0, kc == c.KC - 1, [prekeys[kc], "CST"], [f"ps{b1}"])
            s.mm(s.ps[b2][:, 0:n], onesD, sq[kc % 2], kc == 0, kc == c.KC - 1, [f"~lnsq{kc % 2}", "CST"], [f"ps{b2}"])
        invD = 1.0 / c.D
        s.op("dve", lambda e: e.tensor_scalar(out=mean, in0=s.ps[b1][:, 0:n], scalar1=invD, scalar2=None, op0=ALU.mult),
             [f"ps{b1}"], ["~lnmean"] + tk)
        s.op("dve", lambda e: e.tensor_tensor(out=t1, in0=mean, in1=mean, op=ALU.mult), ["~lnmean"], ["~lnt1"] + tk)
        s.op("dve", lambda e: e.scalar_tensor_tensor(out=rstd, in0=s.ps[b2][:, 0:n], scalar=invD, in1=t1,
                                                     op0=ALU.mult, op1=ALU.subtract), [f"ps{b2}", "~lnt1"], ["~lnrstd"] + tk)
        s.rsqrt(rstd, 1e-5, "~lnrstd")
        for kc in range(c.KC):
            out, okey = dst_fn(kc)
            s.op("dve", lambda e, kc=kc: e.tensor_tensor(out=pre[:, kc, :], in0=pre[:, kc, :], in1=mean, op=ALU.subtract),
                 [prekeys[kc], "~lnmean"], [prekeys[kc]])
            s.op("dve", lambda e, kc=kc: e.tensor_tensor(out=pre[:, kc, :], in0=pre[:, kc, :], in1=rstd, op=ALU.mult),
                 [prekeys[kc], "~lnrstd"], [prekeys[kc]])
            s.op("act", lambda e, kc=kc, out=out: e.activation(out=out, in_=pre[:, kc, :], func=AF.Identity,
                                                               scale=s.lnp(kind, l, kc), bias=s.lnp(kind + 1, l, kc)),
                 [prekeys[kc], "LNP"], [okey] if okey != prekeys[kc] else [okey])

    def ffn(s, l, g, final):
        c = s.cfg
        s.fence()
        ct = s.col_tiles(g)
        c0g = g * c.G

        def loc(c0):
            return c.G + (c0 - c.S) if c0 >= c.S else c0 - c0g
        ACC = s.scr(0, c.KC * c.GN).rearrange("p (k t) -> p k t", k=c.KC)
        H = s.scr(c.KC * c.GN, 8 * c.GN // 2 + 8, BF16)[:, 0:8 * c.GN].rearrange("p (k t) -> p k t", k=8)
        akeys = [f"~ACC{k}" for k in range(c.KC)]
        for (c0, w) in ct:
            lo = loc(c0)
            for kc in range(c.KC):
                if kc % 2:
                    s.op("act", lambda e, kc=kc, lo=lo, c0=c0, w=w: e.mul(out=ACC[:, kc, lo:lo + w], in_=s.XT[:, kc, c0:c0 + w], mul=c.ALPHA),
                         [s.xkey(c0)], [akeys[kc]])
                else:
                    s.op("dve", lambda e, kc=kc, lo=lo, c0=c0, w=w: e.tensor_scalar(
                        out=ACC[:, kc, lo:lo + w], in0=s.XT[:, kc, c0:c0 + w], scalar1=c.ALPHA, scalar2=None, op0=ALU.mult),
                        [s.xkey(c0)], [akeys[kc]])
        wu, wd = s.i["w_up"][l], s.i["w_down"][l]
        nrt = 0
        for sbk in range(c.DFF // 1024):
            for cg in range(2):
                wa = s.wgroup(wu, 0, c.KC, sbk * 1024 + cg * 512, 512)
                for j in range(4):
                    hc = cg * 4 + j
                    bks = [s.nb() for _ in ct]
                    for kc in range(c.KC):
                        wt, wk = wa(kc)
                        for ti, (c0, w) in enumerate(ct):
                            s.mm(s.ps[bks[ti]][:, 0:w], wt[:, j * 128:(j + 1) * 128], s.XT[:, kc, c0:c0 + w],
                                 kc == 0, kc == c.KC - 1, [wk, s.xkey(c0)], [f"ps{bks[ti]}"])
                    for ti, (c0, w) in enumerate(ct):
                        lo, bk = loc(c0), bks[ti]
                        r = nrt % 2
                        nrt += 1
                        s.op("act", lambda e, bk=bk, w=w, r=r: e.activation(out=s.RT[:, r, 0:w], in_=s.ps[bk][:, 0:w], func=AF.Relu),
                             [f"ps{bk}"], [f"RT{r}"])
                        s.op("act", lambda e, w=w, r=r, hc=hc, lo=lo: e.activation(out=H[:, hc, lo:lo + w], in_=s.RT[:, r, 0:w], func=AF.Square),
                             [f"RT{r}"], [f"~H{hc}"])
            for ocg in range(4):
                wa = s.wgroup(wd, sbk * 8, 8, ocg * 512, 512)
                for j in range(4):
                    oc = ocg * 4 + j
                    bks = [s.nb() for _ in ct]
                    for kk in range(8):
                        wt, wk = wa(kk)
                        for ti, (c0, w) in enumerate(ct):
                            lo = loc(c0)
                            s.mm(s.ps[bks[ti]][:, 0:w], wt[:, j * 128:(j + 1) * 128], H[:, kk, lo:lo + w],
                                 kk == 0, kk == 7, [wk, f"~H{kk}"], [f"ps{bks[ti]}"])
                    for ti, (c0, w) in enumerate(ct):
                        lo, bk = loc(c0), bks[ti]
                        s.op("dve", lambda e, bk=bk, w=w, oc=oc, lo=lo: e.tensor_tensor(
                            out=ACC[:, oc, lo:lo + w], in0=s.ps[bk][:, 0:w], in1=ACC[:, oc, lo:lo + w], op=ALU.add),
                            [f"ps{bk}", akeys[oc]], [akeys[oc]])
        tmp_off = c.KC * c.GN
        hkeys = [f"~H{k}" for k in range(8)]
        for (c0, w) in ct:
            lo = loc(c0)
            pre = ACC[:, :, lo:lo + w]
            if not final:
                s.layer_norm(pre, akeys, w, 2, l, lambda kc, c0=c0, w=w: (s.XT[:, kc, c0:c0 + w], s.xkey(c0)), tmp_off, hkeys)
            else:
                s.layer_norm(pre, akeys, w, 2, l, lambda kc, lo=lo, w=w: (ACC[:, kc, lo:lo + w], akeys[kc]), tmp_off, hkeys)
                s.store_y(ACC, akeys, lo, c0, w, tmp_off + 2560, hkeys)

    def store_y(s, ACC, akeys, lo, c0, w, yoff, hkeys):
        c = s.cfg
        out = s.o["ys"] if c0 >= c.S else s.o["yp"]
        r0 = 0 if c0 >= c.S else c0
        YT = [s.scr(yoff + q * 512, 512) for q in range(4)]
        for t0 in range(0, w, 128):
            rows = min(128, w - t0)
            for k4 in range(c.KC // 4):
                bk = s.nb()
                for j in range(4):
                    kc = k4 * 4 + j
                    s.op("pe", lambda e, bk=bk, j=j, kc=kc, rows=rows, t0=t0: e.transpose(
                        s.ps[bk][0:rows, j * 128:(j + 1) * 128], ACC[:, kc, lo + t0:lo + t0 + rows], s.cst("ident")),
                        [akeys[kc], "CST"], [f"ps{bk}"])
                s.cp("dve" if k4 % 2 else "act", YT[k4][0:rows, :], s.ps[bk][0:rows, :], [f"ps{bk}"], [f"~YT{k4}"] + hkeys)
                s.outs.append(s.op("sp", lambda e, rows=rows, t0=t0, k4=k4: e.dma_start(
                    out=out[r0 + t0:r0 + t0 + rows, k4 * 512:(k4 + 1) * 512], in_=YT[k4][0:rows, :]), [f"~YT{k4}"], [], stream=f"YT{k4}"))

    def finish(s):
        s.op("sp", lambda e: None, [], [], extra=s.outs)

    def mix(s, l, g):
        c = s.cfg
        i = l // 2
        G, NS, GN, KC = c.G, c.NS, c.GN, c.KC
        last = (g == c.NG - 1)
        W = s.i["w_in"][i]
        ct = s.col_tiles(g)
        NCH = G // 128
        tts = [(g * G + k * 128, 128) for k in range(NCH)]
        if last:
            tts += [(c.S + j, 1) for j in range(NS)]
        NT = NCH + NS
        c0g = g * G
        ns_ = NS if last else 0

        def loc(c0):
            return G + (c0 - c.S) if c0 >= c.S else c0 - c0g
        s.fence()
        MT = s.scr(0, 8208, BF16)[:, 0:KC * GN].rearrange("p (k t) -> p k t", k=KC)
        PRE0, DT0, SM0 = 8208, 16416, 20520
        DT = s.scr(DT0, 4104, BF16)[:, 0:8 * GN].rearrange("p (k t) -> p k t", k=8)
        SC = s.scr(SM0, NT * 12).rearrange("p (t k) -> p t k", k=12)
        AE = s.scr(SM0 + 120, 4 * NCH).rearrange("p (h k) -> p h k", h=4)
        AES = s.scr(SM0 + 160, 4 * NS).rearrange("p (h k) -> p h k", h=4)
        UES = s.scr(SM0 + 180, 8 * NS * 16).rearrange("p (c j t) -> p c j t", c=8, j=NS)
        BIF = s.SM_[0:4, 0:4]
        PSC = s.SM_[:, 8:24]
        MCAR = s.SM_[0:4, 24:25]
        SMP = s.SM_[0:4, 32:32 + NS]
        AEL = s.SM_[:, 40:44]
        UCAR = s.SM_[:, 64:184].rearrange("p (c t) -> p c t", c=8)
        if g == 0:
            s.dma(s.SM_[0:4, 0:2], s.i["bif"][:, 2 * i:2 * i + 2], [], ["BIF"], s.newstream("bif"))
            s.dma(PSC[:, 0:8], s.i["pscale"][:, 8 * i:8 * i + 8], [], ["PSC"], s.newstream("psc"))
        gt = [s.scr(PRE0 + k * GN, GN) for k in range(8)]
        IG, LF, MM, FF, T1, T2, AL, BE = [t[0:4, :] for t in gt]
        gk = ["~gIG", "~gLF", "~gMM", "~gFF", "~gT1", "~gT2", "~gAL", "~gBE"]
        wg, wgk = s.wtile(W[:, 5120:5128].rearrange("(k p) c -> p k c", p=128), KC, 8)
        for (c0, w) in ct:
            lo = loc(c0)
            b1, b2 = s.nb(), s.nb()
            for kc in range(KC):
                s.mm(s.ps[b1][0:4, 0:w], wg[:, kc, 0:4], s.XT[:, kc, c0:c0 + w], kc == 0, kc == KC - 1, [wgk, s.xkey(c0)], [f"ps{b1}"])
                s.mm(s.ps[b2][0:4, 0:w], wg[:, kc, 4:8], s.XT[:, kc, c0:c0 + w], kc == 0, kc == KC - 1, [wgk, s.xkey(c0)], [f"ps{b2}"])
            s.op("dve", lambda e, b1=b1, lo=lo, w=w: e.tensor_scalar(out=IG[:, lo:lo + w], in0=s.ps[b1][0:4, 0:w], scalar1=BIF[:, 0:1],
                                                                  scalar2=None, op0=ALU.add), [f"ps{b1}", "BIF"], ["~gIG"])
            s.op("dve", lambda e, b2=b2, lo=lo, w=w: e.tensor_scalar(out=T1[:, lo:lo + w], in0=s.ps[b2][0:4, 0:w], scalar1=BIF[:, 1:2],
                                                                  scalar2=None, op0=ALU.add), [f"ps{b2}", "BIF"], ["~gT1"])
        n = G + ns_
        s.op("act", lambda e: e.activation(out=T1[:, 0:n], in_=T1[:, 0:n], func=AF.Exp, scale=-1.0), ["~gT1"], ["~gT1"])
        s.op("dve", lambda e: e.tensor_scalar(out=T1[:, 0:n], in0=T1[:, 0:n], scalar1=1.0, scalar2=None, op0=ALU.add), ["~gT1"], ["~gT1"])
        s.op("act", lambda e: e.activation(out=T1[:, 0:n], in_=T1[:, 0:n], func=AF.Ln), ["~gT1"], ["~gT1"])
        s.op("dve", lambda e: e.tensor_scalar(out=LF[:, 0:n], in0=T1[:, 0:n], scalar1=-1.0, scalar2=None, op0=ALU.mult), ["~gT1"], ["~gLF"])
        s.op("dve", lambda e: e.memset(T2[:, 0:n], 0.0), [], ["~gT2"])
        if g == 0:
            s.op("dve", lambda e: e.memset(MCAR, 0.0), [], ["MCAR"])
        s.op("dve", lambda e: e.tensor_tensor_scan(out=FF[:, 0:G], data0=LF[:, 0:G], data1=T2[:, 0:G], initial=0.0, op0=ALU.add, op1=ALU.add),
             ["~gLF", "~gT2"], ["~gFF"])
        s.op("dve", lambda e: e.tensor_tensor_scan(out=MM[:, 0:G], data0=LF[:, 0:G], data1=IG[:, 0:G], initial=MCAR, op0=ALU.add, op1=ALU.max),
             ["~gLF", "~gIG", "MCAR"], ["~gMM"])
        s.op("dve", lambda e: e.memset(T1[:, 0:n], 0.0), ["~gLF"], ["~gT1"])
        s.op("dve", lambda e: e.tensor_copy(out=T2[:, 0:128], in_=MCAR.to_broadcast([4, 128])), ["MCAR"], ["~gT2"])
        if NCH > 1:
            v3 = lambda t: t[:, 0:G].rearrange("p (c t) -> p c t", t=128)
            s.op("dve", lambda e: e.tensor_copy(out=v3(T1)[:, 1:NCH, :], in_=v3(FF)[:, 0:NCH - 1, 127:128].to_broadcast([4, NCH - 1, 128])),
                 ["~gFF"], ["~gT1"])
            s.op("dve", lambda e: e.tensor_copy(out=v3(T2)[:, 1:NCH, :], in_=v3(MM)[:, 0:NCH - 1, 127:128].to_broadcast([4, NCH - 1, 128])),
                 ["~gMM"], ["~gT2"])
        s.op("dve", lambda e: e.tensor_copy(out=MCAR, in_=MM[:, G - 1:G]), ["~gMM", "~gT2"], ["MCAR"])
        if last:
            s.dma(SMP, s.i["sm"][i].rearrange("s h -> h s"), [], ["SMP"], s.newstream("smp"), allow_slow_non_contiguous=True)
            s.op("dve", lambda e: e.tensor_copy(out=T2[:, G:n], in_=SMP), ["SMP"], ["~gT2"])
            s.op("dve", lambda e: e.tensor_tensor(out=MM[:, G:n], in0=SMP, in1=LF[:, G:n], op=ALU.add), ["SMP", "~gLF"], ["~gMM"])
            s.op("dve", lambda e: e.tensor_tensor(out=MM[:, G:n], in0=MM[:, G:n], in1=IG[:, G:n], op=ALU.max), ["~gIG", "~gMM"], ["~gMM"])
            s.op("dve", lambda e: e.tensor_copy(out=FF[:, G:n], in_=LF[:, G:n]), ["~gLF"], ["~gFF"])
            s.outs.append(s.dma(s.o["mp"][i].rearrange("(h o) -> h o", o=1), MM[:, G - 1:G], ["~gMM"], [], s.newstream("mo")))
            s.outs.append(s.dma(s.o["ms"][i].rearrange("s h -> h s"), MM[:, G:n], ["~gMM"], [], s.newstream("mo"), allow_slow_non_contiguous=True))
        tt = lambda out, a, b, op, R, Wk: s.op("dve", lambda e: e.tensor_tensor(out=out[:, 0:n], in0=a[:, 0:n], in1=b[:, 0:n], op=op), R, Wk)
        tt(AL, FF, T1, ALU.subtract, ["~gFF", "~gT1"], ["~gAL"])
        tt(AL, AL, MM, ALU.subtract, ["~gAL", "~gMM"], ["~gAL"])
        tt(AL, AL, T2, ALU.add, ["~gAL", "~gT2"], ["~gAL"])
        tt(BE, IG, FF, ALU.subtract, ["~gIG", "~gFF"], ["~gBE"])
        tt(BE, BE, T1, ALU.add, ["~gBE", "~gT1"], ["~gBE"])
        tt(BE, BE, T2, ALU.subtract, ["~gBE", "~gT2"], ["~gBE"])
        s.op("act", lambda e: e.activation(out=AL[:, 0:n], in_=AL[:, 0:n], func=AF.Exp), ["~gAL"], ["~gAL"])
        s.op("act", lambda e: e.activation(out=BE[:, 0:n], in_=BE[:, 0:n], func=AF.Exp), ["~gBE"], ["~gBE"])
        s.op("act", lambda e: e.activation(out=FF[:, 0:n], in_=MM[:, 0:n], func=AF.Exp, scale=-1.0), ["~gMM", "~gFF"], ["~gFF"])
        id4 = s.cst("ident")[0:4, 0:4]
        for ti, (c0, rows) in enumerate(tts):
            lo = loc(c0)
            bk = s.nb()
            for q, (src, key) in enumerate(((AL, "~gAL"), (BE, "~gBE"), (FF, "~gFF"))):
                s.mm(s.ps[bk][0:rows, 4 * q:4 * q + 4], src[:, lo:lo + rows], id4, True, True, [key, "CST"], [f"ps{bk}"])
            s.cp("dve", SC[0:rows, ti, :], s.ps[bk][0:rows, 0:12], [f"ps{bk}"], ["~SC"])
        bk = s.nb()
        sel4 = s.cst("sel4").rearrange("p (h m) -> p h m", h=4)
        for hd in range(4):
            s.mm(s.ps[bk][:, hd * NCH:(hd + 1) * NCH], sel4[0:4, hd, :], AL[:, 127:G:128], True, True, ["~gAL", "CST"], [f"ps{bk}"])
            if last:
                s.mm(s.ps[bk][:, 256 + hd * NS:256 + (hd + 1) * NS], sel4[0:4, hd, :], AL[:, G:n], True, True, ["~gAL", "CST"], [f"ps{bk}"])
        s.cp("dve", AE, s.ps[bk][:, 0:4 * NCH].rearrange("p (h k) -> p h k", h=4), [f"ps{bk}"], ["~AE"])
        if last:
            s.cp("dve", AES, s.ps[bk][:, 256:256 + 4 * NS].rearrange("p (h k) -> p h k", h=4), [f"ps{bk}"], ["~AES"])
        s.fence()
        NU = 15 + G
        UE = s.scr(PRE0, NU)
        PA = s.scr(PRE0 + 1056, NU)
        PB = s.scr(PRE0 + 2112, NU)
        SPL = s.scr(PRE0 + 3168, 1024)
        PO = s.scr(PRE0 + 4192, 1024)
        if last:
            for j in range(NS):
                s.dma(SPL[0:15, :], s.i["spool"][i, j], [], ["~SPL"], "spl")
                for h2 in range(2):
                    bk = s.nb()
                    for q in range(4):
                        ch = h2 * 4 + q
                        s.mm(s.ps[bk][:, q * 16:q * 16 + 15], SPL[0:15, ch * 128:(ch + 1) * 128], s.cst("ident")[0:15, 0:15], True, True,
                             ["~SPL", "CST"], [f"ps{bk}"])
                    s.cp("dve", UES[:, h2 * 4:h2 * 4 + 4, j, 0:15], s.ps[bk][:, 0:64].rearrange("p (q t) -> p q t", q=4)[:, :, 0:15],
                         [f"ps{bk}"], ["~UES"])
        for cg in range(2):
            wa = s.wgroup(W, 0, KC, cg * 512, 512)
            for j in range(4):
                ch = cg * 4 + j
                wdw = POOL_W[ch // 2]
                bks = [s.nb() for _ in ct]
                for kc in range(KC):
                    wt, wk = wa(kc)
                    for ti, (c0, w) in enumerate(ct):
                        s.mm(s.ps[bks[ti]][:, 0:w], wt[:, j * 128:(j + 1) * 128], s.XT[:, kc, c0:c0 + w], kc == 0, kc == KC - 1,
                             [wk, s.xkey(c0)], [f"ps{bks[ti]}"])
                if g == 0:
                    s.op("dve", lambda e: e.memset(UE[:, 0:15], 0.0), [], ["~UE"])
                else:
                    s.cp("dve", UE[:, 0:15], UCAR[:, ch, :], ["UCAR"], ["~UE"])
                for ti, (c0, w) in enumerate(ct):
                    if c0 >= c.S:
                        s.cp("act", UES[:, ch, :, 15:16], s.ps[bks[ti]][:, 0:NS].rearrange("p (j o) -> p j o", o=1), [f"ps{bks[ti]}"], ["~UES"])
                    else:
                        lo = loc(c0)
                        s.cp("act", UE[:, 15 + lo:15 + lo + w], s.ps[bks[ti]][:, 0:w], [f"ps{bks[ti]}"], ["~UE"])
                s.cp("dve", UCAR[:, ch, :], UE[:, G:G + 15], ["~UE"], ["UCAR"])
                cur, curk, lo_v = UE, "~UE", 0
                bufs = [(PA, "~PA"), (PB, "~PB")]
                step, bi = 1, 0
                while step < wdw:
                    dst, dk = bufs[bi]
                    nlo = lo_v + step
                    s.op("dve", lambda e, dst=dst, cur=cur, nlo=nlo, step=step: e.tensor_tensor(
                        out=dst[:, nlo:NU], in0=cur[:, nlo:NU], in1=cur[:, nlo - step:NU - step], op=ALU.add), [curk], [dk])
                    cur, curk, lo_v = dst, dk, nlo
                    step *= 2
                    bi ^= 1
                s.op("dve", lambda e, cur=cur, ch=ch, wdw=wdw: e.scalar_tensor_tensor(
                    out=DT[:, ch, 0:G], in0=cur[:, 15:NU], scalar=1.0 / wdw, in1=UE[:, 15:NU], op0=ALU.mult, op1=ALU.subtract),
                    [curk, "~UE"], ["~DT"])
                if g == 0:
                    for t in range(wdw - 1):
                        s.op("dve", lambda e, cur=cur, ch=ch, t=t: e.scalar_tensor_tensor(
                            out=DT[:, ch, t:t + 1], in0=cur[:, 15 + t:16 + t], scalar=1.0 / (t + 1), in1=UE[:, 15 + t:16 + t],
                            op0=ALU.mult, op1=ALU.subtract), [curk, "~UE"], ["~DT"])
                if last:
                    for jj in range(NS):
                        s.op("dve", lambda e, ch=ch, jj=jj, wdw=wdw: e.reduce_sum(out=PA[:, 0:1], in_=UES[:, ch, jj, 16 - wdw:16], axis=AX.X),
                             ["~UES", curk], ["~PA"])
                        s.op("dve", lambda e, ch=ch, jj=jj, wdw=wdw: e.scalar_tensor_tensor(
                            out=DT[:, ch, G + jj:G + jj + 1], in0=PA[:, 0:1], scalar=1.0 / wdw, in1=UES[:, ch, jj, 15:16],
                            op0=ALU.mult, op1=ALU.subtract), ["~PA", "~UES"], ["~DT"])
        if last:
            srcs = [(lambda ch: UCAR[:, ch, :], "UCAR", s.o["poolp"][i])]
            for jj in range(NS):
                srcs.append((lambda ch, jj=jj: UES[:, ch, jj, 1:16], "~UES", s.o["pools"][i, jj]))
            for fn, key, dst in srcs:
                for h2 in range(2):
                    bk = s.nb()
                    for q in range(4):
                        s.mm(s.ps[bk][0:15, q * 128:(q + 1) * 128], fn(h2 * 4 + q), s.cst("ident"), True, True, [key, "CST"], [f"ps{bk}"])
                    s.cp("dve", PO[0:15, h2 * 512:(h2 + 1) * 512], s.ps[bk][0:15, :], [f"ps{bk}"], ["~PO"])
                s.outs.append(s.dma(dst, PO[0:15, :], ["~PO"], [], "po"))
        wp, wpk = s.wtile(s.i["w_pool"][i].rearrange("g (k p) c -> p (g k) c", p=128), 8, 256)
        for gg in range(4):
            for jo in range(2):
                for (c0, w) in ct:
                    lo = loc(c0)
                    bk = s.nb()
                    for kk in range(2):
                        s.mm(s.ps[bk][:, 0:w], wp[:, gg * 2 + kk, jo * 128:(jo + 1) * 128], DT[:, gg * 2 + kk, lo:lo + w], kk == 0, kk == 1,
                             [wpk, "~DT"], [f"ps{bk}"])
                    s.op("dve", lambda e, bk=bk, w=w, lo=lo, oc=gg * 2 + jo: e.tensor_scalar(
                        out=MT[:, oc, lo:lo + w], in0=s.ps[bk][:, 0:w], scalar1=PSC[:, oc:oc + 1], scalar2=None, op0=ALU.mult),
                        [f"ps{bk}", "PSC"], ["~MT"])
        s.fence()
        o_ = PRE0
        QT = s.scr(o_, 1026, BF16)[:, 0:2 * GN].rearrange("p (k t) -> p k t", k=2); o_ += 1026
        KT = s.scr(o_, 1026, BF16)[:, 0:2 * GN].rearrange("p (k t) -> p k t", k=2); o_ += 1026
        KB = s.scr(o_, NT * 128, BF16).rearrange("p (t d) -> p t d", d=256); o_ += NT * 128
        VE = s.scr(o_, NT * 130, BF16).rearrange("p (t d) -> p t d", d=260); o_ += NT * 130
        OS = s.scr(o_, NT * 128, BF16).rearrange("p (t d) -> p t d", d=256); o_ += NT * 128
        HH = s.scr(o_, 256); o_ += 256
        HC = s.scr(o_, 256); o_ += 256
        HN = s.scr(o_, 128, BF16); o_ += 128
        AT = s.scr(o_, 64, BF16); o_ += 64
        SS = s.scr(o_, 16); o_ += 16
        HG = s.scr(o_, 1024); o_ += 1024
        assert o_ <= DT0
        s.dma(HG, s.i["hng"][i:i + 1, :].partition_broadcast(128).rearrange("p o c -> p (o c)") if False else
              s.i["hng"][i:i + 1, :].to_broadcast([128, 1024]), [], ["~HG"], s.newstream("hg"))
        for hd in range(4):
            qc, kcol, vc, oc_ = 1024 + hd * 256, 2048 + hd * 256, 3072 + hd * 256, 4096 + hd * 256
            CST_ = s.CSTATE[:, hd]
            ck_ = f"CST{hd}"
            for (col, dstT, key, mul) in ((qc, QT, "~QT", 1.0), (kcol, KT, "~KT", 0.0625)):
                wa = s.wgroup(W, 0, KC, col, 256)
                for jh in range(2):
                    bks = [s.nb() for _ in ct]
                    for kc in range(KC):
                        wt, wk = wa(kc)
                        for ti, (c0, w) in enumerate(ct):
                            s.mm(s.ps[bks[ti]][:, 0:w], wt[:, jh * 128:(jh + 1) * 128], s.XT[:, kc, c0:c0 + w], kc == 0, kc == KC - 1,
                                 [wk, s.xkey(c0)], [f"ps{bks[ti]}"])
                    for ti, (c0, w) in enumerate(ct):
                        lo = loc(c0)
                        s.op("act", lambda e, bk=bks[ti], w=w, lo=lo, jh=jh, dstT=dstT, mul=mul: e.mul(
                            out=dstT[:, jh, lo:lo + w], in_=s.ps[bk][:, 0:w], mul=mul), [f"ps{bks[ti]}"], [key])
                if col == kcol:
                    for ti, (c0, rows) in enumerate(tts):
                        bk = s.nb()
                        for kc in range(KC):
                            wt, wk = wa(kc)
                            s.mm(s.ps[bk][0:rows, 0:256], s.XT[:, kc, c0:c0 + rows], wt, kc == 0, kc == KC - 1, [wk, s.xkey(c0)], [f"ps{bk}"])
                        s.op("dve", lambda e, bk=bk, rows=rows, ti=ti, hd=hd: e.tensor_scalar(
                            out=KB[0:rows, ti, :], in0=s.ps[bk][0:rows, 0:256], scalar1=SC[0:rows, ti, 4 + hd:5 + hd], scalar2=0.0625,
                            op0=ALU.mult, op1=ALU.mult), [f"ps{bk}", "~SC"], ["~KB"])
            for (col, which) in ((vc, "v"), (oc_, "o")):
                wa = s.wgroup(W, 0, KC, col, 256)
                for ti, (c0, rows) in enumerate(tts):
                    bk = s.nb()
                    for kc in range(KC):
                        wt, wk = wa(kc)
                        s.mm(s.ps[bk][0:rows, 0:256], s.XT[:, kc, c0:c0 + rows], wt, kc == 0, kc == KC - 1, [wk, s.xkey(c0)], [f"ps{bk}"])
                    if which == "v":
                        s.cp("act", VE[0:rows, ti, 0:256], s.ps[bk][0:rows, 0:256], [f"ps{bk}"], ["~VE"])
                        s.op("dve", lambda e, rows=rows, ti=ti: e.memset(VE[0:rows, ti, 256:257], 1.0), [], ["~VE"])
                    else:
                        s.op("act", lambda e, bk=bk, rows=rows, ti=ti: e.activation(out=OS[0:rows, ti, :], in_=s.ps[bk][0:rows, 0:256], func=AF.Sigmoid),
                             [f"ps{bk}"], ["~OS"])
            for ti, (c0, rows) in enumerate(tts):
                lo = loc(c0)
                samp = c0 >= c.S
                jsm = c0 - c.S
                if samp:
                    ST_, stk = s.CSS[:], "CSS"
                    s.dma(ST_[:, :, 0:256], s.i["sC"][i, jsm, hd].rearrange("(j p) e -> p j e", p=128), [], [stk], "css")
                    s.dma(ST_[:, :, 256:257], s.i["sn"][i, jsm, hd].rearrange("(j p o) -> p j o", p=128, o=1), [], [stk], "css",
                          allow_slow_non_contiguous=True)
                else:
                    ST_, stk = CST_, ck_
                    if g == 0 and ti == 0:
                        s.op("dve", lambda e, ST_=ST_: e.memset(ST_, 0.0), [], [stk])
                    else:
                        sc_ap = AEL[:, hd:hd + 1] if ti == 0 else AE[:, hd, ti - 1:ti]
                        s.op("dve", lambda e, ST_=ST_, sc_ap=sc_ap: e.tensor_scalar(out=ST_, in0=ST_, scalar1=sc_ap, scalar2=None, op0=ALU.mult),
                             [stk, "AEL", "~AE"], [stk])
                s.cp("act", s.CB[:], ST_, [stk], ["CB"])
                bk = s.nb()
                for jh in range(2):
                    s.mm(s.ps[bk][0:rows, 0:rows], KT[:, jh, lo:lo + rows], QT[:, jh, lo:lo + rows], jh == 0, jh == 1, ["~KT", "~QT"], [f"ps{bk}"])
                s.op("dve", lambda e, bk=bk, rows=rows, ti=ti, hd=hd: e.scalar_tensor_tensor(
                    out=AT[0:rows, 0:rows], in0=s.ps[bk][0:rows, 0:rows], scalar=SC[0:rows, ti, 4 + hd:5 + hd], in1=s.MTB[0:rows, 0:rows],
                    op0=ALU.mult, op1=ALU.mult), [f"ps{bk}", "~SC", "MTB"], ["~AT"])
                bk = s.nb()
                for jh in range(2):
                    s.mm(s.ps[bk][0:rows, 0:257], QT[:, jh, lo:lo + rows], s.CB[:, jh, :], jh == 0, False, ["~QT", "CB"], [f"ps{bk}"])
                s.mm(s.ps[bk][0:rows, 0:257], AT[0:rows, 0:rows], VE[0:rows, ti, 0:257], False, True, ["~AT", "~VE"], [f"ps{bk}"])
                al = SC[0:rows, ti, hd:hd + 1]
                em = SC[0:rows, ti, 8 + hd:9 + hd]
                s1, s2, s3 = SS[0:rows, 0:1], SS[0:rows, 1:2], SS[0:rows, 2:3]
                s.op("dve", lambda e, bk=bk, rows=rows, al=al, s1=s1: e.tensor_tensor(out=s1, in0=s.ps[bk][0:rows, 256:257], in1=al, op=ALU.mult),
                     [f"ps{bk}", "~SC"], ["~SS"])
                s.op("dve", lambda e, s1=s1, s2=s2: e.tensor_scalar(out=s2, in0=s1, scalar1=-1.0, scalar2=None, op0=ALU.mult), ["~SS"], ["~SS"])
                s.op("dve", lambda e, s1=s1, s2=s2: e.tensor_tensor(out=s2, in0=s1, in1=s2, op=ALU.max), ["~SS"], ["~SS"])
                s.op("dve", lambda e, s2=s2, em=em: e.tensor_tensor(out=s2, in0=s2, in1=em, op=ALU.max), ["~SS", "~SC"], ["~SS"])
                s.op("dve", lambda e, s2=s2: e.reciprocal(out=s2, in_=s2), ["~SS"], ["~SS"])
                s.op("dve", lambda e, s2=s2, s3=s3, al=al: e.tensor_tensor(out=s3, in0=al, in1=s2, op=ALU.mult), ["~SS", "~SC"], ["~SS"])
                s.op("dve", lambda e, bk=bk, rows=rows, s3=s3: e.tensor_scalar(out=HH[0:rows, :], in0=s.ps[bk][0:rows, 0:256], scalar1=s3, scalar2=None,
                                                                         op0=ALU.mult), [f"ps{bk}", "~SS"], ["~HH"])
                s4, s5 = SS[0:rows, 3:4], SS[0:rows, 4:5]
                s.op("dve", lambda e, rows=rows, s4=s4: e.reduce_sum(out=s4, in_=HH[0:rows, :], axis=AX.X), ["~HH"], ["~SS"])
                s.op("dve", lambda e, s4=s4: e.tensor_scalar(out=s4, in0=s4, scalar1=1.0 / 256, scalar2=None, op0=ALU.mult), ["~SS"], ["~SS"])
                s.op("dve", lambda e, rows=rows, s4=s4: e.tensor_scalar(out=HH[0:rows, :], in0=HH[0:rows, :], scalar1=s4, scalar2=None, op0=ALU.subtract),
                     ["~HH", "~SS"], ["~HH"])
                s.op("dve", lambda e, s5=s5: e.memset(s5, 0.0), [], ["~SS5"])
                s.op("act", lambda e, rows=rows, s5=s5: e.activation(out=HC[0:rows, :], in_=HH[0:rows, :], func=AF.Square, accum_out=s5),
                     ["~HH"], ["~HC", "~SS5"])
                s.op("dve", lambda e, s5=s5: e.tensor_scalar(out=s5, in0=s5, scalar1=1.0 / 256, scalar2=None, op0=ALU.mult), ["~SS5"], ["~SS5"])
                s.rsqrt(s5, 1e-6, "~SS5")
                s.op("dve", lambda e, rows=rows, s5=s5, hd=hd: e.scalar_tensor_tensor(
                    out=HC[0:rows, :], in0=HH[0:rows, :], scalar=s5, in1=HG[0:rows, hd * 256:(hd + 1) * 256], op0=ALU.mult, op1=ALU.mult),
                    ["~HH", "~SS5", "~HG", "~HC"], ["~HC"])
                s.op("dve", lambda e, rows=rows, ti=ti: e.tensor_tensor(out=HN[0:rows, :], in0=HC[0:rows, :], in1=OS[0:rows, ti, :], op=ALU.mult),
                     ["~HC", "~OS"], ["~HN"])
                bk = s.nb()
                psb = s.ps[bk][:].bitcast(BF16)
                for jh in range(2):
                    s.op("pe", lambda e, psb=psb, jh=jh, rows=rows: e.transpose(psb[:, jh * 128:jh * 128 + rows], HN[0:rows, jh * 128:(jh + 1) * 128],
                                                                             s.IDB[0:rows, 0:rows]), ["~HN", "IDB"], [f"ps{bk}"])
                s.cp("act", MT[:, 8 + 2 * hd:10 + 2 * hd, lo:lo + rows], psb[:, 0:256].rearrange("p (j t) -> p j t", j=2)[:, :, 0:rows],
                     [f"ps{bk}"], ["~MT"])
                for jh in range(2):
                    bk = s.nb()
                    s.mm(s.ps[bk][:, 0:257], KB[0:rows, ti, jh * 128:(jh + 1) * 128], VE[0:rows, ti, 0:257], True, True, ["~KB", "~VE"], [f"ps{bk}"])
                    s.op("dve", lambda e, bk=bk, jh=jh, ST_=ST_: e.tensor_tensor(out=ST_[:, jh, :], in0=s.ps[bk][:, 0:257], in1=ST_[:, jh, :], op=ALU.add),
                         [f"ps{bk}", stk], [stk])
                if samp:
                    s.op("dve", lambda e, ST_=ST_, hd=hd, jsm=jsm: e.tensor_scalar(out=ST_, in0=ST_, scalar1=AES[:, hd, jsm:jsm + 1], scalar2=None,
                                                                              op0=ALU.mult), [stk, "~AES"], [stk])
                    s.outs.append(s.dma(s.o["Cs"][i, jsm, hd].rearrange("(j p) e -> p j e", p=128), ST_[:, :, 0:256], [stk], [], "cso"))
                    s.outs.append(s.dma(s.o["ns"][i, jsm, hd].rearrange("(j p o) -> p j o", p=128, o=1), ST_[:, :, 256:257], [stk], [], "cso",
                                        allow_slow_non_contiguous=True))
            if last:
                s.op("dve", lambda e, CST_=CST_, hd=hd: e.tensor_scalar(out=CST_, in0=CST_, scalar1=AE[:, hd, NCH - 1:NCH], scalar2=None, op0=ALU.mult),
                     [ck_, "~AE"], [ck_])
                s.outs.append(s.dma(s.o["Cp"][i, hd].rearrange("(j p) e -> p j e", p=128), CST_[:, :, 0:256], [ck_], [], s.newstream("cpo")))
                s.outs.append(s.dma(s.o["np"][i, hd].rearrange("(j p o) -> p j o", p=128, o=1), CST_[:, :, 256:257], [ck_], [], s.newstream("cpo"),
                                    allow_slow_non_contiguous=True))
        s.cp("dve", AEL, AE[:, :, NCH - 1], ["~AE"], ["AEL"])
        s.proj_ln(s.i["w_out"][i], MT, "~MT", ct, loc, 0, l, PRE0, DT0)

    def proj_ln(s, w2d, SRC, srckey, ct, loc, kind, l, PRE0, TMP0, tmpkeys=(), tmps=None):
        c = s.cfg
        KC = c.KC
        s.fence()
        PRE = s.scr(PRE0, KC * 512).rearrange("p (k t) -> p k t", k=KC)
        pk = [f"~PRE{k}" for k in range(KC)]
        for (c0, w) in ct:
            lo = loc(c0)
            for ocg in range(4):
                wa = s.wgroup(w2d, 0, KC, ocg * 512, 512)
                for j in range(4):
                    oc = ocg * 4 + j
                    bk = s.nb()
                    for kc in range(KC):
                        wt, wk = wa(kc)
                        s.mm(s.ps[bk][:, 0:w], wt[:, j * 128:(j + 1) * 128], SRC[:, kc, lo:lo + w], kc == 0, kc == KC - 1, [wk, srckey], [f"ps{bk}"])
                    s.op("dve", lambda e, bk=bk, w=w, oc=oc, c0=c0: e.scalar_tensor_tensor(
                        out=PRE[:, oc, 0:w], in0=s.XT[:, oc, c0:c0 + w], scalar=c.ALPHA, in1=s.ps[bk][:, 0:w], op0=ALU.mult, op1=ALU.add),
                        [f"ps{bk}", s.xkey(c0)], [pk[oc]])
            s.layer_norm(PRE[:, :, 0:w], pk, w, kind, l, lambda kc, c0=c0, w=w: (s.XT[:, kc, c0:c0 + w], s.xkey(c0)), TMP0, tmpkeys, tmps)

    def attn(s, l, g):
        c = s.cfg
        i = l // 2
        G, NS, KC, S = c.G, c.NS, c.KC, c.S
        last = (g == c.NG - 1)
        W = s.i["w_qkv"][i]
        ct = s.col_tiles(g)
        c0g = g * G
        NBP = c.NBP
        SCL = 128 ** -0.5
        s.fence()
        KT = s.scr(0, 2 * c.TT, BF16).rearrange("p (k t) -> p k t", k=4)
        VB = s.scr(4100, 4096, BF16).rearrange("p (t d) -> p t d", d=512)
        KM = s.scr(8196, 32).rearrange("p (k b) -> p k b", k=4)
        KMB = s.scr(8228, 16, BF16).rearrange("p (k b) -> p k b", k=4)
        OT0, PRE0 = 8260, 12372
        OT = s.scr(OT0, 4112, BF16).rearrange("p (k t) -> p k t", k=16)
        o_ = PRE0
        QT4 = s.scr(o_, 1028, BF16).rearrange("p (k t) -> p k t", k=4); o_ += 1028
        SMX = s.scr(o_, 2048); o_ += 2048
        PB = s.scr(o_, 1024, BF16); o_ += 1024
        PT = s.scr(o_, 1024, BF16).rearrange("p (t q) -> p t q", q=128); o_ += 1024
        GS = s.scr(o_, 32).rearrange("p (h b) -> p h b", h=4); o_ += 32
        CMP = s.scr(o_, 256).rearrange("p (h a b) -> p h a b", h=4, a=8); o_ += 256
        RK = s.scr(o_, 32).rearrange("p (h b) -> p h b", h=4); o_ += 32
        SBI = s.scr(o_, 32).rearrange("p (h b) -> p h b", h=4); o_ += 32
        OQ = s.scr(o_, 64, BF16); o_ += 64
        ST4 = s.scr(o_, 16); o_ += 16
        QROW = s.scr(o_, NS * 1024, BF16).rearrange("p (j d) -> p j d", j=NS); o_ += NS * 1024
        assert o_ <= PRE0 + 8192
        KST = [s.scr(PRE0 + 1028 + b * 512, 512) for b in range(2)]
        QSF = s.SM_[:, 192:192 + 16 * NS].rearrange("p (h j) -> p h j", h=16)
        tts = [(c0g + k * 128, 128) for k in range(G // 128)]
        if last:
            tts += [(S + j, 1) for j in range(NS)]
        wa = s.wgroup(W, 0, KC, 2048, 512)
        for j in range(4):
            bks = [s.nb() for _ in ct]
            for kc in range(KC):
                wt, wk = wa(kc)
                for ti, (c0, w) in enumerate(ct):
                    s.mm(s.ps[bks[ti]][:, 0:w], wt[:, j * 128:(j + 1) * 128], s.XT[:, kc, c0:c0 + w], kc == 0, kc == KC - 1,
                         [wk, s.xkey(c0)], [f"ps{bks[ti]}"])
            for ti, (c0, w) in enumerate(ct):
                if c0 < S:
                    b0 = c0 // 256
                    s.op("dve", lambda e, j=j, b0=b0: e.memset(KM[:, j, b0:b0 + 2], 0.0), [], ["~KM"])
                    for bb in range(2):
                        s.op("act", lambda e, bk=bks[ti], j=j, b0=b0, bb=bb, c0=c0: e.activation(
                            out=KT[:, j, c0 + bb * 256:c0 + (bb + 1) * 256], in_=s.ps[bk][:, bb * 256:(bb + 1) * 256], func=AF.Copy,
                            accum_out=KM[:, j, b0 + bb:b0 + bb + 1]), [f"ps{bks[ti]}", "~KM"], ["~KT", "~KM"])
                else:
                    s.cp("act", KT[:, j, c0:c0 + w], s.ps[bks[ti]][:, 0:w], [f"ps{bks[ti]}"], ["~KT"])
        nst = 0
        for which, col in ((("k", 2048), ("v", 2560)) if "noA" not in DBG else ()):
            wv = wa if which == "k" else s.wgroup(W, 0, KC, col, 512)
            for ti, (c0, rows) in enumerate(tts):
                bk = s.nb()
                for kc in range(KC):
                    wt, wk = wv(kc)
                    s.mm(s.ps[bk][0:rows, 0:512], s.XT[:, kc, c0:c0 + rows], wt, kc == 0, kc == KC - 1, [wk, s.xkey(c0)], [f"ps{bk}"])
                if c0 >= S:
                    jj = c0 - S
                    dst = s.KSF[0:1, 0 if which == "k" else 1, jj, :]
                    s.cp("dve", dst, s.ps[bk][0:1, 0:512], [f"ps{bk}"], ["KSF"])
                    s.outs.append(s.dma(s.o["ks" if which == "k" else "vs"][i, jj:jj + 1, :], dst, ["KSF"], [], s.newstream("kso")))
                else:
                    b = nst % 2
                    nst += 1
                    s.cp("dve", KST[b][0:rows, :], s.ps[bk][0:rows, 0:512], [f"ps{bk}"], [f"~KST{b}"])
                    s.outs.append(s.dma(s.o["kp" if which == "k" else "vp"][i, c0:c0 + rows, :], KST[b][0:rows, :], [f"~KST{b}"], [], f"kst{b}"))
                    if which == "v":
                        s.cp("act", VB[0:rows, c0 // 128, :], KST[b][0:rows, :], [f"~KST{b}"], ["~VB"])
        s.cp("dve", KMB[:, :, 0:NBP], KM[:, :, 0:NBP], ["~KM"], ["~KMB"])
        own = s.cst("own").rearrange("p (h k) -> p h k", h=2)
        for st in range(G // 512):
            s.fence()
            cs0 = c0g + st * 512
            lst = last and st == G // 512 - 1
            sct = [(cs0, 512)] + ([(S, NS)] if lst else [])
            sloc = lambda c0: 512 + (c0 - S) if c0 >= S else c0 - cs0
            for kvh in range(4):
                wq = s.wgroup(W, 0, KC, kvh * 512, 512)
                for j in range(4):
                    bks = [s.nb() for _ in sct]
                    for kc in range(KC):
                        wt, wk = wq(kc)
                        for ti, (c0, w) in enumerate(sct):
                            s.mm(s.ps[bks[ti]][:, 0:w], wt[:, j * 128:(j + 1) * 128], s.XT[:, kc, c0:c0 + w], kc == 0, kc == KC - 1,
                                 [wk, s.xkey(c0)], [f"ps{bks[ti]}"])
                    for ti, (c0, w) in enumerate(sct):
                        lo = sloc(c0)
                        s.op("act", lambda e, bk=bks[ti], j=j, lo=lo, w=w: e.mul(out=QT4[:, j, lo:lo + w], in_=s.ps[bk][:, 0:w], mul=SCL),
                             [f"ps{bks[ti]}"], ["~QT4"])
                        if c0 >= S:
                            s.op("act", lambda e, bk=bks[ti], h=kvh * 4 + j: e.mul(out=QSF[:, h, :], in_=s.ps[bk][:, 0:NS], mul=SCL),
                                 [f"ps{bks[ti]}"], ["QSF"])
                if False:
                    for jj in range(NS):
                        bk = s.nb()
                        for kc in range(KC):
                            wt, wk = wq(kc)
                            s.mm(s.ps[bk][0:1, 0:512], s.XT[:, kc, S + jj:S + jj + 1], wt, kc == 0, kc == KC - 1, [wk, "XTs"], [f"ps{bk}"])
                        s.op("act", lambda e, bk=bk, jj=jj, kvh=kvh: e.mul(out=QROW[0:1, jj, kvh * 512:(kvh + 1) * 512], in_=s.ps[bk][0:1, 0:512], mul=SCL),
                             [f"ps{bk}"], ["~QROW"])
                cfl = s.CSTATE[:].rearrange("p a b c -> p (a b c)")
                PBs = [PB, cfl[:, 0:1024].bitcast(BF16)]
                PTs = [PT, cfl[:, 1024:2048].bitcast(BF16).rearrange("p (t q) -> p t q", q=128)]
                CK = ["CST0", "CST1", "CST2", "CST3"]
                pbk = [["~PB0"], ["~PB1"] + CK]
                ptk = [["~PT0"], ["~PT1"] + CK]
                items = [(qt, j) for qt in range(4) for j in range(4)]
                SMXs = [SMX, QROW[:, :, :].rearrange("p j d -> p (j d)").bitcast(F32)[:, 0:2048] if NS * 1024 >= 2048 else SMX]

                def geom(qt):
                    t0 = cs0 + qt * 128
                    jq, hf = t0 // 256, (t0 % 256) // 128
                    return jq, hf, (jq + 1) * 256, jq >= 4

                def gate(qt):
                    jq, hf, nk, sel = geom(qt)
                    if not sel:
                        return
                    bk = s.nb()
                    for j in range(4):
                        s.mm(s.ps[bk][:, j * 8:j * 8 + jq], QT4[:, j, qt * 128:(qt + 1) * 128], KMB[:, kvh, 0:jq], True, True,
                             ["~QT4", "~KMB"], [f"ps{bk}"])
                    s.cp("dve", GS[:, :, 0:jq], s.ps[bk][:, 0:32].rearrange("p (h b) -> p h b", h=4)[:, :, 0:jq], [f"ps{bk}"], ["~GS"])
                    s.op("dve", lambda e: e.tensor_tensor(
                        out=CMP[:, :, 0:jq, 0:jq], in0=GS[:, :, 0:jq].unsqueeze(2).to_broadcast([128, 4, jq, jq]),
                        in1=GS[:, :, 0:jq].unsqueeze(3).to_broadcast([128, 4, jq, jq]), op=ALU.is_gt), ["~GS"], ["~CMP"])
                    s.op("dve", lambda e: e.reduce_sum(out=RK[:, :, 0:jq], in_=CMP[:, :, 0:jq, 0:jq], axis=AX.X), ["~CMP"], ["~RK"])
                    s.op("dve", lambda e: e.tensor_scalar(out=SBI[:, :, 0:jq], in0=RK[:, :, 0:jq], scalar1=2.5, scalar2=NEG,
                                                          op0=ALU.is_ge, op1=ALU.mult), ["~RK"], ["~SBI"])

                def st1a(n):
                    qt, j = items[n]
                    jq, hf, nk, sel = geom(qt)
                    SMX = SMXs[n % 2]
                    sk = f"~SMX{n % 2}"
                    nsl = (nk + 511) // 512
                    osl = jq // 2
                    others = [sl for sl in range(nsl) if sl != osl]
                    eng = {osl: "dve"}
                    for q_, sl in enumerate(others):
                        eng[sl] = "act" if q_ < (len(others) + 1) // 2 else "dve"
                    for sl in range(nsl):
                        wk_ = min(512, nk - sl * 512)
                        bk = s.nb()
                        s.mm(s.ps[bk][:, 0:wk_], QT4[:, j, qt * 128:(qt + 1) * 128], KT[:, kvh, sl * 512:sl * 512 + wk_], True, True,
                             ["~QT4", "~KT"], [f"ps{bk}"])
                        for bb in range(wk_ // 256):
                            b = sl * 2 + bb
                            src = s.ps[bk][:, bb * 256:(bb + 1) * 256]
                            dst = SMX[:, b * 256:(b + 1) * 256]
                            if b == jq:
                                s.op("dve", lambda e, src=src, dst=dst: e.tensor_tensor(out=dst, in0=src, in1=own[:, hf, :], op=ALU.add),
                                     [f"ps{bk}", "CST"], [sk])
                            elif sel and eng[sl] == "dve":
                                s.op("dve", lambda e, src=src, dst=dst, b=b: e.tensor_scalar(out=dst, in0=src, scalar1=SBI[:, j, b:b + 1],
                                                                                        scalar2=None, op0=ALU.add), [f"ps{bk}", "~SBI"], [sk])
                            elif sel:
                                s.op("act", lambda e, src=src, dst=dst, b=b: e.activation(out=dst, in_=src, func=AF.Identity, bias=SBI[:, j, b:b + 1],
                                                                                      scale=1.0), [f"ps{bk}", "~SBI"], [sk])
                            else:
                                s.cp(eng[sl], dst, src, [f"ps{bk}"], [sk])

                def st1b(n):
                    qt, j = items[n]
                    jq, hf, nk, sel = geom(qt)
                    P_, pk_ = PBs[n % 2], pbk[n % 2]
                    mx, rs = ST4[:, n % 2:n % 2 + 1], ST4[:, 4 + n % 3:5 + n % 3]
                    rk = f"~ST4r{n % 3}"
                    mk = f"~ST4m{n % 2}"
                    SMX = SMXs[n % 2]
                    sk = f"~SMX{n % 2}"
                    s.op("dve", lambda e: e.reduce_max(out=mx, in_=SMX[:, 0:nk], axis=AX.X), [sk], [mk])
                    s.op("dve", lambda e: e.tensor_scalar(out=mx, in0=mx, scalar1=-1.0, scalar2=None, op0=ALU.mult), [mk], [mk])
                    s.op("dve", lambda e: e.memset(rs, 0.0), [], [rk])
                    s.op("act", lambda e: e.activation(out=P_[:, 0:nk], in_=SMX[:, 0:nk], func=AF.Exp, bias=mx, scale=1.0, accum_out=rs),
                         [sk, mk, rk], pk_ + [rk])

                def st2(n):
                    qt, j = items[n]
                    jq, hf, nk, sel = geom(qt)
                    P_, pk_ = PBs[n % 2], pbk[n % 2]
                    T_, tk_ = PTs[n % 2], ptk[n % 2]
                    nkt = nk // 128
                    for k8 in range(0, nkt, 8):
                        n8 = min(8, nkt - k8)
                        bk = s.nb()
                        psb = s.ps[bk][:].bitcast(BF16)
                        for q in range(n8):
                            s.op("pe", lambda e, psb=psb, q=q, k8=k8: e.transpose(psb[:, q * 128:(q + 1) * 128], P_[:, (k8 + q) * 128:(k8 + q + 1) * 128],
                                                                              s.IDB[:]), pk_[0:1] + ["IDB"], [f"ps{bk}"])
                        s.cp("act" if (k8 // 8) % 2 else "dve", T_[:, k8:k8 + n8, :], psb[:, 0:n8 * 128].rearrange("p (t q) -> p t q", q=128),
                             [f"ps{bk}"], tk_)

                def st3(n):
                    qt, j = items[n]
                    jq, hf, nk, sel = geom(qt)
                    T_, tk_ = PTs[n % 2], ptk[n % 2]
                    rs = ST4[:, 4 + n % 3:5 + n % 3]
                    rk = f"~ST4r{n % 3}"
                    nkt = nk // 128
                    h = kvh * 4 + j
                    bk = s.nb()
                    for kt in range(nkt):
                        s.mm(s.ps[bk][:, 0:128], T_[:, kt, :], VB[:, kt, kvh * 128:(kvh + 1) * 128], kt == 0, kt == nkt - 1, tk_[0:1] + ["~VB"], [f"ps{bk}"])
                    s.op("dve", lambda e: e.reciprocal(out=rs, in_=rs), [rk], [rk])
                    s.op("dve", lambda e: e.tensor_scalar(out=OQ[:, :], in0=s.ps[bk][:, 0:128], scalar1=rs, scalar2=None, op0=ALU.mult),
                         [f"ps{bk}", rk], ["~OQ"])
                    bk2 = s.nb()
                    psb = s.ps[bk2][:].bitcast(BF16)
                    s.op("pe", lambda e: e.transpose(psb[:, 0:128], OQ[:, :], s.IDB[:]), ["~OQ", "IDB"], [f"ps{bk2}"])
                    s.cp("act", OT[:, h, qt * 128:(qt + 1) * 128], psb[:, 0:128], [f"ps{bk2}"], ["~OT"])

                NI = len(items) if "noB" not in DBG else 0
                for n in range(NI + 3):
                    if n < NI:
                        if items[n][1] == 0:
                            gate(items[n][0])
                        st1a(n)
                    if 1 <= n <= NI:
                        st1b(n - 1)
                    if 2 <= n <= NI + 1:
                        st2(n - 2)
                    if 3 <= n:
                        st3(n - 3)
            if lst and "nosamp" not in DBG:
                for jj in range(NS):
                    s.sample_attn(i, jj, OT, KT, QSF, PRE0)
            cfl = s.CSTATE[:].rearrange("p a b c -> p (a b c)")
            tmps = [cfl[:, q * 512:(q + 1) * 512] for q in range(4)] + [s.RT[:, 0, :]]
            if "noproj" not in DBG:
              s.proj_ln(s.i["w_o"][i], OT, "~OT", sct, sloc, 0, l, PRE0, None, ["CST0", "CST1", "CST2", "CST3", "RT0"], tmps)

    def sample_attn(s, i, jj, OT, KTall, QSF, PRE0):
        c = s.cfg
        NP, NB = c.NPAGES, c.NB
        NP1 = NP + 1
        SCLK = ["CST0", "CST1", "CST2", "CST3"]
        s.fence()
        o_ = PRE0
        IDX = s.scr(o_, NP).bitcast(I32); o_ += NP
        IDF = s.scr(o_, NP); o_ += NP
        PTB = s.scr(o_, NP).bitcast(I32); o_ += NP
        STT = s.scr(o_, NP1 * 16).rearrange("p (g h) -> p g h", h=16); o_ += NP1 * 16
        QREP = s.scr(o_, 2048).rearrange("p (h d) -> p h d", h=16); o_ += 2048
        KS = s.scr(o_, 4 * NB).rearrange("p (k b) -> p k b", k=4); o_ += 4 * NB
        SML = s.scr(o_, 128); o_ += 128
        OSB = s.scr(o_, 512); o_ += 512
        assert o_ <= PRE0 + 5556, o_ - PRE0
        cfl0 = s.CSTATE[:].rearrange("p a b c -> p (a b c)")
        KP = [s.RT[:, 0, :], s.RT[:, 1, :], cfl0[:, 1024:1536], cfl0[:, 1536:2048]] + [QREP.rearrange("p h d -> p (h d)")[:, q * 512:(q + 1) * 512] for q in range(4)]
        KPK = [["RT0"], ["RT1"], ["~KP2"] + SCLK, ["~KP3"] + SCLK, ["~KP4"], ["~KP5"], ["~KP6"], ["~KP7"]]
        NKP = len(KP)
        TMP = s.CSTATE[:].rearrange("p a b c -> p (a b c)")[:, 0:2048].rearrange("p (h d) -> p h d", h=16)
        BB = s.CSTATE[:].rearrange("p a b c -> p (a b c)")[:, 0:16 * NB].rearrange("p (h b) -> p h b", h=16)
        ones = s.cst("ones")
        ident = s.cst("ident")
        s.dma(PTB, s.i["pt"][jj:jj + 1, :].to_broadcast([128, NP]), [], ["~PTB"], s.newstream("ptb"))
        s.cp("dve", IDF, PTB, ["~PTB"], ["~IDF"])
        s.op("dve", lambda e: e.tensor_scalar(out=IDF, in0=IDF, scalar1=128.0, scalar2=s.cst("iotaf"), op0=ALU.mult, op1=ALU.add), ["~IDF", "CST"], ["~IDF"])
        if i > 0:
            s.op("dve", lambda e: e.tensor_scalar(out=IDF, in0=IDF, scalar1=float(i * c.NPOOL * 128), scalar2=None, op0=ALU.add), ["~IDF"], ["~IDF"])
        s.cp("dve", IDX, IDF, ["~IDF"], ["~IDX"])
        cfl = s.CSTATE[:].rearrange("p a b c -> p (a b c)")
        KTs = [cfl[:, 0:256].bitcast(BF16), cfl[:, 256:512].bitcast(BF16)]
        cssf = s.CSS[:].rearrange("p a b -> p (a b)")
        KPb = [cssf[:, 0:256].bitcast(BF16), cssf[:, 256:512].bitcast(BF16)]
        QSB = SML[:, 96:96 + 8 * c.NS].bitcast(BF16).rearrange("p (h j) -> p h j", h=16)
        s.cp("dve", QSB, QSF, ["QSF"], ["~QSB"])
        onesb = s.MTB[:, 127:128]
        ksb = s.nb()
        s.reserved.add(ksb)
        sbk = None
        for pg in range(NP):
            b = pg % 2
            kb = pg % NKP
            s.op("pool", lambda e, kb=kb, pg=pg: e.indirect_dma_start(
                out=KP[kb], out_offset=None, in_=s.i["ck"], in_offset=bass.IndirectOffsetOnAxis(ap=IDX[:, pg:pg + 1], axis=0)),
                ["~IDX"], KPK[kb], stream=f"kpg{kb}")
            s.cp("act", KPb[b], KP[kb], [KPK[kb][0]], [f"~KPb{b}", "CSS"])
            tb = s.nb()
            psb = s.ps[tb][:].bitcast(BF16)
            for kvh in range(4):
                s.op("pe", lambda e, psb=psb, kvh=kvh, b=b: e.transpose(psb[:, kvh * 128:(kvh + 1) * 128], KPb[b][:, kvh * 128:(kvh + 1) * 128], s.IDB[:]),
                     [f"~KPb{b}", "IDB"], [f"ps{tb}"])
            s.cp("dve", KTs[b], psb[:, 0:512], [f"ps{tb}"], [f"~KTs{b}"] + SCLK)
            if pg % 32 == 0:
                sbk = s.nb()
                s.reserved.add(sbk)
            for kvh in range(4):
                col = (pg % 32) * 16 + kvh * 4
                s.mm(s.ps[sbk][:, col:col + 4], KTs[b][:, kvh * 128:(kvh + 1) * 128], QSB[:, kvh * 4:(kvh + 1) * 4, jj], True, True,
                     [f"~KTs{b}", "~QSB"], [f"ps{sbk}"])
            for kvh in range(4):
                s.mm(s.ps[ksb][:, kvh * NP + pg:kvh * NP + pg + 1], KPb[b][:, kvh * 128:(kvh + 1) * 128], onesb, True, True,
                     [f"~KPb{b}", "MTB"], [f"ps{ksb}"])
            if pg % 32 == 31 or pg == NP - 1:
                p0 = pg - pg % 32
                npg = pg - p0 + 1
                s.cp("dve", STT[:, p0:p0 + npg, :], s.ps[sbk][:, 0:npg * 16].rearrange("p (g h) -> p g h", h=16), [f"ps{sbk}"], ["~STT"])
                s.reserved.discard(sbk)
        kv = s.ps[ksb][:, 0:4 * NP].rearrange("p (k b t) -> p k b t", k=4, t=2)
        s.cp("dve", KS, kv[:, :, :, 0], [f"ps{ksb}"], ["~KS"])
        s.op("dve", lambda e: e.tensor_tensor(out=KS, in0=KS, in1=kv[:, :, :, 1], op=ALU.add), [f"ps{ksb}", "~KS"], ["~KS"])
        s.reserved.discard(ksb)
        s.op("dve", lambda e: e.memset(STT[:, NP, :], NEG), [], ["~STT"])
        KS32 = SML[:, 8:12]
        s.cp("dve", KS32, KTall[:, :, c.S + jj], ["~KT"], ["~SMLk"])
        bk = s.nb()
        for kvh in range(4):
            s.mm(s.ps[bk][0:1, kvh * 4:(kvh + 1) * 4], KS32[:, kvh:kvh + 1], QSF[:, kvh * 4:(kvh + 1) * 4, jj], True, True, ["~SMLk", "QSF"], [f"ps{bk}"])
        s.cp("dve", STT[0:1, NP, :], s.ps[bk][0:1, 0:16], [f"ps{bk}"], ["~STT"])
        GSS = SML[0:4, 0:8]
        X4 = s.scr(PRE0 + 5556 - 4 * NB - 8, 4 * NB)[0:4, :].rearrange("p (g b) -> p g b", g=4) if False else None
        gb = s.nb()
        for kvh in range(4):
            s.mm(s.ps[gb][0:4, kvh * NB:(kvh + 1) * NB], QSF[:, kvh * 4:(kvh + 1) * 4, jj], KS[:, kvh, :], True, True, ["QSF", "~KS"], [f"ps{gb}"])
        G4 = OSB[0:4, 0:4 * NB].rearrange("p (k b) -> p k b", k=4)
        s.cp("dve", G4, s.ps[gb][0:4, 0:4 * NB].rearrange("p (k b) -> p k b", k=4), [f"ps{gb}"], ["~OSB"])
        e4 = s.cst("e4").rearrange("p (g b) -> p g b", g=4)
        bbk = [s.nb(), s.nb()]
        X4 = TMP.rearrange("p h d -> p (h d)")[0:4, 0:4 * NB].rearrange("p (g b) -> p g b", g=4)
        for kvh in range(4):
            s.op("dve", lambda e, kvh=kvh: e.max(out=GSS, in_=G4[:, kvh, :]), ["~OSB"], ["~SML"])
            s.op("dve", lambda e, kvh=kvh: e.tensor_scalar(out=G4[:, kvh, :], in0=G4[:, kvh, :], scalar1=GSS[:, 2:3], scalar2=None, op0=ALU.is_ge),
                 ["~OSB", "~SML"], ["~OSB"])
            s.op("dve", lambda e, kvh=kvh: e.tensor_scalar(out=G4[:, kvh, :], in0=G4[:, kvh, :], scalar1=1.0, scalar2=-NEG, op0=ALU.subtract, op1=ALU.mult),
                 ["~OSB"], ["~OSB"])
            s.op("dve", lambda e, kvh=kvh: e.tensor_tensor(out=X4, in0=G4[:, kvh, :].unsqueeze(1).to_broadcast([4, 4, NB]), in1=e4[0:4], op=ALU.mult),
                 ["~OSB", "CST"], SCLK)
            half, off = kvh // 2, (kvh % 2) * 4 * NB
            s.mm(s.ps[bbk[half]][:, off:off + 4 * NB], ones[0:4, :], X4.rearrange("p g b -> p (g b)"), True, True, SCLK + ["CST"], [f"ps{bbk[half]}"])
        for half in range(2):
            s.cp("dve", BB[:, half * 8:(half + 1) * 8, :], s.ps[bbk[half]][:, 0:8 * NB].rearrange("p (h b) -> p h b", h=8), [f"ps{bbk[half]}"], SCLK)
        s.op("dve", lambda e: e.tensor_tensor(
            out=STT[:, 0:NP, :].rearrange("p (b t) h -> p b t h", t=2), in0=STT[:, 0:NP, :].rearrange("p (b t) h -> p b t h", t=2),
            in1=BB.rearrange("p h b -> p b h").unsqueeze(2).to_broadcast([128, NB, 2, 16]), op=ALU.add), SCLK + ["~STT"], ["~STT"])
        RM = SML[:, 16:32]
        s.op("dve", lambda e: e.reduce_max(out=RM, in_=STT.rearrange("p g h -> p h g"), axis=AX.X), ["~STT"], ["~SML"])
        bk = s.nb()
        s.op("pe", lambda e, bk=bk: e.transpose(s.ps[bk][0:16, 0:128], RM, ident), ["~SML", "CST"], [f"ps{bk}"])
        GM = SML[0:16, 32:33]
        s.op("dve", lambda e, bk=bk: e.reduce_max(out=GM, in_=s.ps[bk][0:16, 0:128], axis=AX.X), [f"ps{bk}"], ["~SML2"])
        DG = SML[0:16, 48:64]
        s.op("dve", lambda e: e.tensor_scalar(out=DG, in0=ident[0:16, 0:16], scalar1=GM, scalar2=-1.0, op0=ALU.mult, op1=ALU.mult), ["~SML2", "CST"], ["~SML3"])
        bk = s.nb()
        s.mm(s.ps[bk][:, 0:16], ones[0:16, :], DG, True, True, ["~SML3", "CST"], [f"ps{bk}"])
        NGM = SML[:, 64:80]
        s.cp("dve", NGM, s.ps[bk][:, 0:16], [f"ps{bk}"], ["~SML4"])
        s.op("dve", lambda e: e.tensor_tensor(out=STT, in0=STT, in1=NGM.unsqueeze(1).to_broadcast([128, NP1, 16]), op=ALU.add), ["~STT", "~SML4"], ["~STT"])
        s.op("act", lambda e: e.activation(out=STT, in_=STT, func=AF.Exp), ["~STT"], ["~STT"])
        RS = SML[:, 80:96]
        s.op("dve", lambda e: e.reduce_sum(out=RS, in_=STT.rearrange("p g h -> p h g"), axis=AX.X), ["~STT"], ["~SML5"])
        dbk = s.nb()
        s.mm(s.ps[dbk][0:16, 0:1], RS, ones[:, 0:1], True, True, ["~SML5", "CST"], [f"ps{dbk}"])
        DEN = SML[0:16, 33:34]
        s.op("dve", lambda e: e.reciprocal(out=DEN, in_=s.ps[dbk][0:16, 0:1]), [f"ps{dbk}"], ["~SML6"])
        STTb = cfl[:, 0:NP * 8].bitcast(BF16).rearrange("p (g h) -> p g h", h=16)
        s.cp("dve", STTb, STT[:, 0:NP, :], ["~STT"], ["~STTb"] + SCLK)
        VSB = OSB[0:1, 0:256].bitcast(BF16)
        PSB = OSB[0:1, 256:264].bitcast(BF16)
        s.cp("dve", VSB, s.KSF[0:1, 1, jj, :], ["KSF"], ["~OSB"])
        s.cp("dve", PSB, STT[0:1, NP, :], ["~STT"], ["~OSB"])
        obk = s.nb()
        s.reserved.add(obk)
        for pg in range(NP):
            kb = pg % NKP
            b = pg % 2
            s.op("pool", lambda e, kb=kb, pg=pg: e.indirect_dma_start(
                out=KP[kb], out_offset=None, in_=s.i["cv"], in_offset=bass.IndirectOffsetOnAxis(ap=IDX[:, pg:pg + 1], axis=0)),
                ["~IDX"], KPK[kb], stream=f"kpg{kb}")
            s.cp("act", KPb[b], KP[kb], [KPK[kb][0]], [f"~KPb{b}", "CSS"])
            s.mm(s.ps[obk][0:16, 0:512], STTb[:, pg, :], KPb[b], pg == 0, False, ["~STTb", f"~KPb{b}"], [f"ps{obk}"])
        s.mm(s.ps[obk][0:16, 0:512], PSB, VSB, False, True, ["~OSB"], [f"ps{obk}"])
        s.reserved.discard(obk)
        s.op("dve", lambda e: e.tensor_scalar(out=OSB[0:16, :], in0=s.ps[obk][0:16, 0:512], scalar1=DEN, scalar2=None, op0=ALU.mult),
             [f"ps{obk}", "~SML6"], ["~OSB"])
        selk = s.cst("selk").rearrange("p (k n) -> p k n", k=4)
        bk = s.nb()
        for kvh in range(4):
            s.mm(s.ps[bk][:, 0:16], OSB[0:16, kvh * 128:(kvh + 1) * 128], selk[0:16, kvh, :], kvh == 0, kvh == 3, ["~OSB", "CST"], [f"ps{bk}"])
        s.cp("dve", OT[:, :, 512 + jj:513 + jj], s.ps[bk][:, 0:16].unsqueeze(2), [f"ps{bk}"], ["~OT"])

    def run(s, parts=("ffn",)):
        c = s.cfg
        s.load_consts()
        s.load_x()
        for l in range(c.DEPTH):
            for g in range(c.NG):
                if "mix" in parts and l % 2 == 0:
                    s.mix(l, g)
                if "attn" in parts and l % 2 == 1:
                    s.attn(l, g)
            for g in range(c.NG):
                if "ffn" in parts:
                    s.ffn(l, g, final=(l == c.DEPTH - 1))
        s.finish()
        return s.P.emit(s.nc)


def build(cfg, parts=("ffn",)):
    nc = bass.Bass("TRN2", target_bir_lowering=False)
    with contextlib.ExitStack() as st:
        g = Gen(nc, cfg, st)
        g.run(parts)
    return nc, g


def prep_core(cfg, inp, core, carr):
    c = cfg
    sm_ = [c.NS * core + j for j in range(c.NS)]
    A = np.ascontiguousarray
    f32 = np.float32
    b_if = np.asarray(inp["b_if"], f32)
    bif = np.zeros((4, c.NMIX * 2), f32)
    for i in range(c.NMIX):
        bif[:, 2 * i] = b_if[i, 0:4]
        bif[:, 2 * i + 1] = b_if[i, 4:8]
    ps = np.asarray(inp["pool_scale"], f32)
    pscale = A(ps.reshape(c.NMIX, 8, 128).transpose(2, 0, 1).reshape(128, c.NMIX * 8))
    kinds = [np.asarray(inp[k], f32) for k in ("ln_mix_g", "ln_mix_b", "ln_ffn_g", "ln_ffn_b")]
    lnp = A(np.stack(kinds, 0).reshape(4, c.DEPTH, c.KC, 128).transpose(3, 0, 1, 2).reshape(128, 4 * c.DEPTH * c.KC))
    natt = np.asarray(inp["cache_k"]).shape[0]
    im = dict(
        xp=A(inp["x_prompt"][core]), xs=A(np.asarray(inp["x_sample"])[sm_, 0, :]),
        ck=np.asarray(inp["cache_k"]).reshape(-1, 512), cv=np.asarray(inp["cache_v"]).reshape(-1, 512),
        spool=A(np.asarray(inp["state_pool"])[:, sm_]), sC=A(np.asarray(inp["state_C"])[:, sm_]),
        sn=A(np.asarray(inp["state_n"])[:, sm_]), sm=A(np.asarray(inp["state_m"])[:, sm_]),
        pt=A(np.asarray(inp["page_table"], np.int32)[sm_]), iop=np.arange(128, dtype=np.int32)[:, None].copy(),
        w_in=np.asarray(inp["w_in_mix"]), bif=bif, w_pool=np.asarray(inp["w_pool"]), pscale=pscale,
        hng=np.asarray(inp["mlstm_norm_g"]), w_out=np.asarray(inp["w_out_mix"]),
        w_qkv=np.asarray(inp["w_qkv"]), w_o=np.asarray(inp["w_o"]), lnp=lnp,
        w_up=np.asarray(inp["w_up"]), w_down=np.asarray(inp["w_down"]), cst=carr,
    )
    return im


NCORES = 4
_CACHE = {}


def kernel(**inputs):
    cfg = Cfg(S=2048, G=1024, NS=8 // NCORES, DEPTH=4, DFF=8192, NPAGES=128, NPOOL=int(np.asarray(inputs["cache_k"]).shape[1]))
    if "nc" not in _CACHE:
        _CACHE["nc"] = build(cfg, parts=("mix", "attn", "ffn"))
    nc, g = _CACHE["nc"]
    in_maps = [prep_core(cfg, inputs, core, g.carr) for core in range(NCORES)]
    res = run_bass_kernel_spmd(nc, in_maps, core_ids=list(range(NCORES))).results
    f32 = np.float32
    cat = lambda k: np.stack([np.asarray(res[c][k], f32) for c in range(NCORES)], 0)
    NA, NM, NS = cfg.NATT, cfg.NMIX, cfg.NS
    y_prompt = cat("yp")
    y_sample = cat("ys").reshape(8, 1, cfg.D)
    k_prompt = cat("kp").transpose(1, 0, 2, 3).reshape(NA, 4, cfg.S, 4, 128)
    v_prompt = cat("vp").transpose(1, 0, 2, 3).reshape(NA, 4, cfg.S, 4, 128)
    k_sample = cat("ks").transpose(1, 0, 2, 3).reshape(NA, 8, 1, 4, 128)
    v_sample = cat("vs").transpose(1, 0, 2, 3).reshape(NA, 8, 1, 4, 128)
    pool_prompt = cat("poolp").transpose(1, 0, 2, 3)
    C_prompt = cat("Cp").transpose(1, 0, 2, 3, 4)
    n_prompt = cat("np").transpose(1, 0, 2, 3)
    m_prompt = cat("mp").transpose(1, 0, 2)
    pool_sample = cat("pools").transpose(1, 0, 2, 3, 4).reshape(NM, 8, 15, 1024)
    C_sample = cat("Cs").transpose(1, 0, 2, 3, 4, 5).reshape(NM, 8, 4, 256, 256)
    n_sample = cat("ns").transpose(1, 0, 2, 3, 4).reshape(NM, 8, 4, 256)
    m_sample = cat("ms").transpose(1, 0, 2, 3).reshape(NM, 8, 4)
    outs = (y_prompt, y_sample, k_prompt, v_prompt, k_sample, v_sample, pool_prompt, C_prompt, n_prompt, m_prompt,
            pool_sample, C_sample, n_sample, m_sample)
    return tuple(np.ascontiguousarray(o, dtype=f32) for o in outs)
```
